# Optimizing a Trainium2 kernel written in Bass

```python
import math
import jax, jax.numpy as jnp
from jax import lax
import numpy as np

D_MODEL = 1024
BATCH = 16
SEQ = 2048
DEPTH = 1

GRID_W = 64
CTX_LEN = 256
N_DIR = 2
S5_WIDTH = 1024
S5_GROUP = 16
S5_GROUPS = S5_WIDTH // S5_GROUP
S5_STATE = 64
S5_DT_MIN = 1e-3
S5_DT_MAX = 1e-1
RWKV_WIDTH = 1024
RWKV_HEAD = 64
RWKV_HEADS = RWKV_WIDTH // RWKV_HEAD
LORA_W = 64
LORA_A = 64
LORA_G = 128
RWKV_COLS = 3 * RWKV_WIDTH + LORA_W + LORA_A + LORA_G
RWKV_SPLITS = (RWKV_WIDTH, 2 * RWKV_WIDTH, 3 * RWKV_WIDTH, 3 * RWKV_WIDTH + LORA_W, 3 * RWKV_WIDTH + LORA_W + LORA_A)
N_IN = S5_WIDTH + RWKV_COLS + 2 * D_MODEL
D_FF = 2816
RMS_EPS = 1e-6
LNX_EPS = 64e-5

kernel_name = "hybrid_s5_rwkv7_convffn_prefix_dit"


def rmsnorm(x, g):
    xf = x.astype(jnp.float32)
    return xf * lax.rsqrt(jnp.mean(xf * xf, -1, keepdims=True) + RMS_EPS) * g.astype(jnp.float32)


def modulate(h, shift, scale):
    return h * (1.0 + scale) + shift


def grid_shift(x):
    b, l, ch = x.shape
    rows = l // GRID_W
    q = ch // 4
    gp = jnp.pad(x.reshape(b, rows, GRID_W, ch), ((0, 0), (1, 1), (1, 1), (0, 0)))
    left = gp[:, 1:-1, :-2, :q]
    right = gp[:, 1:-1, 2:, q:2 * q]
    up = gp[:, :-2, 1:-1, 2 * q:3 * q]
    down = gp[:, 2:, 1:-1, 3 * q:]
    return jnp.concatenate([left, right, up, down], -1).reshape(b, l, ch)


def seq_shift(x):
    h = x.shape[-1] // 2
    xp = jnp.pad(x, ((0, 0), (1, 1), (0, 0)))
    return jnp.concatenate([xp[:, :-2, :h], xp[:, 2:, h:]], -1)


def dwconv3(x, w, b):
    xp = jnp.pad(x, ((0, 0), (1, 1), (0, 0)))
    return xp[:, :-2] * w[0] + xp[:, 1:-1] * w[1] + xp[:, 2:] * w[2] + b


def s5_discretize(a_re, a_im, log_dt, b_re, b_im):
    a_re = a_re.astype(jnp.float32)
    a_im = a_im.astype(jnp.float32)
    dt = jnp.exp(log_dt.astype(jnp.float32))[:, None]
    mag = jnp.exp(a_re * dt)
    lam_r, lam_i = mag * jnp.cos(a_im * dt), mag * jnp.sin(a_im * dt)
    den = a_re * a_re + a_im * a_im
    coef_r = ((lam_r - 1.0) * a_re + lam_i * a_im) / den
    coef_i = (lam_i * a_re - (lam_r - 1.0) * a_im) / den
    b_re = b_re.astype(jnp.float32)
    b_im = b_im.astype(jnp.float32)
    bb_r = coef_r[..., None] * b_re - coef_i[..., None] * b_im
    bb_i = coef_r[..., None] * b_im + coef_i[..., None] * b_re
    return lam_r, lam_i, bb_r, bb_i


def _cplx_combine(e1, e2):
    ar1, ai1, br1, bi1 = e1
    ar2, ai2, br2, bi2 = e2
    return (ar2 * ar1 - ai2 * ai1, ar2 * ai1 + ai2 * ar1,
            ar2 * br1 - ai2 * bi1 + br2, ar2 * bi1 + ai2 * br1 + bi2)


def s5_scan(lam_r, lam_i, bu_r, bu_i, h0, reverse):
    if h0 is not None:
        first = -1 if reverse else 0
        h0_r, h0_i = h0
        bu_r = bu_r.at[:, first].add(lam_r * h0_r - lam_i * h0_i)
        bu_i = bu_i.at[:, first].add(lam_r * h0_i + lam_i * h0_r)
    length = bu_r.shape[1]
    lr = jnp.broadcast_to(lam_r, (1, length) + lam_r.shape)
    li = jnp.broadcast_to(lam_i, (1, length) + lam_i.shape)
    _, _, h_r, h_i = lax.associative_scan(_cplx_combine, (lr, li, bu_r, bu_i), reverse=reverse, axis=1)
    return h_r, h_i


def s5_branch(u_lat, u_ctx, p, need_ctx):
    def grouped(u):
        return u.reshape(u.shape[:2] + (S5_GROUPS, S5_GROUP))

    ug_lat, ug_ctx = grouped(u_lat), grouped(u_ctx)
    y_lat = u_lat * p['s5_d']
    y_ctx = u_ctx * p['s5_d'] if need_ctx else None
    for d in range(N_DIR):
        rev = d == 1
        last = 0 if rev else -1
        lam_r, lam_i, bb_r, bb_i = s5_discretize(p['s5_a_re'][d], p['s5_a_im'][d], p['s5_log_dt'][d],
                                                 p['s5_b_re'][d], p['s5_b_im'][d])
        c_r = p['s5_c_re'][d].astype(jnp.float32)
        c_i = p['s5_c_im'][d].astype(jnp.float32)

        def drive(ug):
            return jnp.einsum('blgc,gpc->blgp', ug, bb_r), jnp.einsum('blgc,gpc->blgp', ug, bb_i)

        def readout(h_r, h_i):
            y = jnp.einsum('blgp,gcp->blgc', h_r, c_r) - jnp.einsum('blgp,gcp->blgc', h_i, c_i)
            return y.reshape(h_r.shape[:2] + (S5_WIDTH,))

        hc_r, hc_i = s5_scan(lam_r, lam_i, *drive(ug_ctx), None, rev)
        hl_r, hl_i = s5_scan(lam_r, lam_i, *drive(ug_lat),
                             (hc_r[:, last], hc_i[:, last]), rev)
        y_lat = y_lat + readout(hl_r, hl_i)
        if need_ctx:
            y_ctx = y_ctx + readout(hc_r, hc_i)

    def glu(y):
        za, zb = jnp.split(jax.nn.gelu(y) @ p['s5_glu_w'], 2, -1)
        return za * jax.nn.sigmoid(zb)

    return glu(y_lat), (glu(y_ctx) if need_ctx else None)


def _heads(t):
    return t.reshape(t.shape[:-1] + (RWKV_HEADS, RWKV_HEAD))


def wkv7_scan(r, decay, k, v, a, b, s0, reverse):
    def step(s, inp):
        rt, wt, kt, vt, at, bt = inp
        sa = jnp.einsum('bhvk,bhk->bhv', s, at)
        s = s * wt[:, :, None, :] + sa[..., None] * bt[:, :, None, :] + vt[..., None] * kt[:, :, None, :]
        return s, jnp.einsum('bhvk,bhk->bhv', s, rt)

    xs = tuple(jnp.moveaxis(t, 1, 0) for t in (r, decay, k, v, a, b))
    s_final, ys = lax.scan(step, s0, xs, reverse=reverse)
    return jnp.moveaxis(ys, 0, 1), s_final


def rwkv7_branch(rw_lat, rw_ctx, p, need_ctx):
    def prepare(rw, shift_fn):
        xs = rw + (shift_fn(rw) - rw) * p['rwkv_mu']
        r, k, v, wd, ad, gd = jnp.split(xs, RWKV_SPLITS, -1)
        kk = _heads(k * p['rwkv_k_k'])
        kk = kk * lax.rsqrt(jnp.maximum(jnp.sum(kk * kk, -1, keepdims=True), 1e-24))
        return (r, k, v, kk, wd, ad), gd

    def direction(parts, d):
        r, k, v, kk, wd, ad = parts
        w_log = -jax.nn.softplus(-(p['rwkv_w0'][d] + jnp.tanh(wd) @ p['rwkv_w_up'][d])) - 0.5
        decay = _heads(jnp.exp(-jnp.exp(w_log)))
        a = _heads(jax.nn.sigmoid(p['rwkv_a0'][d] + ad @ p['rwkv_a_up'][d]))
        kd = _heads(k) * (1.0 + (a - 1.0) * _heads(p['rwkv_k_a']))
        rh, vh = _heads(r), _heads(v)
        bonus = jnp.sum(rh * kd * p['rwkv_r_k'][d], -1, keepdims=True) * vh
        return (rh, decay, kd, vh, -kk, kk * a), bonus

    def readout(wkv, bonus, gd):
        mu = jnp.mean(wkv, -1, keepdims=True)
        var = jnp.mean(jnp.square(wkv - mu), -1, keepdims=True)
        yn = ((wkv - mu) * lax.rsqrt(var + LNX_EPS)).reshape(wkv.shape[:2] + (RWKV_WIDTH,))
        yn = yn * p['rwkv_lnx_g'] + p['rwkv_lnx_b']
        g = jax.nn.sigmoid(gd) @ p['rwkv_g_up']
        return ((yn + bonus.reshape(yn.shape)) * g) @ p['rwkv_w_out']

    parts_l, gd_l = prepare(rw_lat, grid_shift)
    parts_c, gd_c = prepare(rw_ctx, seq_shift)
    batch = rw_lat.shape[0]
    s_zero = jnp.zeros((batch, RWKV_HEADS, RWKV_HEAD, RWKV_HEAD), jnp.float32)
    wkv_l = jnp.zeros(rw_lat.shape[:2] + (RWKV_HEADS, RWKV_HEAD), jnp.float32)
    bonus_l = jnp.zeros_like(wkv_l)
    wkv_c = jnp.zeros(rw_ctx.shape[:2] + (RWKV_HEADS, RWKV_HEAD), jnp.float32)
    bonus_c = jnp.zeros_like(wkv_c)
    for d in range(N_DIR):
        rev = d == 1
        ins_c, bon_c = direction(parts_c, d)
        ins_l, bon_l = direction(parts_l, d)
        ys_c, s_ctx = wkv7_scan(*ins_c, s_zero, rev)
        ys_l, _ = wkv7_scan(*ins_l, s_ctx, rev)
        wkv_l = wkv_l + ys_l
        bonus_l = bonus_l + bon_l
        if need_ctx:
            wkv_c = wkv_c + ys_c
            bonus_c = bonus_c + bon_c
    out_l = readout(wkv_l, bonus_l, gd_l)
    out_c = readout(wkv_c, bonus_c, gd_c) if need_ctx else None
    return out_l, out_c


def parallel_mixer(h_lat, h_ctx, p, need_ctx):
    def parts(h):
        return jnp.split(h @ p['w_in'], [S5_WIDTH, S5_WIDTH + RWKV_COLS], -1)

    u_l, rw_l, gate_l = parts(h_lat)
    u_c, rw_c, gate_c = parts(h_ctx)
    a_l, a_c = s5_branch(u_l, u_c, p, need_ctx)
    b_l, b_c = rwkv7_branch(rw_l, rw_c, p, need_ctx)

    def merge(a, b, gates):
        ga, gb = jnp.split(gates, 2, -1)
        return (jax.nn.sigmoid(ga) * a + jax.nn.sigmoid(gb) * b) @ p['w_out']

    return merge(a_l, b_l, gate_l), (merge(a_c, b_c, gate_c) if need_ctx else None)


def conv_ffn(h, p):
    z = dwconv3(h @ p['ffn_w_up'], p['ffn_conv_w'], p['ffn_conv_b'])
    zg, zv = jnp.split(z, 2, -1)
    return (jax.nn.silu(zg) * zv) @ p['ffn_w_down']


def setup_inputs(seed: int = 0) -> dict:
    key = jax.random.key(seed)
    ks = iter(jax.random.split(key, 48))
    f32 = jnp.float32

    def nrm(shape, s):
        return s * jax.random.normal(next(ks), shape, f32)

    def uni(shape, lo, hi):
        return jax.random.uniform(next(ks), shape, f32, lo, hi)

    L, G, P, GS, H, N = DEPTH, S5_GROUPS, S5_STATE, S5_GROUP, RWKV_HEADS, RWKV_HEAD
    return {
        'x': nrm((BATCH, SEQ, D_MODEL), 1.0),
        'c': nrm((BATCH, D_MODEL), 1.0),
        'ctx': nrm((BATCH, CTX_LEN, D_MODEL), 1.0),
        'c_ctx': nrm((D_MODEL,), 1.0),
        'w_mod': nrm((L, D_MODEL, 6 * D_MODEL), 0.5 * D_MODEL ** -0.5),
        'b_mod': nrm((L, 6 * D_MODEL), 0.01),
        'norm1_g': 1.0 + nrm((L, D_MODEL), 0.02),
        'norm2_g': 1.0 + nrm((L, D_MODEL), 0.02),
        'w_in': nrm((L, D_MODEL, N_IN), D_MODEL ** -0.5),
        's5_a_re': -0.5 + nrm((L, N_DIR, G, P), 0.01),
        's5_a_im': math.pi * jnp.arange(P, dtype=f32) + nrm((L, N_DIR, G, P), 0.01),
        's5_log_dt': uni((L, N_DIR, G), math.log(S5_DT_MIN), math.log(S5_DT_MAX)),
        's5_b_re': nrm((L, N_DIR, G, P, GS), (2 * GS) ** -0.5),
        's5_b_im': nrm((L, N_DIR, G, P, GS), (2 * GS) ** -0.5),
        's5_c_re': nrm((L, N_DIR, G, GS, P), 0.5),
        's5_c_im': nrm((L, N_DIR, G, GS, P), 0.5),
        's5_d': nrm((L, S5_WIDTH), 0.5),
        's5_glu_w': nrm((L, S5_WIDTH, 2 * D_MODEL), S5_WIDTH ** -0.5),
        'rwkv_mu': uni((L, RWKV_COLS), 0.0, 1.0),
        'rwkv_w0': uni((L, N_DIR, RWKV_WIDTH), -6.0, -1.0),
        'rwkv_w_up': nrm((L, N_DIR, LORA_W, RWKV_WIDTH), 0.5 * LORA_W ** -0.5),
        'rwkv_a0': nrm((L, N_DIR, RWKV_WIDTH), 0.3),
        'rwkv_a_up': nrm((L, N_DIR, LORA_A, RWKV_WIDTH), 0.5 * LORA_A ** -0.5),
        'rwkv_g_up': nrm((L, LORA_G, RWKV_WIDTH), LORA_G ** -0.5),
        'rwkv_k_k': 0.85 + nrm((L, RWKV_WIDTH), 0.02),
        'rwkv_k_a': 1.0 + nrm((L, RWKV_WIDTH), 0.02),
        'rwkv_r_k': nrm((L, N_DIR, H, N), 0.1),
        'rwkv_lnx_g': 1.0 + nrm((L, RWKV_WIDTH), 0.02),
        'rwkv_lnx_b': nrm((L, RWKV_WIDTH), 0.01),
        'rwkv_w_out': nrm((L, RWKV_WIDTH, D_MODEL), RWKV_WIDTH ** -0.5),
        'w_out': nrm((L, D_MODEL, D_MODEL), D_MODEL ** -0.5),
        'ffn_w_up': nrm((L, D_MODEL, 2 * D_FF), D_MODEL ** -0.5),
        'ffn_conv_w': nrm((L, 3, 2 * D_FF), 3 ** -0.5),
        'ffn_conv_b': nrm((L, 2 * D_FF), 0.01),
        'ffn_w_down': nrm((L, D_FF, D_MODEL), D_FF ** -0.5),
        'final_norm_g': 1.0 + nrm((D_MODEL,), 0.02),
    }


def reference(x, c, ctx, c_ctx, w_mod, b_mod, norm1_g, norm2_g, w_in,
              s5_a_re, s5_a_im, s5_log_dt, s5_b_re, s5_b_im, s5_c_re, s5_c_im, s5_d, s5_glu_w,
              rwkv_mu, rwkv_w0, rwkv_w_up, rwkv_a0, rwkv_a_up, rwkv_g_up, rwkv_k_k, rwkv_k_a, rwkv_r_k,
              rwkv_lnx_g, rwkv_lnx_b, rwkv_w_out, w_out, ffn_w_up, ffn_conv_w, ffn_conv_b, ffn_w_down,
              final_norm_g):
    out_dtype = x.dtype
    h_x = x.astype(jnp.float32)
    h_c = ctx.astype(jnp.float32)
    cond = jax.nn.silu(c.astype(jnp.float32))
    cond_ctx = jax.nn.silu(c_ctx.astype(jnp.float32))
    for layer in range(DEPTH):
        need_ctx = layer < DEPTH - 1
        p = dict(
            w_in=w_in[layer], w_out=w_out[layer],
            s5_a_re=s5_a_re[layer], s5_a_im=s5_a_im[layer], s5_log_dt=s5_log_dt[layer],
            s5_b_re=s5_b_re[layer], s5_b_im=s5_b_im[layer], s5_c_re=s5_c_re[layer], s5_c_im=s5_c_im[layer],
            s5_d=s5_d[layer], s5_glu_w=s5_glu_w[layer],
            rwkv_mu=rwkv_mu[layer], rwkv_w0=rwkv_w0[layer], rwkv_w_up=rwkv_w_up[layer],
            rwkv_a0=rwkv_a0[layer], rwkv_a_up=rwkv_a_up[layer], rwkv_g_up=rwkv_g_up[layer],
            rwkv_k_k=rwkv_k_k[layer], rwkv_k_a=rwkv_k_a[layer], rwkv_r_k=rwkv_r_k[layer],
            rwkv_lnx_g=rwkv_lnx_g[layer], rwkv_lnx_b=rwkv_lnx_b[layer], rwkv_w_out=rwkv_w_out[layer],
            ffn_w_up=ffn_w_up[layer], ffn_conv_w=ffn_conv_w[layer], ffn_conv_b=ffn_conv_b[layer],
            ffn_w_down=ffn_w_down[layer],
        )
        mod = (cond @ w_mod[layer] + b_mod[layer])[:, None, :]
        mod_c = cond_ctx @ w_mod[layer] + b_mod[layer]
        sh1, sc1, gt1, sh2, sc2, gt2 = jnp.split(mod, 6, -1)
        sh1c, sc1c, gt1c, sh2c, sc2c, gt2c = jnp.split(mod_c, 6, -1)

        h_lat = modulate(rmsnorm(h_x, norm1_g[layer]), sh1, sc1)
        h_ctx = modulate(rmsnorm(h_c, norm1_g[layer]), sh1c, sc1c)
        mix_l, mix_c = parallel_mixer(h_lat, h_ctx, p, need_ctx)
        h_x = h_x + gt1 * mix_l
        h_x = h_x + gt2 * conv_ffn(modulate(rmsnorm(h_x, norm2_g[layer]), sh2, sc2), p)
        if need_ctx:
            h_c = h_c + gt1c * mix_c
            h_c = h_c + gt2c * conv_ffn(modulate(rmsnorm(h_c, norm2_g[layer]), sh2c, sc2c), p)
    return rmsnorm(h_x, final_norm_g).astype(out_dtype)
```

```python
import os
import numpy as np
from contextlib import ExitStack
import concourse.bass as bass
import concourse.mybir as mybir
from concourse.ap import AP
from concourse.bass_utils import run_bass_kernel_spmd

F32 = mybir.dt.float32
BF16 = mybir.dt.bfloat16
AF = mybir.ActivationFunctionType
ALU = mybir.AluOpType
AX = mybir.AxisListType

D = 1024
SEQ = 2048
CTX = 256
NB = 2
N_IN = 6400
D_FF = 2816
TB = 256
PADC = 64
CTX0 = PADC
LAT0 = PADC + CTX + PADC
NCOL = LAT0 + SEQ + PADC
KAPPA = 0.6065306597126334


class Buf:
    _n = 0

    def __init__(self, t, name):
        self.t = t
        self.name = name
        Buf._n += 1
        self.id = Buf._n

    def __getitem__(self, k):
        return self.t[k]


class K:
    def __init__(self, nc, n_dma_sems=32):
        self.nc = nc
        self.eng = {"pe": nc.tensor, "act": nc.scalar, "dve": nc.vector, "pool": nc.gpsimd, "sp": nc.sync}
        self.sem = {}
        self.cnt = {}
        self._ctx = []
        for e in self.eng:
            cm = nc.semaphore("sem_" + e)
            self.sem[e] = cm.__enter__()
            self._ctx.append(cm)
            self.cnt[e] = 0
        self.dma_sems = []
        for i in range(n_dma_sems):
            cm = nc.semaphore("dsem%d" % i)
            self.dma_sems.append([cm.__enter__(), 0])
            self._ctx.append(cm)
        self.dma_rr = 0
        self.seen = {e: {} for e in self.eng}
        self.state = {}
        self.ninst = 0

    def _wait(self, e, tok):
        kind, a, v = tok
        key = (kind, a)
        if self.seen[e].get(key, 0) >= v:
            return
        self.seen[e][key] = v
        if kind == "e":
            if a == e and e == "pe":
                return
            self.eng[e].wait_ge(self.sem[a], v)
        else:
            self.eng[e].wait_ge(self.dma_sems[a][0], v)

    def _deps(self, e, reads, writes):
        toks = []
        for (b, k) in reads:
            st = self.state.get((b.id, k))
            if st and st[0] is not None:
                toks.append(st[0])
        for (b, k) in writes:
            st = self.state.get((b.id, k))
            if st:
                if st[0] is not None:
                    toks.append(st[0])
                toks.extend(st[1])
        for t in toks:
            self._wait(e, t)

    def _record(self, tok, reads, writes):
        for (b, k) in reads:
            st = self.state.setdefault((b.id, k), [None, []])
            if tok[0] == "e":
                st[1] = [t for t in st[1] if not (t[0] == "e" and t[1] == tok[1])]
            st[1].append(tok)
        for (b, k) in writes:
            self.state[(b.id, k)] = [tok, []]

    @staticmethod
    def _norm(lst):
        out = []
        for x in lst:
            if isinstance(x, Buf):
                out.append((x, None))
            else:
                out.append(x)
        return out

    def op(self, e, fn, reads=(), writes=()):
        reads = self._norm(reads)
        writes = self._norm(writes)
        self._deps(e, reads, writes)
        ins = fn(self.eng[e])
        self.cnt[e] += 1
        ins.then_inc(self.sem[e], 1)
        tok = ("e", e, self.cnt[e])
        self._record(tok, reads, writes)
        self.ninst += 1
        return ins

    def dma(self, e, out, in_, reads=(), writes=(), **kw):
        reads = self._norm(reads)
        writes = self._norm(writes)
        self._deps(e, reads, writes)
        i = self.dma_rr
        self.dma_rr = (self.dma_rr + 1) % len(self.dma_sems)
        s = self.dma_sems[i]
        if s[1] > 0:
            self._wait(e, ("d", i, s[1]))
        s[1] += 16
        ins = self.eng[e].dma_start(out=out, in_=in_, **kw)
        ins.then_inc(s[0], 16)
        tok = ("d", i, s[1])
        self._record(tok, reads, writes)
        self.ninst += 1
        return tok

    def finish(self, e="sp"):
        for en in self.eng:
            if self.cnt[en] > 0:
                self._wait(e, ("e", en, self.cnt[en]))
        for i, s in enumerate(self.dma_sems):
            if s[1] > 0:
                self._wait(e, ("d", i, s[1]))

    def close(self):
        for cm in reversed(self._ctx):
            cm.__exit__(None, None, None)


INPUT_SHAPES = {
    'x': [NB, SEQ, D], 'c': [NB, D], 'ctx': [NB, CTX, D], 'c_ctx': [D],
    'w_mod': [D, 6 * D], 'b_mod': [6 * D], 'norm1_g': [D], 'norm2_g': [D], 'w_in': [D, N_IN],
    's5_a_re': [2, 64, 64], 's5_a_im': [2, 64, 64], 's5_log_dt': [2, 64],
    's5_b_re': [2, 64, 64, 16], 's5_b_im': [2, 64, 64, 16], 's5_c_re': [2, 64, 16, 64], 's5_c_im': [2, 64, 16, 64],
    's5_d': [D], 's5_glu_w': [D, 2 * D], 'rwkv_mu': [3328], 'rwkv_w0': [2, D], 'rwkv_w_up': [2, 64, D],
    'rwkv_a0': [2, D], 'rwkv_a_up': [2, 64, D], 'rwkv_g_up': [128, D], 'rwkv_k_k': [D], 'rwkv_k_a': [D],
    'rwkv_r_k': [2, D], 'rwkv_lnx_g': [D], 'rwkv_lnx_b': [D], 'rwkv_w_out': [D, D], 'w_out': [D, D],
    'ffn_w_up': [D, 2 * D_FF], 'ffn_conv_w': [3, 2 * D_FF], 'ffn_conv_b': [2 * D_FF], 'ffn_w_down': [D_FF, D],
    'final_norm_g': [D],
}


def _cb(C):
    n = -(-C // 2048)
    while C % n:
        n += 1
    return C // n


def build_program(stage="all", dbg_specs=None):
    nc = bass.Bass("TRN2", target_bir_lowering=False)
    inp = {n: Buf(nc.dram_tensor(n, s, F32, kind="ExternalInput"), n) for n, s in INPUT_SHAPES.items()}
    outb = Buf(nc.dram_tensor("y", [NB, SEQ, D], F32, kind="ExternalOutput"), "y")
    dbg = {}
    for n, s in (dbg_specs or {}).items():
        dbg[n] = Buf(nc.dram_tensor(n, list(s), F32, kind="ExternalOutput"), n)

    def dram(name, shape, dt=F32):
        return Buf(nc.dram_tensor(name, list(shape), dt), name)

    wbf = {n: dram(n + "_bf", INPUT_SHAPES[n], BF16) for n in
           ['w_in', 's5_glu_w', 'rwkv_w_out', 'w_out', 'ffn_w_up', 'ffn_w_down']}
    ys_s = dram("ys_s", [NB, 2, 8, 128, SEQ])
    wkv_s = dram("wkv_s", [NB, 2, SEQ, D])
    bon_s = dram("bon_s", [NB, 2, 16, 64, SEQ])
    hx1_s = dram("hx1_s", [NB, SEQ, D])
    hT_s = dram("hT_s", [NB, 2, 128, 8 * NCOL], BF16)
    hT_done = set()
    hT_cur = [None]

    es = ExitStack()
    k = K(nc)

    uniq = [0]

    def sbt(stack, name, shape, dt=F32):
        uniq[0] += 1
        nm = "%s_%d" % (name, uniq[0])
        return Buf(stack.enter_context(nc.sbuf_tensor(nm, list(shape), dt)), nm)

    def sb(name, shape, dt=F32):
        return sbt(es, name, shape, dt)

    psb = [Buf(es.enter_context(nc.psum_tensor("psb%d" % i, [128, 512], F32)), "psb%d" % i) for i in range(8)]
    pstate = [0]

    def newps():
        p = psb[pstate[0] % 8]
        pstate[0] += 1
        return p

    def barrier():
        for e in k.eng:
            k.finish(e)

    rr = [0]

    def ew():
        rr[0] += 1
        return "dve" if rr[0] % 2 else "act"

    def mm(out, lhsT, rhs, start, stop, R, W):
        k.op("pe", lambda e: e.matmul(out, lhsT=lhsT, rhs=rhs, start=start, stop=stop), reads=R, writes=W)

    def copy(eng, out, in_, R, W):
        if eng == "act":
            k.op("act", lambda e: e.activation(out=out, in_=in_, func=AF.Copy), reads=R, writes=W)
        else:
            k.op(eng, lambda e: e.tensor_copy(out=out, in_=in_), reads=R, writes=W)

    def dout(name, src_ap, R):
        if name in dbg:
            k.dma("sp", dbg[name].t.ap(), src_ap, reads=R, writes=[dbg[name]])

    ident = sb("ident", [128, 128])
    J128 = sb("J128", [128, 128])
    ones = sb("ones", [128, 128])
    identb = sb("identb", [128, 128], BF16)
    maskL = sb("maskL", [128, TB])
    maskR = sb("maskR", [128, TB])
    mask1 = sb("mask1", [128, TB])
    epsT = sb("epsT", [128, 1])
    eps2T = sb("eps2T", [128, 1])
    hpiT = sb("hpiT", [128, 1])
    for t_ in (ident, J128, ones, maskL, maskR, mask1):
        k.op("pool", lambda e, t_=t_: e.memset(t_.t[:], 1.0), writes=[t_])
    k.op("pool", lambda e: e.memset(epsT.t[:], 1e-6), writes=[epsT])
    k.op("pool", lambda e: e.memset(eps2T.t[:], 64e-5), writes=[eps2T])
    k.op("pool", lambda e: e.memset(hpiT.t[:], float(np.pi / 2)), writes=[hpiT])

    def asel_b(buf, pattern, cm, base, op, view=None):
        v = buf.t[:] if view is None else view
        k.op("pool", lambda e: e.affine_select(out=v, in_=v, pattern=pattern, compare_op=op, fill=0.0,
                                               base=base, channel_multiplier=cm), reads=[buf], writes=[buf])

    asel_b(ident, [[-1, 128]], 1, 0, ALU.is_equal)
    asel_b(J128, [[1, 128]], 1, -127, ALU.is_equal)
    k.op("pool", lambda e: e.memset(maskL.t[:].rearrange("p (a b) -> p a b", b=64)[:, :, 0:1], 0.0), reads=[maskL], writes=[maskL])
    k.op("pool", lambda e: e.memset(maskR.t[:].rearrange("p (a b) -> p a b", b=64)[:, :, 63:64], 0.0), reads=[maskR], writes=[maskR])
    copy("dve", identb.t[:], ident.t[:], [ident], [identb])

    stage_t = sb("stage_t", [128, 128])

    def load_cols(src2d, rows, lanes, dst_fn):
        r0 = 0
        while r0 < rows:
            nr = min(128, rows - r0)
            k.dma("sp", stage_t.t[0:nr, 0:lanes], src2d[r0:r0 + nr, :], reads=[], writes=[stage_t])
            p = newps()
            mm(p.t[0:lanes, 0:nr], stage_t.t[0:nr, 0:lanes], ident.t[0:nr, 0:nr], True, True, [stage_t, ident], [p])
            copy("dve", dst_fn(r0, nr), p.t[0:lanes, 0:nr], [p], [])
            r0 += nr

    def colparam(name, src_buf, pat, lanes, ncols, **kw):
        t = sb(name, [lanes, ncols])
        src = src_buf.t.ap().rearrange(pat, **kw)
        load_cols(src, ncols, lanes, lambda r0, nr: t.t[:, r0:r0 + nr])
        k.state[(t.id, None)] = [("e", "dve", k.cnt["dve"]), []]
        return t

    mu64 = colparam("mu64", inp['rwkv_mu'], "(r l) -> r l", 64, 52, l=64)
    mu128 = colparam("mu128", inp['rwkv_mu'], "(r l) -> r l", 128, 26, l=128)
    w0T = colparam("w0T", inp['rwkv_w0'], "d (h l) -> (d h) l", 64, 32, l=64)
    a0T = colparam("a0T", inp['rwkv_a0'], "d (h l) -> (d h) l", 64, 32, l=64)
    rkT = colparam("rkT", inp['rwkv_r_k'], "d (h l) -> (d h) l", 64, 32, l=64)
    kkT = colparam("kkT", inp['rwkv_k_k'], "(h l) -> h l", 64, 16, l=64)
    kaT = colparam("kaT", inp['rwkv_k_a'], "(h l) -> h l", 64, 16, l=64)
    lnxg = colparam("lnxg", inp['rwkv_lnx_g'], "(h l) -> h l", 128, 8, l=128)
    lnxb = colparam("lnxb", inp['rwkv_lnx_b'], "(h l) -> h l", 128, 8, l=128)
    s5d = colparam("s5d", inp['s5_d'], "(h l) -> h l", 128, 8, l=128)
    g1c = colparam("g1c", inp['norm1_g'], "(h l) -> h l", 128, 8, l=128)
    g2c = colparam("g2c", inp['norm2_g'], "(h l) -> h l", 128, 8, l=128)
    cvw = colparam("cvw", inp['ffn_conv_w'], "j (h l) -> (j h) l", 128, 132, l=128)
    cvb = colparam("cvb", inp['ffn_conv_b'], "(h l) -> h l", 128, 44, l=128)
    bmc = colparam("bmc", inp['b_mod'], "(h l) -> h l", 128, 48, l=128)
    omu64 = sb("omu64", [64, 52])
    omu128 = sb("omu128", [128, 26])
    omka = sb("omka", [64, 16])
    k.op("dve", lambda e: e.tensor_scalar(out=omu64.t[:], in0=mu64.t[:], scalar1=-1.0, scalar2=1.0, op0=ALU.mult, op1=ALU.add), reads=[mu64], writes=[omu64])
    k.op("dve", lambda e: e.tensor_scalar(out=omu128.t[:], in0=mu128.t[:], scalar1=-1.0, scalar2=1.0, op0=ALU.mult, op1=ALU.add), reads=[mu128], writes=[omu128])
    k.op("dve", lambda e: e.tensor_scalar(out=omka.t[:], in0=kaT.t[:], scalar1=-1.0, scalar2=1.0, op0=ALU.mult, op1=ALU.add), reads=[kaT], writes=[omka])

    if stage == "p0a":
        k.finish("sp")
        for e_ in ("act", "dve", "pool", "pe"):
            k.finish(e_)
        return nc
    condT = sb("condT", [128, 8, 3])
    modT = sb("modT", [128, 48, 3])
    s1T = sb("s1T", [128, 8, 3])
    s2T = sb("s2T", [128, 8, 3])
    fgB = sb("fgB", [128, D])
    hT = sb("hT", [128, 8, NCOL], BF16)
    xin = [sb("xin%d" % i, [128, D]) for i in range(2)]
    xn = [sb("xn%d" % i, [128, D]) for i in range(2)]
    junk = sb("junk", [128, D])
    ssq = sb("ssq", [128, 4])

    es0 = ExitStack()
    fgrow = sbt(es0, "fgrow", [1, D])
    k.dma("sp", fgrow.t[:], inp['final_norm_g'].t.ap().rearrange("(o n) -> o n", o=1), writes=[fgrow])
    for hf in range(2):
        p = newps()
        mm(p.t[:, :], ones.t[0:1, 0:128], fgrow.t[0:1, hf * 512:(hf + 1) * 512], True, True, [ones, fgrow], [p])
        copy("dve", fgB.t[:, hf * 512:(hf + 1) * 512], p.t[:, :], [p], [fgB])

    cst_f = [sbt(es0, "cst_f%d" % i, [128, 2048]) for i in range(4)]
    cst_b = [sbt(es0, "cst_b%d" % i, [128, 2048], BF16) for i in range(4)]
    ci = 0
    for n, wb in wbf.items():
        R_, C_ = INPUT_SHAPES[n]
        cb = _cb(C_)
        src = inp[n].t.ap()
        dst = wb.t.ap()
        for rc in range(R_ // 128):
            for c0 in range(0, C_, cb):
                f = cst_f[ci % 4]
                bb = cst_b[ci % 4]
                ce = ["act", "pool"][ci % 2]
                k.dma("sp", f.t[:, 0:cb], src[rc * 128:(rc + 1) * 128, c0:c0 + cb], reads=[], writes=[f])
                copy(ce, bb.t[:, 0:cb], f.t[:, 0:cb], [f], [bb])
                k.dma(ce, dst[rc * 128:(rc + 1) * 128, c0:c0 + cb], bb.t[:, 0:cb], reads=[bb], writes=[wb])
                ci += 1

    if stage == "p0b":
        k.finish("sp")
        for e_ in ("act", "dve", "pool", "pe"):
            k.finish(e_)
        return nc
    c3 = sbt(es0, "c3", [3, D])
    k.dma("sp", c3.t[0:2, :], inp['c'].t.ap(), writes=[c3])
    k.dma("sp", c3.t[2:3, :], inp['c_ctx'].t.ap().rearrange("(o n) -> o n", o=1), reads=[c3], writes=[c3])
    cond3 = sbt(es0, "cond3", [3, D])
    k.op("act", lambda e: e.activation(out=cond3.t[:], in_=c3.t[:], func=AF.Silu), reads=[c3], writes=[cond3])
    for kc in range(8):
        p = newps()
        mm(p.t[:, 0:3], cond3.t[0:3, kc * 128:(kc + 1) * 128], ident.t[0:3, 0:3], True, True, [cond3, ident], [p])
        copy("dve", condT.t[:, kc, :], p.t[:, 0:3], [p], [condT])
    wm = [sbt(es0, "wm%d" % i, [128, 8, 256]) for i in range(2)]
    wmod_v = inp['w_mod'].t.ap().rearrange("(kc p) n -> p kc n", p=128)
    for jb in range(24):
        w_ = wm[jb % 2]
        k.dma("sp", w_.t[:], wmod_v[:, :, jb * 256:(jb + 1) * 256], reads=[], writes=[w_])
        for oc in range(2):
            p = newps()
            for kc in range(8):
                mm(p.t[:, 0:3], w_.t[:, kc, oc * 128:(oc + 1) * 128], condT.t[:, kc, :], kc == 0, kc == 7, [w_, condT], [p])
            col = jb * 2 + oc
            k.op("dve", lambda e, p=p, col=col: e.tensor_scalar(out=modT.t[:, col, :], in0=p.t[:, 0:3], scalar1=bmc.t[:, col:col + 1], scalar2=None, op0=ALU.add),
                 reads=[p, bmc], writes=[modT])
    for i in range(3):
        k.op("dve", lambda e, i=i: e.scalar_tensor_tensor(out=s1T.t[:, :, i], in0=modT.t[:, 8:16, i], scalar=1.0, in1=g1c.t[:, 0:8], op0=ALU.add, op1=ALU.mult),
             reads=[modT, g1c], writes=[s1T])
        k.op("dve", lambda e, i=i: e.scalar_tensor_tensor(out=s2T.t[:, :, i], in0=modT.t[:, 32:40, i], scalar=1.0, in1=g2c.t[:, 0:8], op0=ALU.add, op1=ALU.mult),
             reads=[modT, g2c], writes=[s2T])
    dout("dbg_mod", modT.t[:].rearrange("p a b -> p (a b)"), [modT])
    barrier()
    es0.close()

    if stage == "p0c":
        k.finish("sp")
        for e_ in ("act", "dve", "pool", "pe"):
            k.finish(e_)
        return nc
    for kc_ in range(8):
        k.op("dve", lambda e, kc_=kc_: e.memset(hT.t[:, kc_, :], 0.0), writes=[(hT, kc_)])

    def rms_rstd(srcb, src, dst_col, eps=1e-6, n=D):
        k.op("dve", lambda e: e.tensor_tensor(out=junk.t[:, 0:n], in0=src, in1=src, op=ALU.mult), reads=[srcb], writes=[junk])
        k.op("dve", lambda e: e.tensor_reduce(out=ssq.t[:, 0:1], in_=junk.t[:, 0:n], axis=AX.X, op=ALU.add), reads=[junk], writes=[ssq])
        k.op("dve", lambda e: e.tensor_scalar(out=ssq.t[:, 1:2], in0=ssq.t[:, 0:1], scalar1=1.0 / n, scalar2=eps, op0=ALU.mult, op1=ALU.add), reads=[ssq], writes=[ssq])
        k.op("act", lambda e: e.activation(out=ssq.t[:, 1:2], in_=ssq.t[:, 1:2], func=AF.Sqrt), reads=[ssq], writes=[ssq])
        k.op("dve", lambda e: e.reciprocal(out=dst_col, in_=ssq.t[:, 1:2]), reads=[ssq], writes=[ssq])

    if stage in ("hTa", "hTb", "hTc"):
        xt = xin[0]; xn_ = xn[0]
        k.dma("sp", xt.t[:], inp['x'].t.ap()[0, 0:128, :], reads=[], writes=[xt])
        if stage == "hTa":
            k.op("dve", lambda e: e.tensor_tensor(out=junk.t[:], in0=xt.t[:], in1=xt.t[:], op=ALU.mult), reads=[xt], writes=[junk])
            k.op("dve", lambda e: e.tensor_reduce(out=ssq.t[:, 0:1], in_=junk.t[:], axis=AX.X, op=ALU.add), reads=[junk], writes=[ssq])
            k.op("dve", lambda e: e.tensor_scalar(out=xn_.t[:], in0=xt.t[:], scalar1=ssq.t[:, 0:1], scalar2=None, op0=ALU.mult), reads=[xt, ssq], writes=[xn_])
        elif stage == "hTb":
            rms_rstd(xt, xt.t[:], ssq.t[:, 2:3])
            k.op("dve", lambda e: e.tensor_scalar(out=xn_.t[:], in0=xt.t[:], scalar1=ssq.t[:, 2:3], scalar2=None, op0=ALU.mult), reads=[xt, ssq], writes=[xn_])
        else:
            p = newps()
            mm(p.t[:, 0:128], xt.t[:, 0:128], ident.t[:, :], True, True, [xt, ident], [p])
            k.op("dve", lambda e: e.tensor_scalar(out=xn_.t[:, 0:128], in0=p.t[:, 0:128], scalar1=s1T.t[:, 0, 0:1], scalar2=modT.t[:, 0, 0:1], op0=ALU.mult, op1=ALU.add), reads=[p, s1T, modT], writes=[xn_])
            k.op("dve", lambda e: e.tensor_copy(out=xn_.t[:, 128:1024], in_=xt.t[:, 128:1024]), reads=[xt], writes=[xn_])
        k.dma("sp", dbg["dbg_x"].t.ap(), xn_.t[:], reads=[xn_], writes=[dbg["dbg_x"]])
        k.finish("sp")
        for e_ in ("act", "dve", "pool", "pe"):
            k.finish(e_)
        return nc
    def build_hT(b, rev):
        key_ = (b, 1 if rev else 0)
        if hT_cur[0] == key_:
            return
        hT_cur[0] = key_
        hflat = hT.t[:].rearrange("p a n -> p (a n)")
        if key_ in hT_done:
            k.dma("sp", hflat, hT_s.t.ap()[b, key_[1]], reads=[hT_s] + hT_all, writes=hT_all)
            return
        build_hT_raw(b, rev)
        k.dma("sp", hT_s.t.ap()[b, key_[1]], hflat, reads=hT_all, writes=[hT_s])
        hT_done.add(key_)

    def build_hT_raw(b, rev):
        ti = 0
        for (src3, ntile, item, base, seglen) in ((inp['ctx'], 2, 2, CTX0, CTX), (inp['x'], 16, b, LAT0, SEQ)):
            for i in range(min(ntile, int(os.environ.get('HT_NT', '99')))):
                xt = xin[ti % 2]
                xn_ = xn[ti % 2]
                ti += 1
                k.dma("sp", xt.t[:], src3.t.ap()[b, i * 128:(i + 1) * 128, :], reads=[], writes=[xt])
                rms_rstd(xt, xt.t[:], ssq.t[:, 2:3])
                k.op("dve", lambda e, xt=xt, xn_=xn_: e.tensor_scalar(out=xn_.t[:], in0=xt.t[:], scalar1=ssq.t[:, 2:3], scalar2=None, op0=ALU.mult),
                     reads=[xt, ssq], writes=[xn_])
                dest = base + (i * 128 if not rev else seglen - 128 - i * 128)
                T_ = J128 if rev else ident
                for half in range(2):
                    p = newps()
                    for q in range(4):
                        kc = half * 4 + q
                        mm(p.t[:, q * 128:(q + 1) * 128], xn_.t[:, kc * 128:(kc + 1) * 128], T_.t[:, :], True, True, [xn_, T_], [p])
                    for q in range(0 if os.environ.get('HT_SKIP_EVAC') is None else 9, 4):
                        kc = half * 4 + q
                        if False:
                            k.op("act", lambda e, p=p, q=q, kc=kc: e.activation(out=hT.t[:, kc, dest:dest + 128], in_=p.t[:, q * 128:(q + 1) * 128], func=AF.Identity,
                                                                               scale=s1T.t[:, kc, item:item + 1], bias=modT.t[:, kc, item:item + 1]),
                                 reads=[p, s1T, modT], writes=[(hT, kc)])
                        else:
                            k.op("dve", lambda e, p=p, q=q, kc=kc: e.tensor_scalar(out=hT.t[:, kc, dest:dest + 128], in0=p.t[:, q * 128:(q + 1) * 128],
                                                                                  scalar1=s1T.t[:, kc, item:item + 1], scalar2=modT.t[:, kc, item:item + 1],
                                                                                  op0=ALU.mult, op1=ALU.add),
                                 reads=[p, s1T, modT], writes=[(hT, kc)])

    hT_all = [(hT, kc) for kc in range(8)]

    def bc_last(ap, n):
        return AP(ap.tensor, ap.offset, [list(x) for x in ap.ap] + [[0, n]])

    def tt(eng, out, in0, in1, op, R, W):
        k.op(eng, lambda e: e.tensor_tensor(out=out, in0=in0, in1=in1, op=op), reads=R, writes=W)

    def s5_build_tables(d, gh, T):
        goff = gh * 32
        est = ExitStack()
        f = lambda n, sh: sbt(est, n, sh)
        aTr, aTi, dtT, mag, ang, cs, sn, t1, t2, t3, lamr, lami, rden, lm1, cfr, cfi = [f("s5t%d" % i, [64, 32]) for i in range(16)]
        dtrow = f("dtrow", [1, 32])
        Lr = f("Lr", [64, 9, 32])
        Li = f("Li", [64, 9, 32])
        Br = f("Br", [64, 32, 16]); Bi = f("Bi", [64, 32, 16])
        Bbr = f("Bbr", [64, 32, 16]); Bbi = f("Bbi", [64, 32, 16])
        CTr = f("CTr", [64, 32, 16]); CTi = f("CTi", [64, 32, 16])
        tmpa = f("tmpa", [64, 32, 16]); tmpb = f("tmpb", [64, 32, 16])
        Dr = f("Dr", [64, 8, 9, 16]); Dni = f("Dni", [64, 8, 9, 16])
        T1r = f("T1r", [64, 8, 8, 16]); T1i = f("T1i", [64, 8, 8, 16])
        tq1 = f("tq1", [64, 8, 9, 16]); tq2 = f("tq2", [64, 8, 9, 16])
        ZB = [[sbt(est, "ZB%d%d" % (i, j), [64, 240], BF16) for j in range(2)] for i in range(2)]
        ZD = [[sbt(est, "ZD%d%d" % (i, j), [64, 256], BF16) for j in range(2)] for i in range(2)]
        for i in range(2):
            for j in range(2):
                for z in (ZB[i][j], ZD[i][j]):
                    k.op("pool", lambda e, z=z: e.memset(z.t[:], 0.0), writes=[z])

        def lc(src, dstb):
            load_cols(src, 32, 64, lambda r0, nr: dstb.t[:, r0:r0 + nr])
            k.state[(dstb.id, None)] = [("e", "dve", k.cnt["dve"]), []]
        lc(inp['s5_a_re'].t.ap()[d, goff:goff + 32, :], aTr)
        lc(inp['s5_a_im'].t.ap()[d, goff:goff + 32, :], aTi)
        k.dma("sp", dtrow.t[:], inp['s5_log_dt'].t.ap()[d:d + 1, goff:goff + 32], writes=[dtrow])
        p = newps()
        mm(p.t[0:64, 0:32], ones.t[0:1, 0:64], dtrow.t[0:1, :], True, True, [ones, dtrow], [p])
        k.op("act", lambda e: e.activation(out=dtT.t[:], in_=p.t[0:64, 0:32], func=AF.Exp), reads=[p], writes=[dtT])
        tt("dve", t1.t[:], aTr.t[:], dtT.t[:], ALU.mult, [aTr, dtT], [t1])
        k.op("act", lambda e: e.activation(out=mag.t[:], in_=t1.t[:], func=AF.Exp), reads=[t1], writes=[mag])
        tt("dve", ang.t[:], aTi.t[:], dtT.t[:], ALU.mult, [aTi, dtT], [ang])
        k.op("act", lambda e: e.activation(out=sn.t[:], in_=ang.t[:], func=AF.Sin, scale=0.125), reads=[ang], writes=[sn])
        k.op("act", lambda e: e.activation(out=cs.t[:], in_=ang.t[:], func=AF.Sin, scale=-0.125, bias=hpiT.t[0:64, :]), reads=[ang, hpiT], writes=[cs])
        for _ in range(3):
            tt("dve", t1.t[:], cs.t[:], cs.t[:], ALU.mult, [cs], [t1])
            tt("dve", t2.t[:], sn.t[:], sn.t[:], ALU.mult, [sn], [t2])
            tt("dve", t3.t[:], cs.t[:], sn.t[:], ALU.mult, [cs, sn], [t3])
            tt("dve", cs.t[:], t1.t[:], t2.t[:], ALU.subtract, [t1, t2], [cs])
            tt("dve", sn.t[:], t3.t[:], t3.t[:], ALU.add, [t3], [sn])
        tt("dve", lamr.t[:], mag.t[:], cs.t[:], ALU.mult, [mag, cs], [lamr])
        tt("dve", lami.t[:], mag.t[:], sn.t[:], ALU.mult, [mag, sn], [lami])
        tt("dve", t1.t[:], aTr.t[:], aTr.t[:], ALU.mult, [aTr], [t1])
        tt("dve", t2.t[:], aTi.t[:], aTi.t[:], ALU.mult, [aTi], [t2])
        tt("dve", t1.t[:], t1.t[:], t2.t[:], ALU.add, [t1, t2], [t1])
        k.op("dve", lambda e: e.reciprocal(out=rden.t[:], in_=t1.t[:]), reads=[t1], writes=[rden])
        k.op("dve", lambda e: e.tensor_scalar(out=lm1.t[:], in0=lamr.t[:], scalar1=-1.0, scalar2=None, op0=ALU.add), reads=[lamr], writes=[lm1])
        tt("dve", t1.t[:], lm1.t[:], aTr.t[:], ALU.mult, [lm1, aTr], [t1])
        tt("dve", t2.t[:], lami.t[:], aTi.t[:], ALU.mult, [lami, aTi], [t2])
        tt("dve", t1.t[:], t1.t[:], t2.t[:], ALU.add, [t1, t2], [t1])
        tt("dve", cfr.t[:], t1.t[:], rden.t[:], ALU.mult, [t1, rden], [cfr])
        tt("dve", t1.t[:], lami.t[:], aTr.t[:], ALU.mult, [lami, aTr], [t1])
        tt("dve", t2.t[:], lm1.t[:], aTi.t[:], ALU.mult, [lm1, aTi], [t2])
        tt("dve", t1.t[:], t1.t[:], t2.t[:], ALU.subtract, [t1, t2], [t1])
        tt("dve", cfi.t[:], t1.t[:], rden.t[:], ALU.mult, [t1, rden], [cfi])
        k.op("dve", lambda e: e.memset(Lr.t[:, 0, :], 1.0), writes=[Lr])
        k.op("dve", lambda e: e.memset(Li.t[:, 0, :], 0.0), writes=[Li])
        for q in range(8):
            tt("dve", t1.t[:], Lr.t[:, q, :], lamr.t[:], ALU.mult, [Lr, lamr], [t1])
            tt("dve", t2.t[:], Li.t[:, q, :], lami.t[:], ALU.mult, [Li, lami], [t2])
            tt("dve", Lr.t[:, q + 1, :], t1.t[:], t2.t[:], ALU.subtract, [t1, t2], [Lr])
            tt("dve", t1.t[:], Lr.t[:, q, :], lami.t[:], ALU.mult, [Lr, lami], [t1])
            tt("dve", t2.t[:], Li.t[:, q, :], lamr.t[:], ALU.mult, [Li, lamr], [t2])
            tt("dve", Li.t[:, q + 1, :], t1.t[:], t2.t[:], ALU.add, [t1, t2], [Li])
        copy("dve", cs.t[:], Lr.t[:, 8, :], [Lr], [cs])
        copy("dve", sn.t[:], Li.t[:, 8, :], [Li], [sn])
        for r_ in range(7):
            copy("dve", T['LPr'].t[:, r_, 0, :], cs.t[:], [cs], [T['LPr']])
            copy("dve", T['LPr'].t[:, r_, 1, :], cs.t[:], [cs], [T['LPr']])
            k.op("dve", lambda e, r_=r_: e.tensor_scalar(out=T['LPis'].t[:, r_, 0, :], in0=sn.t[:], scalar1=-1.0, scalar2=None, op0=ALU.mult), reads=[sn], writes=[T['LPis']])
            copy("dve", T['LPis'].t[:, r_, 1, :], sn.t[:], [sn], [T['LPis']])
            if r_ < 6:
                tt("dve", t1.t[:], cs.t[:], cs.t[:], ALU.mult, [cs], [t1])
                tt("dve", t2.t[:], sn.t[:], sn.t[:], ALU.mult, [sn], [t2])
                tt("dve", t3.t[:], cs.t[:], sn.t[:], ALU.mult, [cs, sn], [t3])
                tt("dve", cs.t[:], t1.t[:], t2.t[:], ALU.subtract, [t1, t2], [cs])
                tt("dve", sn.t[:], t3.t[:], t3.t[:], ALU.add, [t3], [sn])
        for (srcn, dstb) in (('s5_b_re', Br), ('s5_b_im', Bi)):
            for q in range(2):
                k.dma("sp", dstb.t[:, q * 16:(q + 1) * 16, :], inp[srcn].t.ap()[d, goff + q * 16:goff + (q + 1) * 16].rearrange("g p c -> p g c"), reads=[dstb], writes=[dstb])
        tt("dve", tmpa.t[:], Br.t[:], bc_last(cfr.t[:], 16), ALU.mult, [Br, cfr], [tmpa])
        tt("dve", tmpb.t[:], Bi.t[:], bc_last(cfi.t[:], 16), ALU.mult, [Bi, cfi], [tmpb])
        tt("dve", Bbr.t[:], tmpa.t[:], tmpb.t[:], ALU.subtract, [tmpa, tmpb], [Bbr])
        tt("dve", tmpa.t[:], Bi.t[:], bc_last(cfr.t[:], 16), ALU.mult, [Bi, cfr], [tmpa])
        tt("dve", tmpb.t[:], Br.t[:], bc_last(cfi.t[:], 16), ALU.mult, [Br, cfi], [tmpb])
        tt("dve", Bbi.t[:], tmpa.t[:], tmpb.t[:], ALU.add, [tmpa, tmpb], [Bbi])
        for (srcn, dstb) in (('s5_c_re', CTr), ('s5_c_im', CTi)):
            cv = inp[srcn].t.ap()[d, goff:goff + 32].rearrange("g c p -> (g c) p")
            for rt in range(4):
                k.dma("sp", stage_t.t[:, 0:64], cv[rt * 128:(rt + 1) * 128, :], reads=[], writes=[stage_t])
                p = newps()
                mm(p.t[0:64, 0:128], stage_t.t[:, 0:64], ident.t[:, :], True, True, [stage_t, ident], [p])
                copy("dve", dstb.t[:, rt * 8:(rt + 1) * 8, :].rearrange("p g c -> p (g c)"), p.t[0:64, 0:128], [p], [dstb])

        def ap4(base_ap, dims):
            return AP(base_ap.tensor, base_ap.offset, [list(base_ap.ap[0])] + dims)
        for gb in range(4):
            g0 = gb * 8
            ctr_v = ap4(CTr.t[:, g0:g0 + 8, :], [[16, 8], [0, 9], [1, 16]])
            cti_v = ap4(CTi.t[:, g0:g0 + 8, :], [[16, 8], [0, 9], [1, 16]])
            lr_v = ap4(Lr.t[:, 0:9, g0:g0 + 8], [[1, 8], [32, 9], [0, 16]])
            li_v = ap4(Li.t[:, 0:9, g0:g0 + 8], [[1, 8], [32, 9], [0, 16]])
            tt("dve", tq1.t[:], ctr_v, lr_v, ALU.mult, [CTr, Lr], [tq1])
            tt("pool", tq2.t[:], cti_v, li_v, ALU.mult, [CTi, Li], [tq2])
            tt("dve", Dr.t[:], tq1.t[:], tq2.t[:], ALU.subtract, [tq1, tq2], [Dr])
            tt("dve", tq1.t[:], ctr_v, li_v, ALU.mult, [CTr, Li], [tq1])
            tt("pool", tq2.t[:], cti_v, lr_v, ALU.mult, [CTi, Lr], [tq2])
            k.op("dve", lambda e: e.scalar_tensor_tensor(out=Dni.t[:], in0=tq1.t[:], scalar=-1.0, in1=tq2.t[:], op0=ALU.mult, op1=ALU.subtract),
                 reads=[tq1, tq2], writes=[Dni])
            bbr_v = ap4(Bbr.t[:, g0:g0 + 8, :], [[16, 8], [0, 8], [1, 16]])
            bbi_v = ap4(Bbi.t[:, g0:g0 + 8, :], [[16, 8], [0, 8], [1, 16]])
            l7r = ap4(Lr.t[:, 7:8, g0:g0 + 8], [[1, 8], [-32, 8], [0, 16]])
            l7i = ap4(Li.t[:, 7:8, g0:g0 + 8], [[1, 8], [-32, 8], [0, 16]])
            q1 = tq1.t[:, :, 0:8, :]
            q2 = tq2.t[:, :, 0:8, :]
            tt("dve", q1, bbr_v, l7r, ALU.mult, [Bbr, Lr], [tq1])
            tt("pool", q2, bbi_v, l7i, ALU.mult, [Bbi, Li], [tq2])
            tt("dve", T1r.t[:], q1, q2, ALU.subtract, [tq1, tq2], [T1r])
            tt("dve", q1, bbi_v, l7r, ALU.mult, [Bbi, Lr], [tq1])
            tt("pool", q2, bbr_v, l7i, ALU.mult, [Bbr, Li], [tq2])
            tt("dve", T1i.t[:], q1, q2, ALU.add, [tq1, tq2], [T1i])
            for (Tsrc, Wt) in ((T1r, T['Wr']), (T1i, T['Wi'])):
                p = newps()
                for gi in range(8):
                    mm(p.t[:, gi * 64:(gi + 1) * 64], Tsrc.t[:, gi, :, :].rearrange("p t c -> p (t c)"), ident.t[0:64, 0:64], True, True, [Tsrc, ident], [p])
                copy(ew(), Wt.t[:, g0:g0 + 8, :].rearrange("p g m -> p (g m)"), p.t[:, :], [p], [Wt])
            copy("act", T['Q'].t[:, g0:g0 + 8, 0, :].rearrange("p g (t c) -> p g t c", c=16), Dr.t[:, :, 1:9, :], [Dr], [T['Q']])
            copy("act", T['Q'].t[:, g0:g0 + 8, 1, :].rearrange("p g (t c) -> p g t c", c=16), Dni.t[:, :, 1:9, :], [Dni], [T['Q']])
            for gq in range(2):
                p = newps()
                for gi4 in range(4):
                    gi = gq * 4 + gi4
                    g = g0 + gi
                    zb = ZB[gi % 2]
                    zd = ZD[gi % 2]
                    copy("pool", zb[0].t[:, 112:128], Bbr.t[:, g, :], [Bbr], [zb[0]])
                    copy("pool", zb[1].t[:, 112:128], Bbi.t[:, g, :], [Bbi], [zb[1]])
                    copy("act", zd[0].t[:, 128:256], Dr.t[:, gi, 0:8, :].rearrange("p t c -> p (t c)"), [Dr], [zd[0]])
                    copy("act", zd[1].t[:, 128:256], Dni.t[:, gi, 0:8, :].rearrange("p t c -> p (t c)"), [Dni], [zd[1]])
                    o = p.t[:, gi4 * 128:(gi4 + 1) * 128]
                    for s_ in range(8):
                        mm(o, zb[0].t[:, 112 - 16 * s_:240 - 16 * s_], zd[0].t[:, 128 - 16 * s_:256 - 16 * s_], s_ == 0, False, [zb[0], zd[0]], [p])
                    for s_ in range(8):
                        mm(o, zb[1].t[:, 112 - 16 * s_:240 - 16 * s_], zd[1].t[:, 128 - 16 * s_:256 - 16 * s_], False, s_ == 7, [zb[1], zd[1]], [p])
                copy(ew(), T['A'].t[:, g0 + gq * 4:g0 + gq * 4 + 4, :].rearrange("p g m -> p (g m)"), p.t[:, :], [p], [T['A']])
        barrier()
        est.close()

    def s5_run(b, d, gh, T, W_):
        up2, Ug2, Hb2, Hbf, Yg, ysblk, wu, T13, T24 = W_
        winv = wbf['w_in'].t.ap().rearrange("(kc p) n -> p kc n", p=128)
        blocks = [(CTX0, 32, None)] + [(LAT0 + 512 * i, 64, 512 * i) for i in range(4)]
        for Hb in Hb2:
            k.op("dve", lambda e, Hb=Hb: e.memset(Hb.t[:], 0.0), writes=[(Hb, 'c0'), (Hb, 'inc')])
        def front(bi):
            col0, nj, tok0 = blocks[bi]
            up, Ug, Hb = up2[bi % 2], Ug2[bi % 2], Hb2[bi % 2]
            ntok = 8 * nj
            gpb = 512 // nj
            for half in range(2):
                w_ = wu[0]
                c00 = gh * 512 + half * 256
                k.dma("sp", w_.t[:], winv[:, :, c00:c00 + 256], reads=[wbf['w_in'], w_], writes=[w_])
                for oc2 in range(2):
                    oc = half * 2 + oc2
                    p = newps()
                    for kc in range(8):
                        mm(p.t[:, 0:ntok], w_.t[:, kc, oc2 * 128:(oc2 + 1) * 128], hT.t[:, kc, col0:col0 + ntok], kc == 0, kc == 7, [w_, (hT, kc)], [p])
                    copy("act", up.t[:, oc, :, 0:nj], p.t[:, 0:ntok].rearrange("p (j t) -> p t j", t=8), [p], [up])
            for gq in range(32 // gpb):
                p = newps()
                for gi in range(gpb):
                    g_ = gq * gpb + gi
                    kt, g8 = g_ // 8, g_ % 8
                    for tau in range(8):
                        mm(p.t[:, gi * nj:(gi + 1) * nj], T['Sel'].t[:, g8 * 8 + tau, :], up.t[:, kt, tau, 0:nj], tau == 0, tau == 7, [T['Sel'], up], [p])
                copy("act", Ug.t[:, gq * gpb:(gq + 1) * gpb, 0:nj], p.t[:, 0:gpb * nj].rearrange("p (g j) -> p g j", j=nj), [p], [Ug])
            for part, Wt in ((0, T['Wr']), (1, T['Wi'])):
                for gq in range(32 // gpb):
                    p = newps()
                    for gi in range(gpb):
                        g_ = gq * gpb + gi
                        mm(p.t[0:64, gi * nj:(gi + 1) * nj], Wt.t[:, g_, :], Ug.t[:, g_, 0:nj], True, True, [Wt, Ug], [p])
                    copy("act", Hb.t[:, part, gq * gpb:(gq + 1) * gpb, 1:nj + 1], p.t[0:64, 0:gpb * nj].rearrange("p (g j) -> p g j", j=nj), [p], [(Hb, 'inc')])

        def scan(bi):
            col0, nj, tok0 = blocks[bi]
            Hb = Hb2[bi % 2]
            HK = [(Hb, 'c0'), (Hb, 'inc')]

            def cstep(dst_c0, src_c0, stride, cnt, r_):
                def colv(c0, swap):
                    base = Hb.t[:, :, :, c0:c0 + 1]
                    if swap:
                        return AP(base.tensor, base.offset + 32 * 65, [list(base.ap[0]), [-32 * 65, 2], [65, 32], [stride, cnt]])
                    return AP(base.tensor, base.offset, [list(base.ap[0]), [32 * 65, 2], [65, 32], [stride, cnt]])
                tt("dve", T13.t[:, :, :, 0:cnt], colv(src_c0, False), bc_last(T['LPr'].t[:, r_, :, :], cnt), ALU.mult, HK + [T['LPr']], [T13])
                tt("pool", T24.t[:, :, :, 0:cnt], colv(src_c0, True), bc_last(T['LPis'].t[:, r_, :, :], cnt), ALU.mult, HK + [T['LPis']], [T24])
                tt("dve", colv(dst_c0, False), colv(dst_c0, False), T13.t[:, :, :, 0:cnt], ALU.add, HK + [T13], [(Hb, 'inc')])
                tt("dve", colv(dst_c0, False), colv(dst_c0, False), T24.t[:, :, :, 0:cnt], ALU.add, HK + [T24], [(Hb, 'inc')])
            cstep(1, 0, 1, 1, 0)
            nr = nj.bit_length() - 1
            for r_ in range(nr):
                s_ = 1 << r_
                cstep(1 + 2 * s_ - 1, 1 + s_ - 1, 2 * s_, nj // (2 * s_), r_)
            for r_ in range(nr - 2, -1, -1):
                s_ = 1 << r_
                cnt = (nj - 3 * s_) // (2 * s_) + 1
                cstep(1 + 3 * s_ - 1, 1 + 2 * s_ - 1, 2 * s_, cnt, r_)
            if bi + 1 < len(blocks):
                Hn = Hb2[(bi + 1) % 2]
                copy("dve", Hn.t[:, :, :, 0], Hb.t[:, :, :, nj], [(Hb, 'inc')], [(Hn, 'c0')])

        def back(bi):
            col0, nj, tok0 = blocks[bi]
            if tok0 is None:
                return
            Ug, Hb = Ug2[bi % 2], Hb2[bi % 2]
            ntok = 8 * nj
            gpb = 512 // nj
            copy("act", Hbf.t[:, :, :, 0:nj], Hb.t[:, :, :, 0:nj], [(Hb, 'c0'), (Hb, 'inc')], [Hbf])
            for gq in range(32 // gpb):
                p = newps()
                for gi in range(gpb):
                    g_ = gq * gpb + gi
                    o = p.t[:, gi * nj:(gi + 1) * nj]
                    mm(o, T['A'].t[:, g_, :], Ug.t[:, g_, 0:nj], True, False, [T['A'], Ug], [p])
                    mm(o, T['Q'].t[:, g_, 0, :], Hbf.t[:, 0, g_, 0:nj], False, False, [T['Q'], Hbf], [p])
                    mm(o, T['Q'].t[:, g_, 1, :], Hbf.t[:, 1, g_, 0:nj], False, True, [T['Q'], Hbf], [p])
                copy(ew(), Yg.t[:, gq * gpb:(gq + 1) * gpb, 0:nj], p.t[:, 0:gpb * nj].rearrange("p (g j) -> p g j", j=nj), [p], [Yg])
            for kt in range(4):
                p = newps()
                for tau in range(8):
                    for g8 in range(8):
                        mm(p.t[:, tau * nj:(tau + 1) * nj], T['Sel'].t[:, tau * 8 + g8, :], Yg.t[:, kt * 8 + g8, 0:nj], g8 == 0, g8 == 7, [T['Sel'], Yg], [p])
                copy(ew(), ysblk.t[:, kt % 2, 0:ntok].rearrange("p (j t) -> p t j", t=8), p.t[:, 0:ntok].rearrange("p (t j) -> p t j", t=8), [p], [ysblk])
                if kt % 2 == 1:
                    k0 = gh * 4 + kt - 1
                    k.dma("pool", ys_s.t.ap()[b, d, k0:k0 + 2, :, tok0:tok0 + ntok].rearrange("kt p t -> p kt t"), ysblk.t[:, :, 0:ntok], reads=[ysblk], writes=[ys_s])

        front(0)
        for bi in range(len(blocks)):
            scan(bi)
            if bi + 1 < len(blocks):
                front(bi + 1)
            back(bi)

    e2 = [0]

    def eng2():
        e2[0] += 1
        return "dve" if e2[0] % 2 else "pool"

    def rwkv_phase(d, b_list):
        esr = ExitStack()
        f = lambda n, sh, dt=F32: sbt(esr, n, sh, dt)
        tri2 = f("tri2", [64, 128])
        m_su = f("m_su", [128, 4, 64]); m_ue = f("m_ue", [128, 4, 64]); m_lt = f("m_lt", [128, 4, 64]); I4 = f("I4", [128, 4, 64])
        bones = f("bones", [128, 128])
        for t_ in (tri2, m_su, m_ue, m_lt, I4):
            k.op("pool", lambda e, t_=t_: e.memset(t_.t[:], 1.0), writes=[t_])
        asel_b(tri2, [[1, 64]], -1, 0, ALU.is_ge, view=tri2.t[:, 0:64])
        asel_b(tri2, [[1, 64]], -1, 0, ALU.is_gt, view=tri2.t[:, 64:128])
        for (mb, pat, cm_, op_) in ((m_su, [[0, 4], [1, 64]], -1, ALU.is_gt), (m_ue, [[0, 4], [1, 64]], -1, ALU.is_ge),
                                    (m_lt, [[0, 4], [-1, 64]], 1, ALU.is_gt), (I4, [[0, 4], [-1, 64]], 1, ALU.is_equal)):
            asel_b(mb, pat, cm_, 0, op_, view=mb.t[0:64])
            k.dma("sp", mb.t[64:128], mb.t[0:64], reads=[mb], writes=[mb])
        k.op("pool", lambda e: e.memset(bones.t[:], 0.0), writes=[bones])
        k.op("pool", lambda e: e.memset(bones.t[0:64, 0:64], 1.0), reads=[bones], writes=[bones])
        k.op("pool", lambda e: e.memset(bones.t[64:128, 64:128], 1.0), reads=[bones], writes=[bones])
        def pair_param(name, src_ap2d, rows):
            t_ = f(name, [128, rows])
            load_cols(src_ap2d, rows, 128, lambda r0, nr: t_.t[:, r0:r0 + nr])
            k.state[(t_.id, None)] = [("e", "dve", k.cnt["dve"]), []]
            return t_
        w0P = pair_param("w0P", inp['rwkv_w0'].t.ap()[d].rearrange("(h l) -> h l", l=128), 8)
        a0P = pair_param("a0P", inp['rwkv_a0'].t.ap()[d].rearrange("(h l) -> h l", l=128), 8)
        rkP = pair_param("rkP", inp['rwkv_r_k'].t.ap()[d].rearrange("(h l) -> h l", l=128), 8)
        kkP = pair_param("kkP", inp['rwkv_k_k'].t.ap().rearrange("(h l) -> h l", l=128), 8)
        kaP = pair_param("kaP", inp['rwkv_k_a'].t.ap().rearrange("(h l) -> h l", l=128), 8)
        omkaP = f("omkaP", [128, 8])
        k.op("dve", lambda e: e.tensor_scalar(out=omkaP.t[:], in0=kaP.t[:], scalar1=-1.0, scalar2=1.0, op0=ALU.mult, op1=ALU.add), reads=[kaP], writes=[omkaP])
        wupb = f("wupb", [64, D], BF16); aupb = f("aupb", [64, D], BF16)
        lst = f("lst", [64, D])
        k.dma("sp", lst.t[:], inp['rwkv_w_up'].t.ap()[d], writes=[lst])
        copy("dve", wupb.t[:], lst.t[:], [lst], [wupb])
        k.dma("sp", lst.t[:], inp['rwkv_a_up'].t.ap()[d], reads=[lst], writes=[lst])
        copy("dve", aupb.t[:], lst.t[:], [lst], [aupb])
        wb4 = [f("wb4_%d" % i, [128, 8, 128], BF16) for i in range(4)]
        AT = f("AT", [128, 8, TB], BF16); RT = f("RT", [128, 8, TB], BF16); BT = f("BT", [128, 8, TB], BF16); KT = f("KT", [128, 8, TB], BF16)
        Vtok = f("Vtok", [128, 8, 4, 64], BF16); BHtok = f("BHtok", [128, 8, 4, 64], BF16); KHtok = f("KHtok", [128, 8, 4, 64], BF16)
        gT = f("gT", [128, 8, 4])
        twd = f("twd", [64, TB], BF16); xad = f("xad", [64, TB], BF16)
        t64 = [f("t64_%d" % i, [64, TB]) for i in range(2)]
        tl = {n: f("t_" + n, [128, TB]) for n in ("r", "k", "v", "tmp", "tmp2", "kk", "asig", "kd", "bb", "sgw", "kkn")}
        tb16 = {n: f("tb_" + n, [128, TB], BF16) for n in ("bh", "kh", "vb")}
        E = {n: f("E_" + n, [128, 4, 64]) for n in ("in", "ex", "inv", "rat", "ds")}
        cpad = [f("cpad%d" % i, [128, 4, 96]) for i in range(2)]; cT = f("cT", [128, 4])
        for t_ in cpad:
            k.op("pool", lambda e, t_=t_: e.memset(t_.t[:], 0.0), writes=[t_])
        S0T = f("S0T", [128, 8, 64]); S0Tb = f("S0Tb", [128, 8, 64], BF16)
        Pb = [[f("Pb%d%d" % (g_, i), [128, 4, 64], BF16) for i in range(2)] for g_ in range(2)]
        PTb = [[f("PTb%d%d" % (g_, i), [128, 4, 64], BF16) for i in range(2)] for g_ in range(2)]
        Nb = [[f("Nb%d%d" % (g_, i), [128, 4, 64], BF16) for i in range(2)] for g_ in range(2)]
        AakT = [f("AakT%d" % i, [128, 4, 64], BF16) for i in range(2)]; ArbT = [f("ArbT%d" % i, [128, 4, 64], BF16) for i in range(2)]; ArkT = [f("ArkT%d" % i, [128, 4, 64], BF16) for i in range(2)]
        Xf = [f("Xf%d" % i, [128, 4, 64], BF16) for i in range(2)]; Ub = f("Ub", [128, 8, 64], BF16); Ych = [f("Ych%d" % i, [128, 8, 64]) for i in range(2)]
        stmp = [f("stmp%d" % i, [128, 4, 64]) for i in range(2)]
        winv = wbf['w_in'].t.ap().rearrange("(kc p) n -> p kc n", p=128)
        wi = [0]
        ych_i = [0]
        HV = (slice(0, 64), slice(64, 128))

        for b in b_list:
            build_hT(b, d == 1)
            k.op("dve", lambda e: e.memset(S0T.t[:], 0.0), writes=[(S0T, 0), (S0T, 1)])
            k.op("dve", lambda e: e.memset(S0Tb.t[:], 0.0), writes=[S0Tb])
            for kb in range(9):
                seg_ctx = kb == 0
                col0 = CTX0 if seg_ctx else LAT0 + (kb - 1) * TB
                tok0 = (kb - 1) * TB

                def offs(c64):
                    if seg_ctx:
                        o = -1 if c64 < 26 else 1
                        return (-o if d == 1 else o), mask1
                    o = [-1, 1, -64, 64][c64 // 13]
                    if d == 1:
                        o = -o
                    return o, (mask1 if abs(o) == 64 else (maskL if o == -1 else maskR))

                def lerp(P, rows, c64, mu_t, omu_t, mcol, dst_ap, tmp_ap):
                    o, msk = offs(c64)
                    k.op("dve", lambda e: e.scalar_tensor_tensor(out=tmp_ap, in0=P.t[rows, 64 + o:64 + o + TB], scalar=mu_t.t[rows, mcol:mcol + 1], in1=msk.t[rows, :], op0=ALU.mult, op1=ALU.mult),
                         reads=[P, mu_t, msk], writes=[tl["tmp"]])
                    k.op("dve", lambda e: e.scalar_tensor_tensor(out=dst_ap, in0=P.t[rows, 64:64 + TB], scalar=omu_t.t[rows, mcol:mcol + 1], in1=tmp_ap, op0=ALU.mult, op1=ALU.add),
                         reads=[P, omu_t, tl["tmp"]], writes=[])

                def proj64(c64, dst):
                    wt = wb4[wi[0] % 4]
                    wi[0] += 1
                    cc = 1024 + c64 * 64
                    k.dma("sp", wt.t[:, :, 0:64], winv[:, :, cc:cc + 64], reads=[wbf['w_in']], writes=[wt])
                    P = newps()
                    for kc in range(8):
                        mm(P.t[0:64, 0:TB + 128], wt.t[:, kc, 0:64], hT.t[:, kc, col0 - 64:col0 + TB + 64], kc == 0, kc == 7, [wt, (hT, kc)], [P])
                    lerp(P, slice(0, 64), c64, mu64, omu64, c64, dst.t[:], tl["tmp"].t[0:64, :])
                    k.state[(dst.id, None)] = [("e", "dve", k.cnt["dve"]), []]

                def proj128(j, dst):
                    wt = wb4[wi[0] % 4]
                    wi[0] += 1
                    cc = 1024 + j * 128
                    k.dma("sp", wt.t[:], winv[:, :, cc:cc + 128], reads=[wbf['w_in']], writes=[wt])
                    P = newps()
                    for kc in range(8):
                        mm(P.t[:, 0:TB + 128], wt.t[:, kc, :], hT.t[:, kc, col0 - 64:col0 + TB + 64], kc == 0, kc == 7, [wt, (hT, kc)], [P])
                    if offs(2 * j) == offs(2 * j + 1):
                        lerp(P, slice(0, 128), 2 * j, mu128, omu128, j, dst.t[:], tl["tmp"].t[:])
                    else:
                        for hp in range(2):
                            lerp(P, HV[hp], 2 * j + hp, mu128, omu128, j, dst.t[HV[hp], :], tl["tmp"].t[HV[hp], :])
                    k.state[(dst.id, None)] = [("e", "dve", k.cnt["dve"]), []]

                proj64(48, t64[0])
                k.op("act", lambda e: e.activation(out=twd.t[:], in_=t64[0].t[:], func=AF.Tanh), reads=[t64[0]], writes=[twd])
                proj64(49, t64[1])
                copy("act", xad.t[:], t64[1].t[:], [t64[1]], [xad])
                for HP in range(8):
                    r_h, k_h, v_h, tmp, tmp2, kk, asig, kd, bb, sgw, kkn = [tl[n] for n in ("r", "k", "v", "tmp", "tmp2", "kk", "asig", "kd", "bb", "sgw", "kkn")]
                    proj128(HP, r_h); proj128(8 + HP, k_h); proj128(16 + HP, v_h)
                    psl = slice(HP * 128, (HP + 1) * 128)
                    p = newps()
                    mm(p.t[:, 0:TB], wupb.t[:, psl], twd.t[:], True, True, [wupb, twd], [p])
                    k.op("dve", lambda e, p=p: e.tensor_scalar(out=tmp2.t[:], in0=p.t[:, 0:TB], scalar1=w0P.t[:, HP:HP + 1], scalar2=None, op0=ALU.add), reads=[p, w0P], writes=[tmp2])
                    k.op("act", lambda e: e.activation(out=sgw.t[:], in_=tmp2.t[:], func=AF.Sigmoid), reads=[tmp2], writes=[sgw])
                    copy("pool", cpad[0].t[:, :, 32:96], sgw.t[:].rearrange("p (c t) -> p c t", t=64), [sgw], [cpad[0]])
                    src_, dst_ = cpad[0], cpad[1]
                    for s_ in (1, 2, 4, 8, 16, 32):
                        tt("dve", dst_.t[:, :, 32:96], src_.t[:, :, 32:96], src_.t[:, :, 32 - s_:96 - s_], ALU.add, [src_], [dst_])
                        src_, dst_ = dst_, src_
                    incl = src_.t[:, :, 32:96]
                    k.op("act", lambda e, incl=incl: e.activation(out=E["in"].t[:], in_=incl, func=AF.Exp, scale=-KAPPA), reads=[src_], writes=[E["in"]])
                    k.op("act", lambda e, incl=incl: e.activation(out=E["inv"].t[:], in_=incl, func=AF.Exp, scale=KAPPA), reads=[src_], writes=[E["inv"]])
                    tt("dve", E["ds"].t[:], incl, sgw.t[:].rearrange("p (c t) -> p c t", t=64), ALU.subtract, [src_, sgw], [E["ds"]])
                    k.op("act", lambda e: e.activation(out=E["ex"].t[:], in_=E["ds"].t[:], func=AF.Exp, scale=-KAPPA), reads=[E["ds"]], writes=[E["ex"]])
                    copy("dve", cT.t[:], src_.t[:, :, 95], [src_], [cT])
                    tt("dve", E["ds"].t[:], incl, bc_last(cT.t[:], 64), ALU.subtract, [src_, cT], [E["ds"]])
                    k.op("act", lambda e: e.activation(out=E["rat"].t[:], in_=E["ds"].t[:], func=AF.Exp, scale=KAPPA), reads=[E["ds"]], writes=[E["rat"]])
                    copy("pool", gT.t[:, HP, :], E["in"].t[:, :, 63], [E["in"]], [gT])
                    p = newps()
                    mm(p.t[:, 0:TB], aupb.t[:, psl], xad.t[:], True, True, [aupb, xad], [p])
                    k.op("dve", lambda e, p=p: e.tensor_scalar(out=tmp2.t[:], in0=p.t[:, 0:TB], scalar1=a0P.t[:, HP:HP + 1], scalar2=None, op0=ALU.add), reads=[p, a0P], writes=[tmp2])
                    k.op("act", lambda e: e.activation(out=asig.t[:], in_=tmp2.t[:], func=AF.Sigmoid), reads=[tmp2], writes=[asig])
                    k.op("pool", lambda e: e.tensor_scalar(out=kk.t[:], in0=k_h.t[:], scalar1=kkP.t[:, HP:HP + 1], scalar2=None, op0=ALU.mult), reads=[k_h, kkP], writes=[kk])
                    tt("pool", tmp.t[:], kk.t[:], kk.t[:], ALU.mult, [kk], [tmp])
                    p = newps()
                    mm(p.t[:, 0:TB], bones.t[:, :], tmp.t[:], True, True, [bones, tmp], [p])
                    k.op("dve", lambda e, p=p: e.tensor_scalar(out=tmp2.t[:], in0=p.t[:, 0:TB], scalar1=1e-24, scalar2=None, op0=ALU.max), reads=[p], writes=[tmp2])
                    k.op("act", lambda e: e.activation(out=tmp2.t[:], in_=tmp2.t[:], func=AF.Sqrt), reads=[tmp2], writes=[tmp2])
                    k.op("dve", lambda e: e.reciprocal(out=tmp2.t[:], in_=tmp2.t[:]), reads=[tmp2], writes=[tmp2])
                    tt("dve", kkn.t[:], kk.t[:], tmp2.t[:], ALU.mult, [kk, tmp2], [kkn])
                    k.op("pool", lambda e: e.tensor_scalar(out=tmp.t[:], in0=asig.t[:], scalar1=kaP.t[:, HP:HP + 1], scalar2=omkaP.t[:, HP:HP + 1], op0=ALU.mult, op1=ALU.add), reads=[asig, kaP, omkaP], writes=[tmp])
                    tt("pool", kd.t[:], k_h.t[:], tmp.t[:], ALU.mult, [k_h, tmp], [kd])
                    tt("pool", bb.t[:], kkn.t[:], asig.t[:], ALU.mult, [kkn, asig], [bb])
                    k.op("dve", lambda e: e.scalar_tensor_tensor(out=tmp.t[:], in0=r_h.t[:], scalar=rkP.t[:, HP:HP + 1], in1=kd.t[:], op0=ALU.mult, op1=ALU.mult), reads=[r_h, rkP, kd], writes=[tmp])
                    if not seg_ctx:
                        p = newps()
                        mm(p.t[:, 0:TB], bones.t[:, :], tmp.t[:], True, True, [bones, tmp], [p])
                        tt("dve", tmp2.t[:], p.t[:, 0:TB], v_h.t[:], ALU.mult, [p, v_h], [tmp2])
                        k.dma("pool", bon_s.t.ap()[b, d, 2 * HP:2 * HP + 2, :, tok0:tok0 + TB].rearrange("h l t -> (h l) t"), tmp2.t[:], reads=[tmp2], writes=[bon_s])
                    v3 = lambda t_: t_.t[:].rearrange("p (c t) -> p c t", t=64)
                    tt("dve", RT.t[:, HP, :].rearrange("p (c t) -> p c t", t=64), v3(r_h), E["in"].t[:], ALU.mult, [r_h, E["in"]], [RT])
                    k.op("dve", lambda e: e.scalar_tensor_tensor(out=AT.t[:, HP, :].rearrange("p (c t) -> p c t", t=64), in0=v3(kkn), scalar=-1.0, in1=E["ex"].t[:], op0=ALU.mult, op1=ALU.mult),
                         reads=[kkn, E["ex"]], writes=[AT])
                    tt("dve", BT.t[:, HP, :].rearrange("p (c t) -> p c t", t=64), v3(bb), E["inv"].t[:], ALU.mult, [bb, E["inv"]], [BT])
                    tt("pool", KT.t[:, HP, :].rearrange("p (c t) -> p c t", t=64), v3(kd), E["inv"].t[:], ALU.mult, [kd, E["inv"]], [KT])
                    tt("dve", v3(tb16["bh"]), v3(bb), E["rat"].t[:], ALU.mult, [bb, E["rat"]], [tb16["bh"]])
                    tt("pool", v3(tb16["kh"]), v3(kd), E["rat"].t[:], ALU.mult, [kd, E["rat"]], [tb16["kh"]])
                    copy("act", tb16["vb"].t[:], v_h.t[:], [v_h], [tb16["vb"]])
                    for (srcb, dstb) in ((tb16["vb"], Vtok), (tb16["bh"], BHtok), (tb16["kh"], KHtok)):
                        p = newps()
                        for c in range(4):
                            for hp in range(2):
                                mm(p.t[HV[hp], c * 64:(c + 1) * 64], srcb.t[HV[hp], c * 64:(c + 1) * 64], identb.t[HV[hp], HV[hp]], True, True, [srcb, identb], [p])
                        copy(ew(), dstb.t[:, HP, :, :].rearrange("p c t -> p (c t)"), p.t[:, 0:256], [p], [dstb])
                for c in range(4):
                    sl = slice(c * 64, (c + 1) * 64)
                    Yc = Ych[ych_i[0] % 2]
                    ych_i[0] += 1
                    v4 = lambda pb: pb.t[:, 0:256].rearrange("p (h t) -> p h t", t=64)

                    def heads(hg):
                        for hi in range(4):
                            for hp in range(2):
                                yield hi, 4 * hg + hi, HV[hp], slice(hi * 64, (hi + 1) * 64)
                    stt_ = [None, None]
                    for hg in range(2):
                        pAB, pRB, pAK, pRK, pA = newps(), newps(), newps(), newps(), newps()
                        for hi, HP, hv, o_ in heads(hg):
                            mm(pAB.t[hv, o_], BT.t[hv, HP, sl], AT.t[hv, HP, sl], True, True, [BT, AT], [pAB])
                            mm(pRB.t[hv, o_], BT.t[hv, HP, sl], RT.t[hv, HP, sl], True, True, [BT, RT], [pRB])
                            mm(pAK.t[hv, o_], KT.t[hv, HP, sl], AT.t[hv, HP, sl], True, True, [KT, AT], [pAK])
                            mm(pRK.t[hv, o_], KT.t[hv, HP, sl], RT.t[hv, HP, sl], True, True, [KT, RT], [pRK])
                            mm(pA.t[hv, o_], AT.t[hv, HP, sl], BT.t[hv, HP, sl], True, True, [AT, BT], [pA])
                        Pc, PTc, Nc = Pb[hg][0], PTb[hg][0], Nb[hg][0]
                        tt("dve", Pc.t[:], v4(pAB), m_su.t[:], ALU.mult, [pAB, m_su], [Pc])
                        tt("dve", PTc.t[:], v4(pA), m_lt.t[:], ALU.mult, [pA, m_lt], [PTc])
                        tt("dve", AakT[hg].t[:], v4(pAK), m_su.t[:], ALU.mult, [pAK, m_su], [AakT[hg]])
                        tt("dve", ArbT[hg].t[:], v4(pRB), m_ue.t[:], ALU.mult, [pRB, m_ue], [ArbT[hg]])
                        tt("dve", ArkT[hg].t[:], v4(pRK), m_ue.t[:], ALU.mult, [pRK, m_ue], [ArkT[hg]])
                        tt("pool", Nc.t[:], Pc.t[:], I4.t[:], ALU.add, [Pc, I4], [Nc])
                        stt_[hg] = [Pc, PTc, Nc, 0]
                    for j in range(1, 6):
                        nxt = [None, None]
                        for hg in range(2):
                            Pc, PTc, Nc, cur = stt_[hg]
                            Pn, PTn, Nn = Pb[hg][1 - cur], PTb[hg][1 - cur], Nb[hg][1 - cur]
                            pPT = newps()
                            for hi, HP, hv, o_ in heads(hg):
                                mm(pPT.t[hv, o_], Pc.t[hv, hi, :], PTc.t[hv, hi, :], True, True, [Pc, PTc], [pPT])
                            copy("act" if hg == 0 else "dve", PTn.t[:], v4(pPT), [pPT], [PTn])
                            if j < 5:
                                pP = newps()
                                for hi, HP, hv, o_ in heads(hg):
                                    mm(pP.t[hv, o_], PTc.t[hv, hi, :], Pc.t[hv, hi, :], True, True, [Pc, PTc], [pP])
                                copy("dve" if hg == 0 else "act", Pn.t[:], v4(pP), [pP], [Pn])
                            nxt[hg] = (Pn, PTn, Nn)
                        for hg in range(2):
                            Pc, PTc, Nc, cur = stt_[hg]
                            Pn, PTn, Nn = nxt[hg]
                            pN = newps()
                            for hi, HP, hv, o_ in heads(hg):
                                mm(pN.t[hv, o_], PTn.t[hv, hi, :], Nc.t[hv, hi, :], True, True, [PTn, Nc], [pN])
                            tt("dve", Nn.t[:], v4(pN), Nc.t[:], ALU.add, [pN, Nc], [Nn])
                            stt_[hg] = [Pn, PTn, Nn, 1 - cur]
                    for hg in range(2):
                        pX = newps()
                        for hi, HP, hv, o_ in heads(hg):
                            mm(pX.t[hv, o_], AT.t[hv, HP, sl], S0Tb.t[hv, HP, :], True, False, [AT, S0Tb], [pX])
                            mm(pX.t[hv, o_], AakT[hg].t[hv, hi, :], Vtok.t[hv, HP, c, :], False, True, [AakT[hg], Vtok], [pX])
                        copy("act", Xf[hg].t[:], v4(pX), [pX], [Xf[hg]])
                    for hg in range(2):
                        Nc = stt_[hg][2]
                        pU = newps()
                        for hi, HP, hv, o_ in heads(hg):
                            mm(pU.t[hv, o_], Nc.t[hv, hi, :], Xf[hg].t[hv, hi, :], True, True, [Nc, Xf[hg]], [pU])
                        copy("dve", Ub.t[:, 4 * hg:4 * hg + 4, :], v4(pU), [pU], [(Ub, hg)])
                    for hg in range(2):
                        pY = newps()
                        for hi, HP, hv, o_ in heads(hg):
                            mm(pY.t[hv, o_], RT.t[hv, HP, sl], S0Tb.t[hv, HP, :], True, False, [RT, S0Tb], [pY])
                            mm(pY.t[hv, o_], ArbT[hg].t[hv, hi, :], Ub.t[hv, HP, :], False, False, [ArbT[hg], (Ub, hg)], [pY])
                            mm(pY.t[hv, o_], ArkT[hg].t[hv, hi, :], Vtok.t[hv, HP, c, :], False, True, [ArkT[hg], Vtok], [pY])
                        copy("act", Yc.t[:, 4 * hg:4 * hg + 4, :], v4(pY), [pY], [Yc])
                    for hg in range(2):
                        gsl = slice(4 * hg, 4 * hg + 4)
                        pS = newps()
                        for hi, HP, hv, o_ in heads(hg):
                            mm(pS.t[hv, o_], BHtok.t[hv, HP, c, :], Ub.t[hv, HP, :], True, False, [BHtok, (Ub, hg)], [pS])
                            mm(pS.t[hv, o_], KHtok.t[hv, HP, c, :], Vtok.t[hv, HP, c, :], False, True, [KHtok, Vtok], [pS])
                        tt("pool", stmp[hg].t[:], S0T.t[:, gsl, :], bc_last(gT.t[:, gsl, c], 64), ALU.mult, [(S0T, hg), gT], [stmp[hg]])
                        tt("dve", S0T.t[:, gsl, :], v4(pS), stmp[hg].t[:], ALU.add, [pS, stmp[hg]], [(S0T, hg)])
                        copy("act", S0Tb.t[:, gsl, :], S0T.t[:, gsl, :], [(S0T, hg)], [S0Tb])
                    if not seg_ctx:
                        wv_ = wkv_s.t.ap()[b, d, tok0 + c * 64:tok0 + (c + 1) * 64, :].rearrange("t (g hp v) -> t g hp v", hp=2, v=64)
                        for hp in range(2):
                            k.dma("act", wv_[:, :, hp, :], Yc.t[HV[hp], :, :], reads=[Yc], writes=[wkv_s])
        barrier()
        esr.close()

    def rev_last(ap, n):
        dims = [list(x) for x in ap.ap]
        dims[-1] = [-1, n]
        return AP(ap.tensor, ap.offset + n - 1, dims)

    def final_phase(b):
        esf = ExitStack()
        f = lambda n, sh, dt=F32: sbt(esf, n, sh, dt)
        h2T = f("h2T", [128, 8, SEQ + 2], BF16)
        gtBb = f("gtBb", [128, 2, D])
        sgt = f("sgt", [128, TB]); sgt2 = f("sgt2", [128, TB])
        wq = [f("wq%d" % i, [128, 8, 128], BF16) for i in range(4)]
        esA = ExitStack()
        fa = lambda n, sh, dt=F32: sbt(esA, n, sh, dt)
        G = [fa("G%d" % i, [128, 8, TB]) for i in range(5)]
        gl = fa("gl", [128, 8, TB], BF16); rw = fa("rw", [128, 8, TB], BF16); merged = fa("merged", [128, 8, TB], BF16)
        sgb = fa("sgb", [128, TB], BF16)
        wo = fa("wo", [128, 8, 512], BF16)
        condBb = fa("condBb", [128, 8, 128])
        wmf = [fa("wmf%d" % i, [128, 8, 128]) for i in range(2)]
        bmr = fa("bmr", [1, 128])
        st8 = fa("st8", [128, 32])
        gupb = fa("gupb", [128, D], BF16)
        wqi = [0]

        def loadw(name, c0, ncols=128):
            wt = wq[wqi[0] % 4]
            wqi[0] += 1
            k.dma("sp", wt.t[:, :, 0:ncols], wbf[name].t.ap().rearrange("(kc p) n -> p kc n", p=128)[:, :, c0:c0 + ncols], reads=[wbf[name]], writes=[wt])
            return wt

        for kc in range(8):
            k.op("dve", lambda e, kc=kc: e.tensor_copy(out=condBb.t[:, kc, :], in_=condT.t[:, kc, b:b + 1].to_broadcast([128, 128])), reads=[condT], writes=[condBb])
        wmod_v = inp['w_mod'].t.ap().rearrange("(kc p) n -> p kc n", p=128)
        for which, c00 in ((0, 2048), (1, 5120)):
            for q in range(8):
                cc = c00 + q * 128
                w_ = wmf[q % 2]
                k.dma("sp", w_.t[:], wmod_v[:, :, cc:cc + 128], reads=[w_], writes=[w_])
                k.dma("sp", bmr.t[:], inp['b_mod'].t.ap().rearrange("(o n) -> o n", o=1)[:, cc:cc + 128], reads=[bmr], writes=[bmr])
                p = newps()
                for kc in range(8):
                    mm(p.t[:, 0:128], condBb.t[:, kc, :], w_.t[:, kc, :], kc == 0, False, [w_, condBb], [p])
                mm(p.t[:, 0:128], ones.t[0:1, 0:128], bmr.t[0:1, :], False, True, [ones, bmr], [p])
                copy("act", gtBb.t[:, which, q * 128:(q + 1) * 128], p.t[:, 0:128], [p], [gtBb])
        k.dma("sp", xin[0].t[:], inp['rwkv_g_up'].t.ap(), reads=[xin[0]], writes=[xin[0]])
        copy("dve", gupb.t[:], xin[0].t[:], [xin[0]], [gupb])
        for kc_ in range(8):
            k.op("dve", lambda e, kc_=kc_: e.memset(h2T.t[:, kc_, :], 0.0), writes=[h2T])
        build_hT(b, False)

        def mirror(t0, n):
            return SEQ - t0 - n

        for i in range(8):
            t0 = i * TB
            col0 = LAT0 + t0
            m0 = mirror(t0, TB)
            k.dma("sp", G[0].t[:], ys_s.t.ap()[b, 0, :, :, t0:t0 + TB].rearrange("kt p t -> p kt t"), reads=[ys_s, G[0]], writes=[G[0]])
            k.dma("sp", G[1].t[:], ys_s.t.ap()[b, 1, :, :, m0:m0 + TB].rearrange("kt p t -> p kt t"), reads=[ys_s, G[1]], writes=[G[1]])
            for oc in range(8):
                wt = loadw('w_in', oc * 128)
                p = newps()
                for kc in range(8):
                    mm(p.t[:, 0:TB], wt.t[:, kc, :], hT.t[:, kc, col0:col0 + TB], kc == 0, kc == 7, [wt, (hT, kc)], [p])
                k.op("dve", lambda e, p=p, oc=oc: e.scalar_tensor_tensor(out=G[2].t[:, oc, :], in0=p.t[:, 0:TB], scalar=s5d.t[:, oc:oc + 1], in1=G[0].t[:, oc, :], op0=ALU.mult, op1=ALU.add),
                     reads=[p, s5d, G[0]], writes=[G[2]])
                tt("dve", G[2].t[:, oc, :], G[2].t[:, oc, :], rev_last(G[1].t[:, oc, :], TB), ALU.add, [G[2], G[1]], [G[2]])
            for oc in range(8):
                x_ = G[2].t[:, oc, :]
                tt("pool", sgt.t[:], x_, x_, ALU.mult, [G[2]], [sgt])
                k.op("pool", lambda e: e.tensor_scalar(out=sgt.t[:], in0=sgt.t[:], scalar1=0.044715, scalar2=1.0, op0=ALU.mult, op1=ALU.add), reads=[sgt], writes=[sgt])
                tt("pool", sgt.t[:], sgt.t[:], x_, ALU.mult, [sgt, G[2]], [sgt])
                k.op("act", lambda e: e.activation(out=sgt2.t[:], in_=sgt.t[:], func=AF.Sigmoid, scale=1.5957691216), reads=[sgt], writes=[sgt2])
                tt("dve", gl.t[:, oc, :], x_, sgt2.t[:], ALU.mult, [G[2], sgt2], [gl])
            for oc in range(8):
                wa = loadw('s5_glu_w', oc * 128)
                wb_ = loadw('s5_glu_w', D + oc * 128)
                pa = newps(); pb = newps()
                for kc in range(8):
                    mm(pa.t[:, 0:TB], wa.t[:, kc, :], gl.t[:, kc, :], kc == 0, kc == 7, [wa, gl], [pa])
                for kc in range(8):
                    mm(pb.t[:, 0:TB], wb_.t[:, kc, :], gl.t[:, kc, :], kc == 0, kc == 7, [wb_, gl], [pb])
                k.op("act", lambda e, pb=pb: e.activation(out=sgt.t[:], in_=pb.t[:, 0:TB], func=AF.Sigmoid), reads=[pb], writes=[sgt])
                tt("dve", G[3].t[:, oc, :], pa.t[:, 0:TB], sgt.t[:], ALU.mult, [pa, sgt], [G[3]])
            for j in range(2):
                ts = t0 + j * 128
                ms = mirror(ts, 128)
                k.dma("sp", xin[0].t[:], wkv_s.t.ap()[b, 0, ts:ts + 128, :], reads=[wkv_s, xin[0]], writes=[xin[0]])
                k.dma("sp", xin[1].t[:], wkv_s.t.ap()[b, 1, ms:ms + 128, :], reads=[wkv_s, xin[1]], writes=[xin[1]])
                for hf in range(2):
                    p = newps()
                    mm(p.t[:, :], ident.t[:, :], xin[0].t[:, hf * 512:(hf + 1) * 512], True, False, [ident, xin[0]], [p])
                    mm(p.t[:, :], J128.t[:, :], xin[1].t[:, hf * 512:(hf + 1) * 512], False, True, [J128, xin[1]], [p])
                    pv = p.t[:, :].rearrange("p (h v) -> p h v", v=64)
                    xc = xn[0].t[:, hf * 512:(hf + 1) * 512].rearrange("p (h v) -> p h v", v=64)
                    sq = junk.t[:, hf * 512:(hf + 1) * 512].rearrange("p (h v) -> p h v", v=64)
                    yn = xn[1].t[:, hf * 512:(hf + 1) * 512].rearrange("p (h v) -> p h v", v=64)
                    k.op("dve", lambda e, pv=pv: e.tensor_reduce(out=st8.t[:, 0:8], in_=pv, axis=AX.X, op=ALU.add), reads=[p], writes=[st8])
                    k.op("dve", lambda e: e.tensor_scalar(out=st8.t[:, 8:16], in0=st8.t[:, 0:8], scalar1=1.0 / 64, scalar2=None, op0=ALU.mult), reads=[st8], writes=[st8])
                    tt("dve", xc, pv, bc_last(st8.t[:, 8:16], 64), ALU.subtract, [p, st8], [xn[0]])
                    tt("pool", sq, xc, xc, ALU.mult, [xn[0]], [junk])
                    k.op("dve", lambda e, sq=sq: e.tensor_reduce(out=st8.t[:, 16:24], in_=sq, axis=AX.X, op=ALU.add), reads=[junk], writes=[st8])
                    k.op("dve", lambda e: e.tensor_scalar(out=st8.t[:, 16:24], in0=st8.t[:, 16:24], scalar1=1.0 / 64, scalar2=64e-5, op0=ALU.mult, op1=ALU.add), reads=[st8], writes=[st8])
                    k.op("act", lambda e: e.activation(out=st8.t[:, 16:24], in_=st8.t[:, 16:24], func=AF.Sqrt), reads=[st8], writes=[st8])
                    k.op("dve", lambda e: e.reciprocal(out=st8.t[:, 24:32], in_=st8.t[:, 16:24]), reads=[st8], writes=[st8])
                    tt("dve", yn, xc, bc_last(st8.t[:, 24:32], 64), ALU.mult, [xn[0], st8], [xn[1]])
                for half in range(2):
                    p = newps()
                    for q in range(4):
                        kc = half * 4 + q
                        mm(p.t[:, q * 128:(q + 1) * 128], xn[1].t[:, kc * 128:(kc + 1) * 128], ident.t[:, :], True, True, [xn[1], ident], [p])
                    for q in range(4):
                        kc = half * 4 + q
                        k.op("dve", lambda e, p=p, q=q, kc=kc: e.tensor_scalar(out=G[2].t[:, kc, j * 128:(j + 1) * 128], in0=p.t[:, q * 128:(q + 1) * 128],
                                                                              scalar1=lnxg.t[:, kc:kc + 1], scalar2=lnxb.t[:, kc:kc + 1], op0=ALU.mult, op1=ALU.add),
                             reads=[p, lnxg, lnxb], writes=[G[2]])
            bon_v = bon_s.t.ap().rearrange("b d (kt hp) l t -> b d kt (hp l) t", hp=2)
            k.dma("sp", G[0].t[:], bon_v[b, 0, :, :, t0:t0 + TB].rearrange("kt p t -> p kt t"), reads=[bon_s, G[0]], writes=[G[0]])
            k.dma("sp", G[1].t[:], bon_v[b, 1, :, :, m0:m0 + TB].rearrange("kt p t -> p kt t"), reads=[bon_s, G[1]], writes=[G[1]])
            for oc in range(8):
                tt("pool", G[2].t[:, oc, :], G[2].t[:, oc, :], G[0].t[:, oc, :], ALU.add, [G[2], G[0]], [G[2]])
                tt("dve", G[2].t[:, oc, :], G[2].t[:, oc, :], rev_last(G[1].t[:, oc, :], TB), ALU.add, [G[2], G[1]], [G[2]])
            wt = loadw('w_in', 4224)
            P = newps(); Ps = newps()
            for kc in range(8):
                mm(P.t[:, 0:TB], wt.t[:, kc, :], hT.t[:, kc, col0:col0 + TB], kc == 0, kc == 7, [wt, (hT, kc)], [P])
            for kc in range(8):
                mm(Ps.t[:, 0:TB], wt.t[:, kc, :], hT.t[:, kc, col0 + 64:col0 + 64 + TB], kc == 0, kc == 7, [wt, (hT, kc)], [Ps])
            k.op("dve", lambda e: e.tensor_scalar(out=sgt.t[:], in0=Ps.t[:, 0:TB], scalar1=mu128.t[:, 25:26], scalar2=None, op0=ALU.mult), reads=[Ps, mu128], writes=[sgt])
            k.op("dve", lambda e: e.scalar_tensor_tensor(out=sgt2.t[:], in0=P.t[:, 0:TB], scalar=omu128.t[:, 25:26], in1=sgt.t[:], op0=ALU.mult, op1=ALU.add), reads=[P, omu128, sgt], writes=[sgt2])
            k.op("act", lambda e: e.activation(out=sgb.t[:], in_=sgt2.t[:], func=AF.Sigmoid), reads=[sgt2], writes=[sgb])
            for oc in range(8):
                p = newps()
                mm(p.t[:, 0:TB], gupb.t[:, oc * 128:(oc + 1) * 128], sgb.t[:], True, True, [gupb, sgb], [p])
                tt("dve", rw.t[:, oc, :], p.t[:, 0:TB], G[2].t[:, oc, :], ALU.mult, [p, G[2]], [rw])
            for oc in range(8):
                wt = loadw('rwkv_w_out', oc * 128)
                p = newps()
                for kc in range(8):
                    mm(p.t[:, 0:TB], wt.t[:, kc, :], rw.t[:, kc, :], kc == 0, kc == 7, [wt, rw], [p])
                copy("act", G[4].t[:, oc, :], p.t[:, 0:TB], [p], [G[4]])
            for oc in range(8):
                wa = loadw('w_in', 4352 + oc * 128)
                wb_ = loadw('w_in', 5376 + oc * 128)
                pa = newps(); pb = newps()
                for kc in range(8):
                    mm(pa.t[:, 0:TB], wa.t[:, kc, :], hT.t[:, kc, col0:col0 + TB], kc == 0, kc == 7, [wa, (hT, kc)], [pa])
                for kc in range(8):
                    mm(pb.t[:, 0:TB], wb_.t[:, kc, :], hT.t[:, kc, col0:col0 + TB], kc == 0, kc == 7, [wb_, (hT, kc)], [pb])
                k.op("act", lambda e, pa=pa: e.activation(out=sgt.t[:], in_=pa.t[:, 0:TB], func=AF.Sigmoid), reads=[pa], writes=[sgt])
                k.op("act", lambda e, pb=pb: e.activation(out=sgt2.t[:], in_=pb.t[:, 0:TB], func=AF.Sigmoid), reads=[pb], writes=[sgt2])
                tt("dve", sgt.t[:], sgt.t[:], G[3].t[:, oc, :], ALU.mult, [sgt, G[3]], [sgt])
                tt("pool", sgt2.t[:], sgt2.t[:], G[4].t[:, oc, :], ALU.mult, [sgt2, G[4]], [sgt2])
                tt("dve", merged.t[:, oc, :], sgt.t[:], sgt2.t[:], ALU.add, [sgt, sgt2], [merged])
            for j in range(2):
                ts = t0 + j * 128
                k.dma("sp", xin[0].t[:], inp['x'].t.ap()[b, ts:ts + 128, :], reads=[xin[0]], writes=[xin[0]])
                for hf in range(2):
                    k.dma("sp", wo.t[:], wbf['w_out'].t.ap().rearrange("(kc p) n -> p kc n", p=128)[:, :, hf * 512:(hf + 1) * 512], reads=[wbf['w_out'], wo], writes=[wo])
                    p = newps()
                    for kc in range(8):
                        mm(p.t[:, :], merged.t[:, kc, j * 128:(j + 1) * 128], wo.t[:, kc, :], kc == 0, kc == 7, [merged, wo], [p])
                    tt("dve", xn[0].t[:, hf * 512:(hf + 1) * 512], p.t[:, :], gtBb.t[:, 0, hf * 512:(hf + 1) * 512], ALU.mult, [p, gtBb], [xn[0]])
                tt("pool", xn[0].t[:], xn[0].t[:], xin[0].t[:], ALU.add, [xn[0], xin[0]], [xn[0]])
                k.dma("pool", hx1_s.t.ap()[b, ts:ts + 128, :], xn[0].t[:], reads=[xn[0]], writes=[hx1_s])
                rms_rstd(xn[0], xn[0].t[:], ssq.t[:, 2:3])
                k.op("dve", lambda e: e.tensor_scalar(out=xn[1].t[:], in0=xn[0].t[:], scalar1=ssq.t[:, 2:3], scalar2=None, op0=ALU.mult), reads=[xn[0], ssq], writes=[xn[1]])
                for half in range(2):
                    p = newps()
                    for q in range(4):
                        kc = half * 4 + q
                        mm(p.t[:, q * 128:(q + 1) * 128], xn[1].t[:, kc * 128:(kc + 1) * 128], ident.t[:, :], True, True, [xn[1], ident], [p])
                    for q in range(4):
                        kc = half * 4 + q
                        k.op("dve", lambda e, p=p, q=q, kc=kc: e.tensor_scalar(out=h2T.t[:, kc, 1 + ts:1 + ts + 128], in0=p.t[:, q * 128:(q + 1) * 128],
                                                                              scalar1=s2T.t[:, kc, b:b + 1], scalar2=modT.t[:, 24 + kc, b:b + 1], op0=ALU.mult, op1=ALU.add),
                             reads=[p, s2T, modT], writes=[h2T])
        barrier()
        esA.close()
        esB = ExitStack()
        ff = sbt(esB, "ff", [128, 22, TB], BF16)
        wd_ = [sbt(esB, "wdn%d" % i, [128, D], BF16) for i in range(4)]
        wdv = wbf['ffn_w_down'].t.ap()
        wdi = [0]
        for i in range(8):
            t0 = i * TB
            for jc in range(22):
                wg = loadw('ffn_w_up', jc * 128)
                wv = loadw('ffn_w_up', D_FF + jc * 128)
                pg = newps(); pv_ = newps()
                for kc in range(8):
                    mm(pg.t[:, 0:TB + 2], wg.t[:, kc, :], h2T.t[:, kc, t0:t0 + TB + 2], kc == 0, kc == 7, [wg, h2T], [pg])
                for kc in range(8):
                    mm(pv_.t[:, 0:TB + 2], wv.t[:, kc, :], h2T.t[:, kc, t0:t0 + TB + 2], kc == 0, kc == 7, [wv, h2T], [pv_])
                for (pp, ch, dstt) in ((pg, jc, sgt), (pv_, 22 + jc, sgt2)):
                    k.op("dve", lambda e, pp=pp, ch=ch, dstt=dstt: e.tensor_scalar(out=dstt.t[:], in0=pp.t[:, 0:TB], scalar1=cvw.t[:, ch:ch + 1], scalar2=cvb.t[:, ch:ch + 1], op0=ALU.mult, op1=ALU.add),
                         reads=[pp, cvw, cvb], writes=[dstt])
                    k.op("dve", lambda e, pp=pp, ch=ch, dstt=dstt: e.scalar_tensor_tensor(out=dstt.t[:], in0=pp.t[:, 1:TB + 1], scalar=cvw.t[:, 44 + ch:45 + ch], in1=dstt.t[:], op0=ALU.mult, op1=ALU.add),
                         reads=[pp, cvw, dstt], writes=[dstt])
                    k.op("dve", lambda e, pp=pp, ch=ch, dstt=dstt: e.scalar_tensor_tensor(out=dstt.t[:], in0=pp.t[:, 2:TB + 2], scalar=cvw.t[:, 88 + ch:89 + ch], in1=dstt.t[:], op0=ALU.mult, op1=ALU.add),
                         reads=[pp, cvw, dstt], writes=[dstt])
                k.op("act", lambda e: e.activation(out=sgt.t[:], in_=sgt.t[:], func=AF.Silu), reads=[sgt], writes=[sgt])
                tt("pool", ff.t[:, jc, :], sgt.t[:], sgt2.t[:], ALU.mult, [sgt, sgt2], [ff])
            for j in range(2):
                ts = t0 + j * 128
                pa = newps(); pb = newps()
                for kc in range(22):
                    wt = wd_[wdi[0] % 4]
                    wdi[0] += 1
                    k.dma("sp", wt.t[:], wdv[kc * 128:(kc + 1) * 128, :], reads=[wbf['ffn_w_down']], writes=[wt])
                    mm(pa.t[:, :], ff.t[:, kc, j * 128:(j + 1) * 128], wt.t[:, 0:512], kc == 0, kc == 21, [ff, wt], [pa])
                    mm(pb.t[:, :], ff.t[:, kc, j * 128:(j + 1) * 128], wt.t[:, 512:1024], kc == 0, kc == 21, [ff, wt], [pb])
                k.dma("sp", xin[0].t[:], hx1_s.t.ap()[b, ts:ts + 128, :], reads=[hx1_s, xin[0]], writes=[xin[0]])
                tt("dve", xn[0].t[:, 0:512], pa.t[:, :], gtBb.t[:, 1, 0:512], ALU.mult, [pa, gtBb], [xn[0]])
                tt("dve", xn[0].t[:, 512:1024], pb.t[:, :], gtBb.t[:, 1, 512:1024], ALU.mult, [pb, gtBb], [xn[0]])
                tt("pool", xn[0].t[:], xn[0].t[:], xin[0].t[:], ALU.add, [xn[0], xin[0]], [xn[0]])
                rms_rstd(xn[0], xn[0].t[:], ssq.t[:, 2:3])
                k.op("dve", lambda e: e.scalar_tensor_tensor(out=xn[1].t[:], in0=xn[0].t[:], scalar=ssq.t[:, 2:3], in1=fgB.t[:], op0=ALU.mult, op1=ALU.mult), reads=[xn[0], ssq, fgB], writes=[xn[1]])
                k.dma("pool", outb.t.ap()[b, ts:ts + 128, :], xn[1].t[:], reads=[xn[1]], writes=[outb])
        barrier()
        esB.close()
        esf.close()

    def finish():
        k.finish("sp")
        for e in ("act", "dve", "pool", "pe"):
            k.finish(e)
        k.close()
        es.close()

    if stage in ("hT0", "hT1"):
        rev = stage == "hT1"
        build_hT(0, rev)
        hf = sb("hf", [128, NCOL])
        for kc in range(8 if os.environ.get('HT_SKIP_DUMP') is None else 0):
            copy(os.environ.get("DUMPENG", "dve"), hf.t[:], hT.t[:, kc, :], hT_all, [hf])
            k.dma("sp", dbg["dbg_hT"].t.ap()[kc], hf.t[:], reads=[hf], writes=[dbg["dbg_hT"]])
        finish()
        return nc
    def s5_phase(d, b_list):
        es5 = ExitStack()
        Sel_ = sbt(es5, "Sel", [128, 64, 128], BF16)
        k.op("pool", lambda e: e.memset(Sel_.t[:], 1.0), writes=[Sel_])
        sv = Sel_.t[:].rearrange("p (a b) m -> p a b m", a=8)
        for (pat, cm_, base_, op_) in (([[-16, 8], [16, 8], [-1, 128]], 1, 0, ALU.is_equal),
                                       ([[-16, 8], [0, 8], [0, 128]], 1, 0, ALU.is_ge),
                                       ([[16, 8], [0, 8], [0, 128]], -1, 15, ALU.is_ge)):
            asel_b(Sel_, pat, cm_, base_, op_, view=sv)
        T = {'A': sbt(es5, "tA", [128, 32, 128], BF16), 'Wr': sbt(es5, "tWr", [128, 32, 64], BF16), 'Wi': sbt(es5, "tWi", [128, 32, 64], BF16),
             'Q': sbt(es5, "tQ", [64, 32, 2, 128], BF16), 'LPr': sbt(es5, "tLPr", [64, 7, 2, 32]), 'LPis': sbt(es5, "tLPi", [64, 7, 2, 32]), 'Sel': Sel_}
        for gh in range(2):
            s5_build_tables(d, gh, T)
            esw = ExitStack()
            W_ = ([sbt(esw, "up%d" % i, [128, 4, 8, 64], BF16) for i in range(2)], [sbt(esw, "Ug%d" % i, [128, 32, 64], BF16) for i in range(2)],
                  [sbt(esw, "Hb%d" % i, [64, 2, 32, 65]) for i in range(2)],
                  sbt(esw, "Hbf", [64, 2, 32, 64], BF16), sbt(esw, "Yg", [128, 32, 64], BF16), sbt(esw, "ysblk", [128, 2, 512]),
                  [sbt(esw, "wu%d" % i, [128, 8, 256], BF16) for i in range(1)], sbt(esw, "T13", [64, 2, 32, 32]), sbt(esw, "T24", [64, 2, 32, 32]))
            for b in (b_list if gh == 0 else b_list[::-1]):
                build_hT(b, d == 1)
                s5_run(b, d, gh, T, W_)
            barrier()
            esw.close()
        es5.close()
        return None

    if stage in ("s5_0", "s5_1"):
        d = int(stage[-1])
        globals_sel = {}
        s5_phase(d, [0])
        esd = ExitStack()
        yb = sbt(esd, "ybd", [128, 8, TB])
        for blk in range(8):
            k.dma("sp", yb.t[:], ys_s.t.ap()[0, d, :, :, blk * TB:(blk + 1) * TB].rearrange("kt p t -> p kt t"), reads=[ys_s, yb], writes=[yb])
            k.dma("sp", dbg["dbg_ys"].t.ap()[:, :, blk * TB:(blk + 1) * TB].rearrange("kt p t -> p kt t"), yb.t[:], reads=[yb], writes=[dbg["dbg_ys"]])
        barrier()
        esd.close()
        finish()
        return nc
    if stage in ("rw_0", "rw_1"):
        d = int(stage[-1])
        rwkv_phase(d, [0])
        esd = ExitStack()
        yb = sbt(esd, "ybd", [128, D])
        for blk in range(16):
            k.dma("sp", yb.t[:], wkv_s.t.ap()[0, d, blk * 128:(blk + 1) * 128, :], reads=[wkv_s, yb], writes=[yb])
            k.dma("sp", dbg["dbg_wkv"].t.ap()[blk * 128:(blk + 1) * 128, :], yb.t[:], reads=[yb], writes=[dbg["dbg_wkv"]])
        bb_ = sbt(esd, "bbd", [64, 16, 256])
        for blk in range(8):
            k.dma("sp", bb_.t[:], bon_s.t.ap()[0, d, :, :, blk * 256:(blk + 1) * 256].rearrange("h p t -> p h t"), reads=[bon_s, bb_], writes=[bb_])
            k.dma("sp", dbg["dbg_bon"].t.ap()[:, :, blk * 256:(blk + 1) * 256].rearrange("h p t -> p h t"), bb_.t[:], reads=[bb_], writes=[dbg["dbg_bon"]])
        barrier()
        esd.close()
        finish()
        return nc
    if stage == "all":
        for d in range(2):
            s5_phase(d, list(range(NB)))
            rwkv_phase(d, list(range(NB)))
        for b in range(NB):
            final_phase(b)
        finish()
        return nc
    raise NotImplementedError(stage)


_CACHE = {}


def kernel(**inputs):
    n_cores = 8
    if "nc" not in _CACHE:
        _CACHE["nc"] = build_program("all")
    nc = _CACHE["nc"]
    in_maps = []
    for ci in range(n_cores):
        m = {}
        for n, shp in INPUT_SHAPES.items():
            a = np.asarray(inputs[n])
            if n in ('x', 'c', 'ctx'):
                a = a[NB * ci:NB * (ci + 1)]
            m[n] = np.ascontiguousarray(a, dtype=np.float32).reshape(shp)
        in_maps.append(m)
    res = run_bass_kernel_spmd(nc, in_maps, core_ids=list(range(n_cores)))
    out = np.concatenate([r["y"] for r in res.results], axis=0)
    return out.astype(np.float32)
```

```python
import os
import numpy as np
from contextlib import ExitStack
import concourse.bass as bass
import concourse.mybir as mybir
from concourse.ap import AP
from concourse.bass_utils import run_bass_kernel_spmd

F32 = mybir.dt.float32
BF16 = mybir.dt.bfloat16
AF = mybir.ActivationFunctionType
ALU = mybir.AluOpType
AX = mybir.AxisListType

D = 1024
SEQ = 2048
CTX = 256
NB = 2
N_IN = 6400
D_FF = 2816
TB = 256
PADC = 64
CTX0 = PADC
LAT0 = PADC + CTX + PADC
NCOL = LAT0 + SEQ + PADC
KAPPA = 0.6065306597126334


class Buf:
    _n = 0

    def __init__(self, t, name):
        self.t = t
        self.name = name
        Buf._n += 1
        self.id = Buf._n

    def __getitem__(self, k):
        return self.t[k]


class K:
    def __init__(self, nc, n_dma_sems=32):
        self.nc = nc
        self.eng = {"pe": nc.tensor, "act": nc.scalar, "dve": nc.vector, "pool": nc.gpsimd, "sp": nc.sync}
        self.sem = {}
        self.cnt = {}
        self._ctx = []
        for e in self.eng:
            cm = nc.semaphore("sem_" + e)
            self.sem[e] = cm.__enter__()
            self._ctx.append(cm)
            self.cnt[e] = 0
        self.dma_sems = []
        for i in range(n_dma_sems):
            cm = nc.semaphore("dsem%d" % i)
            self.dma_sems.append([cm.__enter__(), 0])
            self._ctx.append(cm)
        self.dma_rr = 0
        self.seen = {e: {} for e in self.eng}
        self.state = {}
        self.ninst = 0

    def _wait(self, e, tok):
        kind, a, v = tok
        key = (kind, a)
        if self.seen[e].get(key, 0) >= v:
            return
        self.seen[e][key] = v
        if kind == "e":
            if a == e and e == "pe":
                return
            self.eng[e].wait_ge(self.sem[a], v)
        else:
            self.eng[e].wait_ge(self.dma_sems[a][0], v)

    def _deps(self, e, reads, writes):
        toks = []
        for (b, k) in reads:
            st = self.state.get((b.id, k))
            if st and st[0] is not None:
                toks.append(st[0])
        for (b, k) in writes:
            st = self.state.get((b.id, k))
            if st:
                if st[0] is not None:
                    toks.append(st[0])
                toks.extend(st[1])
        for t in toks:
            self._wait(e, t)

    def _record(self, tok, reads, writes):
        for (b, k) in reads:
            st = self.state.setdefault((b.id, k), [None, []])
            if tok[0] == "e":
                st[1] = [t for t in st[1] if not (t[0] == "e" and t[1] == tok[1])]
            st[1].append(tok)
        for (b, k) in writes:
            self.state[(b.id, k)] = [tok, []]

    @staticmethod
    def _norm(lst):
        out = []
        for x in lst:
            if isinstance(x, Buf):
                out.append((x, None))
            else:
                out.append(x)
        return out

    def op(self, e, fn, reads=(), writes=()):
        reads = self._norm(reads)
        writes = self._norm(writes)
        self._deps(e, reads, writes)
        ins = fn(self.eng[e])
        self.cnt[e] += 1
        ins.then_inc(self.sem[e], 1)
        tok = ("e", e, self.cnt[e])
        self._record(tok, reads, writes)
        self.ninst += 1
        return ins

    def dma(self, e, out, in_, reads=(), writes=(), **kw):
        reads = self._norm(reads)
        writes = self._norm(writes)
        self._deps(e, reads, writes)
        i = self.dma_rr
        self.dma_rr = (self.dma_rr + 1) % len(self.dma_sems)
        s = self.dma_sems[i]
        if s[1] > 0:
            self._wait(e, ("d", i, s[1]))
        s[1] += 16
        ins = self.eng[e].dma_start(out=out, in_=in_, **kw)
        ins.then_inc(s[0], 16)
        tok = ("d", i, s[1])
        self._record(tok, reads, writes)
        self.ninst += 1
        return tok

    def finish(self, e="sp"):
        for en in self.eng:
            if self.cnt[en] > 0:
                self._wait(e, ("e", en, self.cnt[en]))
        for i, s in enumerate(self.dma_sems):
            if s[1] > 0:
                self._wait(e, ("d", i, s[1]))

    def close(self):
        for cm in reversed(self._ctx):
            cm.__exit__(None, None, None)


INPUT_SHAPES = {
    'x': [NB, SEQ, D], 'c': [NB, D], 'ctx': [NB, CTX, D], 'c_ctx': [D],
    'w_mod': [D, 6 * D], 'b_mod': [6 * D], 'norm1_g': [D], 'norm2_g': [D], 'w_in': [D, N_IN],
    's5_a_re': [2, 64, 64], 's5_a_im': [2, 64, 64], 's5_log_dt': [2, 64],
    's5_b_re': [2, 64, 64, 16], 's5_b_im': [2, 64, 64, 16], 's5_c_re': [2, 64, 16, 64], 's5_c_im': [2, 64, 16, 64],
    's5_d': [D], 's5_glu_w': [D, 2 * D], 'rwkv_mu': [3328], 'rwkv_w0': [2, D], 'rwkv_w_up': [2, 64, D],
    'rwkv_a0': [2, D], 'rwkv_a_up': [2, 64, D], 'rwkv_g_up': [128, D], 'rwkv_k_k': [D], 'rwkv_k_a': [D],
    'rwkv_r_k': [2, D], 'rwkv_lnx_g': [D], 'rwkv_lnx_b': [D], 'rwkv_w_out': [D, D], 'w_out': [D, D],
    'ffn_w_up': [D, 2 * D_FF], 'ffn_conv_w': [3, 2 * D_FF], 'ffn_conv_b': [2 * D_FF], 'ffn_w_down': [D_FF, D],
    'final_norm_g': [D],
}


def _cb(C):
    n = -(-C // 2048)
    while C % n:
        n += 1
    return C // n


def build_program(stage="all", dbg_specs=None):
    nc = bass.Bass("TRN2", target_bir_lowering=False)
    inp = {n: Buf(nc.dram_tensor(n, s, F32, kind="ExternalInput"), n) for n, s in INPUT_SHAPES.items()}
    outb = Buf(nc.dram_tensor("y", [NB, SEQ, D], F32, kind="ExternalOutput"), "y")
    dbg = {}
    for n, s in (dbg_specs or {}).items():
        dbg[n] = Buf(nc.dram_tensor(n, list(s), F32, kind="ExternalOutput"), n)

    def dram(name, shape, dt=F32):
        return Buf(nc.dram_tensor(name, list(shape), dt), name)

    wbf = {n: dram(n + "_bf", INPUT_SHAPES[n], BF16) for n in
           ['w_in', 's5_glu_w', 'rwkv_w_out', 'w_out', 'ffn_w_up', 'ffn_w_down']}
    ys_s = dram("ys_s", [NB, 2, 8, 128, SEQ])
    wkv_s = dram("wkv_s", [NB, 2, SEQ, D])
    bon_s = dram("bon_s", [NB, 2, 16, 64, SEQ])
    hx1_s = dram("hx1_s", [NB, SEQ, D])
    hT_s = dram("hT_s", [NB, 2, 128, 8 * NCOL], BF16)
    hT_done = set()

    es = ExitStack()
    k = K(nc)

    uniq = [0]

    def sbt(stack, name, shape, dt=F32):
        uniq[0] += 1
        nm = "%s_%d" % (name, uniq[0])
        return Buf(stack.enter_context(nc.sbuf_tensor(nm, list(shape), dt)), nm)

    def sb(name, shape, dt=F32):
        return sbt(es, name, shape, dt)

    psb = [Buf(es.enter_context(nc.psum_tensor("psb%d" % i, [128, 512], F32)), "psb%d" % i) for i in range(8)]
    pstate = [0]

    def newps():
        p = psb[pstate[0] % 8]
        pstate[0] += 1
        return p

    def barrier():
        for e in k.eng:
            k.finish(e)

    rr = [0]

    def ew():
        rr[0] += 1
        return "dve" if rr[0] % 2 else "act"

    def mm(out, lhsT, rhs, start, stop, R, W):
        k.op("pe", lambda e: e.matmul(out, lhsT=lhsT, rhs=rhs, start=start, stop=stop), reads=R, writes=W)

    def copy(eng, out, in_, R, W):
        if eng == "act":
            k.op("act", lambda e: e.activation(out=out, in_=in_, func=AF.Copy), reads=R, writes=W)
        else:
            k.op(eng, lambda e: e.tensor_copy(out=out, in_=in_), reads=R, writes=W)

    def dout(name, src_ap, R):
        if name in dbg:
            k.dma("sp", dbg[name].t.ap(), src_ap, reads=R, writes=[dbg[name]])

    ident = sb("ident", [128, 128])
    J128 = sb("J128", [128, 128])
    ones = sb("ones", [128, 128])
    identb = sb("identb", [128, 128], BF16)
    maskL = sb("maskL", [128, TB])
    maskR = sb("maskR", [128, TB])
    mask1 = sb("mask1", [128, TB])
    epsT = sb("epsT", [128, 1])
    eps2T = sb("eps2T", [128, 1])
    hpiT = sb("hpiT", [128, 1])
    for t_ in (ident, J128, ones, maskL, maskR, mask1):
        k.op("pool", lambda e, t_=t_: e.memset(t_.t[:], 1.0), writes=[t_])
    k.op("pool", lambda e: e.memset(epsT.t[:], 1e-6), writes=[epsT])
    k.op("pool", lambda e: e.memset(eps2T.t[:], 64e-5), writes=[eps2T])
    k.op("pool", lambda e: e.memset(hpiT.t[:], float(np.pi / 2)), writes=[hpiT])

    def asel_b(buf, pattern, cm, base, op, view=None):
        v = buf.t[:] if view is None else view
        k.op("pool", lambda e: e.affine_select(out=v, in_=v, pattern=pattern, compare_op=op, fill=0.0,
                                               base=base, channel_multiplier=cm), reads=[buf], writes=[buf])

    asel_b(ident, [[-1, 128]], 1, 0, ALU.is_equal)
    asel_b(J128, [[1, 128]], 1, -127, ALU.is_equal)
    k.op("pool", lambda e: e.memset(maskL.t[:].rearrange("p (a b) -> p a b", b=64)[:, :, 0:1], 0.0), reads=[maskL], writes=[maskL])
    k.op("pool", lambda e: e.memset(maskR.t[:].rearrange("p (a b) -> p a b", b=64)[:, :, 63:64], 0.0), reads=[maskR], writes=[maskR])
    copy("dve", identb.t[:], ident.t[:], [ident], [identb])

    stage_t = sb("stage_t", [128, 128])

    def load_cols(src2d, rows, lanes, dst_fn):
        r0 = 0
        while r0 < rows:
            nr = min(128, rows - r0)
            k.dma("sp", stage_t.t[0:nr, 0:lanes], src2d[r0:r0 + nr, :], reads=[], writes=[stage_t])
            p = newps()
            mm(p.t[0:lanes, 0:nr], stage_t.t[0:nr, 0:lanes], ident.t[0:nr, 0:nr], True, True, [stage_t, ident], [p])
            copy("dve", dst_fn(r0, nr), p.t[0:lanes, 0:nr], [p], [])
            r0 += nr

    def colparam(name, src_buf, pat, lanes, ncols, **kw):
        t = sb(name, [lanes, ncols])
        src = src_buf.t.ap().rearrange(pat, **kw)
        load_cols(src, ncols, lanes, lambda r0, nr: t.t[:, r0:r0 + nr])
        k.state[(t.id, None)] = [("e", "dve", k.cnt["dve"]), []]
        return t

    mu64 = colparam("mu64", inp['rwkv_mu'], "(r l) -> r l", 64, 52, l=64)
    mu128 = colparam("mu128", inp['rwkv_mu'], "(r l) -> r l", 128, 26, l=128)
    w0T = colparam("w0T", inp['rwkv_w0'], "d (h l) -> (d h) l", 64, 32, l=64)
    a0T = colparam("a0T", inp['rwkv_a0'], "d (h l) -> (d h) l", 64, 32, l=64)
    rkT = colparam("rkT", inp['rwkv_r_k'], "d (h l) -> (d h) l", 64, 32, l=64)
    kkT = colparam("kkT", inp['rwkv_k_k'], "(h l) -> h l", 64, 16, l=64)
    kaT = colparam("kaT", inp['rwkv_k_a'], "(h l) -> h l", 64, 16, l=64)
    lnxg = colparam("lnxg", inp['rwkv_lnx_g'], "(h l) -> h l", 128, 8, l=128)
    lnxb = colparam("lnxb", inp['rwkv_lnx_b'], "(h l) -> h l", 128, 8, l=128)
    s5d = colparam("s5d", inp['s5_d'], "(h l) -> h l", 128, 8, l=128)
    g1c = colparam("g1c", inp['norm1_g'], "(h l) -> h l", 128, 8, l=128)
    g2c = colparam("g2c", inp['norm2_g'], "(h l) -> h l", 128, 8, l=128)
    cvw = colparam("cvw", inp['ffn_conv_w'], "j (h l) -> (j h) l", 128, 132, l=128)
    cvb = colparam("cvb", inp['ffn_conv_b'], "(h l) -> h l", 128, 44, l=128)
    bmc = colparam("bmc", inp['b_mod'], "(h l) -> h l", 128, 48, l=128)
    omu64 = sb("omu64", [64, 52])
    omu128 = sb("omu128", [128, 26])
    omka = sb("omka", [64, 16])
    k.op("dve", lambda e: e.tensor_scalar(out=omu64.t[:], in0=mu64.t[:], scalar1=-1.0, scalar2=1.0, op0=ALU.mult, op1=ALU.add), reads=[mu64], writes=[omu64])
    k.op("dve", lambda e: e.tensor_scalar(out=omu128.t[:], in0=mu128.t[:], scalar1=-1.0, scalar2=1.0, op0=ALU.mult, op1=ALU.add), reads=[mu128], writes=[omu128])
    k.op("dve", lambda e: e.tensor_scalar(out=omka.t[:], in0=kaT.t[:], scalar1=-1.0, scalar2=1.0, op0=ALU.mult, op1=ALU.add), reads=[kaT], writes=[omka])

    if stage == "p0a":
        k.finish("sp")
        for e_ in ("act", "dve", "pool", "pe"):
            k.finish(e_)
        return nc
    condT = sb("condT", [128, 8, 3])
    modT = sb("modT", [128, 48, 3])
    s1T = sb("s1T", [128, 8, 3])
    s2T = sb("s2T", [128, 8, 3])
    fgB = sb("fgB", [128, D])
    hT = sb("hT", [128, 8, NCOL], BF16)
    xin = [sb("xin%d" % i, [128, D]) for i in range(2)]
    xn = [sb("xn%d" % i, [128, D]) for i in range(2)]
    junk = sb("junk", [128, D])
    ssq = sb("ssq", [128, 4])

    es0 = ExitStack()
    fgrow = sbt(es0, "fgrow", [1, D])
    k.dma("sp", fgrow.t[:], inp['final_norm_g'].t.ap().rearrange("(o n) -> o n", o=1), writes=[fgrow])
    for hf in range(2):
        p = newps()
        mm(p.t[:, :], ones.t[0:1, 0:128], fgrow.t[0:1, hf * 512:(hf + 1) * 512], True, True, [ones, fgrow], [p])
        copy("dve", fgB.t[:, hf * 512:(hf + 1) * 512], p.t[:, :], [p], [fgB])

    cst_f = [sbt(es0, "cst_f%d" % i, [128, 2048]) for i in range(4)]
    cst_b = [sbt(es0, "cst_b%d" % i, [128, 2048], BF16) for i in range(4)]
    ci = 0
    for n, wb in wbf.items():
        R_, C_ = INPUT_SHAPES[n]
        cb = _cb(C_)
        src = inp[n].t.ap()
        dst = wb.t.ap()
        for rc in range(R_ // 128):
            for c0 in range(0, C_, cb):
                f = cst_f[ci % 4]
                bb = cst_b[ci % 4]
                ce = ["act", "pool"][ci % 2]
                k.dma("sp", f.t[:, 0:cb], src[rc * 128:(rc + 1) * 128, c0:c0 + cb], reads=[], writes=[f])
                copy(ce, bb.t[:, 0:cb], f.t[:, 0:cb], [f], [bb])
                k.dma(ce, dst[rc * 128:(rc + 1) * 128, c0:c0 + cb], bb.t[:, 0:cb], reads=[bb], writes=[wb])
                ci += 1

    if stage == "p0b":
        k.finish("sp")
        for e_ in ("act", "dve", "pool", "pe"):
            k.finish(e_)
        return nc
    c3 = sbt(es0, "c3", [3, D])
    k.dma("sp", c3.t[0:2, :], inp['c'].t.ap(), writes=[c3])
    k.dma("sp", c3.t[2:3, :], inp['c_ctx'].t.ap().rearrange("(o n) -> o n", o=1), reads=[c3], writes=[c3])
    cond3 = sbt(es0, "cond3", [3, D])
    k.op("act", lambda e: e.activation(out=cond3.t[:], in_=c3.t[:], func=AF.Silu), reads=[c3], writes=[cond3])
    for kc in range(8):
        p = newps()
        mm(p.t[:, 0:3], cond3.t[0:3, kc * 128:(kc + 1) * 128], ident.t[0:3, 0:3], True, True, [cond3, ident], [p])
        copy("dve", condT.t[:, kc, :], p.t[:, 0:3], [p], [condT])
    wm = [sbt(es0, "wm%d" % i, [128, 8, 256]) for i in range(2)]
    wmod_v = inp['w_mod'].t.ap().rearrange("(kc p) n -> p kc n", p=128)
    for jb in range(24):
        w_ = wm[jb % 2]
        k.dma("sp", w_.t[:], wmod_v[:, :, jb * 256:(jb + 1) * 256], reads=[], writes=[w_])
        for oc in range(2):
            p = newps()
            for kc in range(8):
                mm(p.t[:, 0:3], w_.t[:, kc, oc * 128:(oc + 1) * 128], condT.t[:, kc, :], kc == 0, kc == 7, [w_, condT], [p])
            col = jb * 2 + oc
            k.op("dve", lambda e, p=p, col=col: e.tensor_scalar(out=modT.t[:, col, :], in0=p.t[:, 0:3], scalar1=bmc.t[:, col:col + 1], scalar2=None, op0=ALU.add),
                 reads=[p, bmc], writes=[modT])
    for i in range(3):
        k.op("dve", lambda e, i=i: e.scalar_tensor_tensor(out=s1T.t[:, :, i], in0=modT.t[:, 8:16, i], scalar=1.0, in1=g1c.t[:, 0:8], op0=ALU.add, op1=ALU.mult),
             reads=[modT, g1c], writes=[s1T])
        k.op("dve", lambda e, i=i: e.scalar_tensor_tensor(out=s2T.t[:, :, i], in0=modT.t[:, 32:40, i], scalar=1.0, in1=g2c.t[:, 0:8], op0=ALU.add, op1=ALU.mult),
             reads=[modT, g2c], writes=[s2T])
    dout("dbg_mod", modT.t[:].rearrange("p a b -> p (a b)"), [modT])
    barrier()
    es0.close()

    if stage == "p0c":
        k.finish("sp")
        for e_ in ("act", "dve", "pool", "pe"):
            k.finish(e_)
        return nc
    for kc_ in range(8):
        k.op("dve", lambda e, kc_=kc_: e.memset(hT.t[:, kc_, :], 0.0), writes=[(hT, kc_)])

    def rms_rstd(srcb, src, dst_col, eps=1e-6, n=D):
        k.op("dve", lambda e: e.tensor_tensor(out=junk.t[:, 0:n], in0=src, in1=src, op=ALU.mult), reads=[srcb], writes=[junk])
        k.op("dve", lambda e: e.tensor_reduce(out=ssq.t[:, 0:1], in_=junk.t[:, 0:n], axis=AX.X, op=ALU.add), reads=[junk], writes=[ssq])
        k.op("dve", lambda e: e.tensor_scalar(out=ssq.t[:, 1:2], in0=ssq.t[:, 0:1], scalar1=1.0 / n, scalar2=eps, op0=ALU.mult, op1=ALU.add), reads=[ssq], writes=[ssq])
        k.op("act", lambda e: e.activation(out=ssq.t[:, 1:2], in_=ssq.t[:, 1:2], func=AF.Sqrt), reads=[ssq], writes=[ssq])
        k.op("dve", lambda e: e.reciprocal(out=dst_col, in_=ssq.t[:, 1:2]), reads=[ssq], writes=[ssq])

    if stage in ("hTa", "hTb", "hTc"):
        xt = xin[0]; xn_ = xn[0]
        k.dma("sp", xt.t[:], inp['x'].t.ap()[0, 0:128, :], reads=[], writes=[xt])
        if stage == "hTa":
            k.op("dve", lambda e: e.tensor_tensor(out=junk.t[:], in0=xt.t[:], in1=xt.t[:], op=ALU.mult), reads=[xt], writes=[junk])
            k.op("dve", lambda e: e.tensor_reduce(out=ssq.t[:, 0:1], in_=junk.t[:], axis=AX.X, op=ALU.add), reads=[junk], writes=[ssq])
            k.op("dve", lambda e: e.tensor_scalar(out=xn_.t[:], in0=xt.t[:], scalar1=ssq.t[:, 0:1], scalar2=None, op0=ALU.mult), reads=[xt, ssq], writes=[xn_])
        elif stage == "hTb":
            rms_rstd(xt, xt.t[:], ssq.t[:, 2:3])
            k.op("dve", lambda e: e.tensor_scalar(out=xn_.t[:], in0=xt.t[:], scalar1=ssq.t[:, 2:3], scalar2=None, op0=ALU.mult), reads=[xt, ssq], writes=[xn_])
        else:
            p = newps()
            mm(p.t[:, 0:128], xt.t[:, 0:128], ident.t[:, :], True, True, [xt, ident], [p])
            k.op("dve", lambda e: e.tensor_scalar(out=xn_.t[:, 0:128], in0=p.t[:, 0:128], scalar1=s1T.t[:, 0, 0:1], scalar2=modT.t[:, 0, 0:1], op0=ALU.mult, op1=ALU.add), reads=[p, s1T, modT], writes=[xn_])
            k.op("dve", lambda e: e.tensor_copy(out=xn_.t[:, 128:1024], in_=xt.t[:, 128:1024]), reads=[xt], writes=[xn_])
        k.dma("sp", dbg["dbg_x"].t.ap(), xn_.t[:], reads=[xn_], writes=[dbg["dbg_x"]])
        k.finish("sp")
        for e_ in ("act", "dve", "pool", "pe"):
            k.finish(e_)
        return nc
    def build_hT(b, rev):
        key_ = (b, 1 if rev else 0)
        hflat = hT.t[:].rearrange("p a n -> p (a n)")
        if key_ in hT_done:
            k.dma("sp", hflat, hT_s.t.ap()[b, key_[1]], reads=[hT_s] + hT_all, writes=hT_all)
            return
        build_hT_raw(b, rev)
        k.dma("sp", hT_s.t.ap()[b, key_[1]], hflat, reads=hT_all, writes=[hT_s])
        hT_done.add(key_)

    def build_hT_raw(b, rev):
        ti = 0
        for (src3, ntile, item, base, seglen) in ((inp['ctx'], 2, 2, CTX0, CTX), (inp['x'], 16, b, LAT0, SEQ)):
            for i in range(min(ntile, int(os.environ.get('HT_NT', '99')))):
                xt = xin[ti % 2]
                xn_ = xn[ti % 2]
                ti += 1
                k.dma("sp", xt.t[:], src3.t.ap()[b, i * 128:(i + 1) * 128, :], reads=[], writes=[xt])
                rms_rstd(xt, xt.t[:], ssq.t[:, 2:3])
                k.op("dve", lambda e, xt=xt, xn_=xn_: e.tensor_scalar(out=xn_.t[:], in0=xt.t[:], scalar1=ssq.t[:, 2:3], scalar2=None, op0=ALU.mult),
                     reads=[xt, ssq], writes=[xn_])
                dest = base + (i * 128 if not rev else seglen - 128 - i * 128)
                T_ = J128 if rev else ident
                for half in range(2):
                    p = newps()
                    for q in range(4):
                        kc = half * 4 + q
                        mm(p.t[:, q * 128:(q + 1) * 128], xn_.t[:, kc * 128:(kc + 1) * 128], T_.t[:, :], True, True, [xn_, T_], [p])
                    for q in range(0 if os.environ.get('HT_SKIP_EVAC') is None else 9, 4):
                        kc = half * 4 + q
                        if False:
                            k.op("act", lambda e, p=p, q=q, kc=kc: e.activation(out=hT.t[:, kc, dest:dest + 128], in_=p.t[:, q * 128:(q + 1) * 128], func=AF.Identity,
                                                                               scale=s1T.t[:, kc, item:item + 1], bias=modT.t[:, kc, item:item + 1]),
                                 reads=[p, s1T, modT], writes=[(hT, kc)])
                        else:
                            k.op("dve", lambda e, p=p, q=q, kc=kc: e.tensor_scalar(out=hT.t[:, kc, dest:dest + 128], in0=p.t[:, q * 128:(q + 1) * 128],
                                                                                  scalar1=s1T.t[:, kc, item:item + 1], scalar2=modT.t[:, kc, item:item + 1],
                                                                                  op0=ALU.mult, op1=ALU.add),
                                 reads=[p, s1T, modT], writes=[(hT, kc)])

    hT_all = [(hT, kc) for kc in range(8)]

    def bc_last(ap, n):
        return AP(ap.tensor, ap.offset, [list(x) for x in ap.ap] + [[0, n]])

    def tt(eng, out, in0, in1, op, R, W):
        k.op(eng, lambda e: e.tensor_tensor(out=out, in0=in0, in1=in1, op=op), reads=R, writes=W)

    def s5_build_tables(d, gh, T):
        goff = gh * 32
        est = ExitStack()
        f = lambda n, sh: sbt(est, n, sh)
        aTr, aTi, dtT, mag, ang, cs, sn, t1, t2, t3, lamr, lami, rden, lm1, cfr, cfi = [f("s5t%d" % i, [64, 32]) for i in range(16)]
        dtrow = f("dtrow", [1, 32])
        Lr = f("Lr", [64, 9, 32])
        Li = f("Li", [64, 9, 32])
        Br = f("Br", [64, 32, 16]); Bi = f("Bi", [64, 32, 16])
        Bbr = f("Bbr", [64, 32, 16]); Bbi = f("Bbi", [64, 32, 16])
        CTr = f("CTr", [64, 32, 16]); CTi = f("CTi", [64, 32, 16])
        tmpa = f("tmpa", [64, 32, 16]); tmpb = f("tmpb", [64, 32, 16])
        Dr = f("Dr", [64, 8, 9, 16]); Dni = f("Dni", [64, 8, 9, 16])
        T1r = f("T1r", [64, 8, 8, 16]); T1i = f("T1i", [64, 8, 8, 16])
        tq1 = f("tq1", [64, 8, 9, 16]); tq2 = f("tq2", [64, 8, 9, 16])
        ZB = [[sbt(est, "ZB%d%d" % (i, j), [64, 240], BF16) for j in range(2)] for i in range(2)]
        ZD = [[sbt(est, "ZD%d%d" % (i, j), [64, 256], BF16) for j in range(2)] for i in range(2)]
        for i in range(2):
            for j in range(2):
                for z in (ZB[i][j], ZD[i][j]):
                    k.op("pool", lambda e, z=z: e.memset(z.t[:], 0.0), writes=[z])

        def lc(src, dstb):
            load_cols(src, 32, 64, lambda r0, nr: dstb.t[:, r0:r0 + nr])
            k.state[(dstb.id, None)] = [("e", "dve", k.cnt["dve"]), []]
        lc(inp['s5_a_re'].t.ap()[d, goff:goff + 32, :], aTr)
        lc(inp['s5_a_im'].t.ap()[d, goff:goff + 32, :], aTi)
        k.dma("sp", dtrow.t[:], inp['s5_log_dt'].t.ap()[d:d + 1, goff:goff + 32], writes=[dtrow])
        p = newps()
        mm(p.t[0:64, 0:32], ones.t[0:1, 0:64], dtrow.t[0:1, :], True, True, [ones, dtrow], [p])
        k.op("act", lambda e: e.activation(out=dtT.t[:], in_=p.t[0:64, 0:32], func=AF.Exp), reads=[p], writes=[dtT])
        tt("dve", t1.t[:], aTr.t[:], dtT.t[:], ALU.mult, [aTr, dtT], [t1])
        k.op("act", lambda e: e.activation(out=mag.t[:], in_=t1.t[:], func=AF.Exp), reads=[t1], writes=[mag])
        tt("dve", ang.t[:], aTi.t[:], dtT.t[:], ALU.mult, [aTi, dtT], [ang])
        k.op("act", lambda e: e.activation(out=sn.t[:], in_=ang.t[:], func=AF.Sin, scale=0.125), reads=[ang], writes=[sn])
        k.op("act", lambda e: e.activation(out=cs.t[:], in_=ang.t[:], func=AF.Sin, scale=-0.125, bias=hpiT.t[0:64, :]), reads=[ang, hpiT], writes=[cs])
        for _ in range(3):
            tt("dve", t1.t[:], cs.t[:], cs.t[:], ALU.mult, [cs], [t1])
            tt("dve", t2.t[:], sn.t[:], sn.t[:], ALU.mult, [sn], [t2])
            tt("dve", t3.t[:], cs.t[:], sn.t[:], ALU.mult, [cs, sn], [t3])
            tt("dve", cs.t[:], t1.t[:], t2.t[:], ALU.subtract, [t1, t2], [cs])
            tt("dve", sn.t[:], t3.t[:], t3.t[:], ALU.add, [t3], [sn])
        tt("dve", lamr.t[:], mag.t[:], cs.t[:], ALU.mult, [mag, cs], [lamr])
        tt("dve", lami.t[:], mag.t[:], sn.t[:], ALU.mult, [mag, sn], [lami])
        tt("dve", t1.t[:], aTr.t[:], aTr.t[:], ALU.mult, [aTr], [t1])
        tt("dve", t2.t[:], aTi.t[:], aTi.t[:], ALU.mult, [aTi], [t2])
        tt("dve", t1.t[:], t1.t[:], t2.t[:], ALU.add, [t1, t2], [t1])
        k.op("dve", lambda e: e.reciprocal(out=rden.t[:], in_=t1.t[:]), reads=[t1], writes=[rden])
        k.op("dve", lambda e: e.tensor_scalar(out=lm1.t[:], in0=lamr.t[:], scalar1=-1.0, scalar2=None, op0=ALU.add), reads=[lamr], writes=[lm1])
        tt("dve", t1.t[:], lm1.t[:], aTr.t[:], ALU.mult, [lm1, aTr], [t1])
        tt("dve", t2.t[:], lami.t[:], aTi.t[:], ALU.mult, [lami, aTi], [t2])
        tt("dve", t1.t[:], t1.t[:], t2.t[:], ALU.add, [t1, t2], [t1])
        tt("dve", cfr.t[:], t1.t[:], rden.t[:], ALU.mult, [t1, rden], [cfr])
        tt("dve", t1.t[:], lami.t[:], aTr.t[:], ALU.mult, [lami, aTr], [t1])
        tt("dve", t2.t[:], lm1.t[:], aTi.t[:], ALU.mult, [lm1, aTi], [t2])
        tt("dve", t1.t[:], t1.t[:], t2.t[:], ALU.subtract, [t1, t2], [t1])
        tt("dve", cfi.t[:], t1.t[:], rden.t[:], ALU.mult, [t1, rden], [cfi])
        k.op("dve", lambda e: e.memset(Lr.t[:, 0, :], 1.0), writes=[Lr])
        k.op("dve", lambda e: e.memset(Li.t[:, 0, :], 0.0), writes=[Li])
        for q in range(8):
            tt("dve", t1.t[:], Lr.t[:, q, :], lamr.t[:], ALU.mult, [Lr, lamr], [t1])
            tt("dve", t2.t[:], Li.t[:, q, :], lami.t[:], ALU.mult, [Li, lami], [t2])
            tt("dve", Lr.t[:, q + 1, :], t1.t[:], t2.t[:], ALU.subtract, [t1, t2], [Lr])
            tt("dve", t1.t[:], Lr.t[:, q, :], lami.t[:], ALU.mult, [Lr, lami], [t1])
            tt("dve", t2.t[:], Li.t[:, q, :], lamr.t[:], ALU.mult, [Li, lamr], [t2])
            tt("dve", Li.t[:, q + 1, :], t1.t[:], t2.t[:], ALU.add, [t1, t2], [Li])
        copy("dve", cs.t[:], Lr.t[:, 8, :], [Lr], [cs])
        copy("dve", sn.t[:], Li.t[:, 8, :], [Li], [sn])
        for r_ in range(7):
            copy("dve", T['LPr'].t[:, r_, 0, :], cs.t[:], [cs], [T['LPr']])
            copy("dve", T['LPr'].t[:, r_, 1, :], cs.t[:], [cs], [T['LPr']])
            k.op("dve", lambda e, r_=r_: e.tensor_scalar(out=T['LPis'].t[:, r_, 0, :], in0=sn.t[:], scalar1=-1.0, scalar2=None, op0=ALU.mult), reads=[sn], writes=[T['LPis']])
            copy("dve", T['LPis'].t[:, r_, 1, :], sn.t[:], [sn], [T['LPis']])
            if r_ < 6:
                tt("dve", t1.t[:], cs.t[:], cs.t[:], ALU.mult, [cs], [t1])
                tt("dve", t2.t[:], sn.t[:], sn.t[:], ALU.mult, [sn], [t2])
                tt("dve", t3.t[:], cs.t[:], sn.t[:], ALU.mult, [cs, sn], [t3])
                tt("dve", cs.t[:], t1.t[:], t2.t[:], ALU.subtract, [t1, t2], [cs])
                tt("dve", sn.t[:], t3.t[:], t3.t[:], ALU.add, [t3], [sn])
        for (srcn, dstb) in (('s5_b_re', Br), ('s5_b_im', Bi)):
            for q in range(2):
                k.dma("sp", dstb.t[:, q * 16:(q + 1) * 16, :], inp[srcn].t.ap()[d, goff + q * 16:goff + (q + 1) * 16].rearrange("g p c -> p g c"), reads=[dstb], writes=[dstb])
        tt("dve", tmpa.t[:], Br.t[:], bc_last(cfr.t[:], 16), ALU.mult, [Br, cfr], [tmpa])
        tt("dve", tmpb.t[:], Bi.t[:], bc_last(cfi.t[:], 16), ALU.mult, [Bi, cfi], [tmpb])
        tt("dve", Bbr.t[:], tmpa.t[:], tmpb.t[:], ALU.subtract, [tmpa, tmpb], [Bbr])
        tt("dve", tmpa.t[:], Bi.t[:], bc_last(cfr.t[:], 16), ALU.mult, [Bi, cfr], [tmpa])
        tt("dve", tmpb.t[:], Br.t[:], bc_last(cfi.t[:], 16), ALU.mult, [Br, cfi], [tmpb])
        tt("dve", Bbi.t[:], tmpa.t[:], tmpb.t[:], ALU.add, [tmpa, tmpb], [Bbi])
        for (srcn, dstb) in (('s5_c_re', CTr), ('s5_c_im', CTi)):
            cv = inp[srcn].t.ap()[d, goff:goff + 32].rearrange("g c p -> (g c) p")
            for rt in range(4):
                k.dma("sp", stage_t.t[:, 0:64], cv[rt * 128:(rt + 1) * 128, :], reads=[], writes=[stage_t])
                p = newps()
                mm(p.t[0:64, 0:128], stage_t.t[:, 0:64], ident.t[:, :], True, True, [stage_t, ident], [p])
                copy("dve", dstb.t[:, rt * 8:(rt + 1) * 8, :].rearrange("p g c -> p (g c)"), p.t[0:64, 0:128], [p], [dstb])

        def ap4(base_ap, dims):
            return AP(base_ap.tensor, base_ap.offset, [list(base_ap.ap[0])] + dims)
        for gb in range(4):
            g0 = gb * 8
            ctr_v = ap4(CTr.t[:, g0:g0 + 8, :], [[16, 8], [0, 9], [1, 16]])
            cti_v = ap4(CTi.t[:, g0:g0 + 8, :], [[16, 8], [0, 9], [1, 16]])
            lr_v = ap4(Lr.t[:, 0:9, g0:g0 + 8], [[1, 8], [32, 9], [0, 16]])
            li_v = ap4(Li.t[:, 0:9, g0:g0 + 8], [[1, 8], [32, 9], [0, 16]])
            tt("dve", tq1.t[:], ctr_v, lr_v, ALU.mult, [CTr, Lr], [tq1])
            tt("pool", tq2.t[:], cti_v, li_v, ALU.mult, [CTi, Li], [tq2])
            tt("dve", Dr.t[:], tq1.t[:], tq2.t[:], ALU.subtract, [tq1, tq2], [Dr])
            tt("dve", tq1.t[:], ctr_v, li_v, ALU.mult, [CTr, Li], [tq1])
            tt("pool", tq2.t[:], cti_v, lr_v, ALU.mult, [CTi, Lr], [tq2])
            k.op("dve", lambda e: e.scalar_tensor_tensor(out=Dni.t[:], in0=tq1.t[:], scalar=-1.0, in1=tq2.t[:], op0=ALU.mult, op1=ALU.subtract),
                 reads=[tq1, tq2], writes=[Dni])
            bbr_v = ap4(Bbr.t[:, g0:g0 + 8, :], [[16, 8], [0, 8], [1, 16]])
            bbi_v = ap4(Bbi.t[:, g0:g0 + 8, :], [[16, 8], [0, 8], [1, 16]])
            l7r = ap4(Lr.t[:, 7:8, g0:g0 + 8], [[1, 8], [-32, 8], [0, 16]])
            l7i = ap4(Li.t[:, 7:8, g0:g0 + 8], [[1, 8], [-32, 8], [0, 16]])
            q1 = tq1.t[:, :, 0:8, :]
            q2 = tq2.t[:, :, 0:8, :]
            tt("dve", q1, bbr_v, l7r, ALU.mult, [Bbr, Lr], [tq1])
            tt("pool", q2, bbi_v, l7i, ALU.mult, [Bbi, Li], [tq2])
            tt("dve", T1r.t[:], q1, q2, ALU.subtract, [tq1, tq2], [T1r])
            tt("dve", q1, bbi_v, l7r, ALU.mult, [Bbi, Lr], [tq1])
            tt("pool", q2, bbr_v, l7i, ALU.mult, [Bbr, Li], [tq2])
            tt("dve", T1i.t[:], q1, q2, ALU.add, [tq1, tq2], [T1i])
            for (Tsrc, Wt) in ((T1r, T['Wr']), (T1i, T['Wi'])):
                p = newps()
                for gi in range(8):
                    mm(p.t[:, gi * 64:(gi + 1) * 64], Tsrc.t[:, gi, :, :].rearrange("p t c -> p (t c)"), ident.t[0:64, 0:64], True, True, [Tsrc, ident], [p])
                copy(ew(), Wt.t[:, g0:g0 + 8, :].rearrange("p g m -> p (g m)"), p.t[:, :], [p], [Wt])
            copy("act", T['Q'].t[:, g0:g0 + 8, 0, :].rearrange("p g (t c) -> p g t c", c=16), Dr.t[:, :, 1:9, :], [Dr], [T['Q']])
            copy("act", T['Q'].t[:, g0:g0 + 8, 1, :].rearrange("p g (t c) -> p g t c", c=16), Dni.t[:, :, 1:9, :], [Dni], [T['Q']])
            for gq in range(2):
                p = newps()
                for gi4 in range(4):
                    gi = gq * 4 + gi4
                    g = g0 + gi
                    zb = ZB[gi % 2]
                    zd = ZD[gi % 2]
                    copy("pool", zb[0].t[:, 112:128], Bbr.t[:, g, :], [Bbr], [zb[0]])
                    copy("pool", zb[1].t[:, 112:128], Bbi.t[:, g, :], [Bbi], [zb[1]])
                    copy("act", zd[0].t[:, 128:256], Dr.t[:, gi, 0:8, :].rearrange("p t c -> p (t c)"), [Dr], [zd[0]])
                    copy("act", zd[1].t[:, 128:256], Dni.t[:, gi, 0:8, :].rearrange("p t c -> p (t c)"), [Dni], [zd[1]])
                    o = p.t[:, gi4 * 128:(gi4 + 1) * 128]
                    for s_ in range(8):
                        mm(o, zb[0].t[:, 112 - 16 * s_:240 - 16 * s_], zd[0].t[:, 128 - 16 * s_:256 - 16 * s_], s_ == 0, False, [zb[0], zd[0]], [p])
                    for s_ in range(8):
                        mm(o, zb[1].t[:, 112 - 16 * s_:240 - 16 * s_], zd[1].t[:, 128 - 16 * s_:256 - 16 * s_], False, s_ == 7, [zb[1], zd[1]], [p])
                copy(ew(), T['A'].t[:, g0 + gq * 4:g0 + gq * 4 + 4, :].rearrange("p g m -> p (g m)"), p.t[:, :], [p], [T['A']])
        barrier()
        est.close()

    def s5_run(b, d, gh, T, W_):
        up2, Ug2, Hb2, Hbf, Yg, ysblk, wu, T13, T24 = W_
        winv = wbf['w_in'].t.ap().rearrange("(kc p) n -> p kc n", p=128)
        blocks = [(CTX0, 32, None)] + [(LAT0 + 512 * i, 64, 512 * i) for i in range(4)]
        for Hb in Hb2:
            k.op("dve", lambda e, Hb=Hb: e.memset(Hb.t[:], 0.0), writes=[(Hb, 'c0'), (Hb, 'inc')])
        def front(bi):
            col0, nj, tok0 = blocks[bi]
            up, Ug, Hb = up2[bi % 2], Ug2[bi % 2], Hb2[bi % 2]
            ntok = 8 * nj
            gpb = 512 // nj
            for half in range(2):
                w_ = wu[0]
                c00 = gh * 512 + half * 256
                k.dma("sp", w_.t[:], winv[:, :, c00:c00 + 256], reads=[wbf['w_in'], w_], writes=[w_])
                for oc2 in range(2):
                    oc = half * 2 + oc2
                    p = newps()
                    for kc in range(8):
                        mm(p.t[:, 0:ntok], w_.t[:, kc, oc2 * 128:(oc2 + 1) * 128], hT.t[:, kc, col0:col0 + ntok], kc == 0, kc == 7, [w_, (hT, kc)], [p])
                    copy("act", up.t[:, oc, :, 0:nj], p.t[:, 0:ntok].rearrange("p (j t) -> p t j", t=8), [p], [up])
            for gq in range(32 // gpb):
                p = newps()
                for gi in range(gpb):
                    g_ = gq * gpb + gi
                    kt, g8 = g_ // 8, g_ % 8
                    for tau in range(8):
                        mm(p.t[:, gi * nj:(gi + 1) * nj], T['Sel'].t[:, g8 * 8 + tau, :], up.t[:, kt, tau, 0:nj], tau == 0, tau == 7, [T['Sel'], up], [p])
                copy("act", Ug.t[:, gq * gpb:(gq + 1) * gpb, 0:nj], p.t[:, 0:gpb * nj].rearrange("p (g j) -> p g j", j=nj), [p], [Ug])
            for part, Wt in ((0, T['Wr']), (1, T['Wi'])):
                for gq in range(32 // gpb):
                    p = newps()
                    for gi in range(gpb):
                        g_ = gq * gpb + gi
                        mm(p.t[0:64, gi * nj:(gi + 1) * nj], Wt.t[:, g_, :], Ug.t[:, g_, 0:nj], True, True, [Wt, Ug], [p])
                    copy("act", Hb.t[:, part, gq * gpb:(gq + 1) * gpb, 1:nj + 1], p.t[0:64, 0:gpb * nj].rearrange("p (g j) -> p g j", j=nj), [p], [(Hb, 'inc')])

        def scan(bi):
            col0, nj, tok0 = blocks[bi]
            Hb = Hb2[bi % 2]
            HK = [(Hb, 'c0'), (Hb, 'inc')]

            def cstep(dst_c0, src_c0, stride, cnt, r_):
                def colv(c0, swap):
                    base = Hb.t[:, :, :, c0:c0 + 1]
                    if swap:
                        return AP(base.tensor, base.offset + 32 * 65, [list(base.ap[0]), [-32 * 65, 2], [65, 32], [stride, cnt]])
                    return AP(base.tensor, base.offset, [list(base.ap[0]), [32 * 65, 2], [65, 32], [stride, cnt]])
                tt("dve", T13.t[:, :, :, 0:cnt], colv(src_c0, False), bc_last(T['LPr'].t[:, r_, :, :], cnt), ALU.mult, HK + [T['LPr']], [T13])
                tt("pool", T24.t[:, :, :, 0:cnt], colv(src_c0, True), bc_last(T['LPis'].t[:, r_, :, :], cnt), ALU.mult, HK + [T['LPis']], [T24])
                tt("dve", colv(dst_c0, False), colv(dst_c0, False), T13.t[:, :, :, 0:cnt], ALU.add, HK + [T13], [(Hb, 'inc')])
                tt("dve", colv(dst_c0, False), colv(dst_c0, False), T24.t[:, :, :, 0:cnt], ALU.add, HK + [T24], [(Hb, 'inc')])
            cstep(1, 0, 1, 1, 0)
            nr = nj.bit_length() - 1
            for r_ in range(nr):
                s_ = 1 << r_
                cstep(1 + 2 * s_ - 1, 1 + s_ - 1, 2 * s_, nj // (2 * s_), r_)
            for r_ in range(nr - 2, -1, -1):
                s_ = 1 << r_
                cnt = (nj - 3 * s_) // (2 * s_) + 1
                cstep(1 + 3 * s_ - 1, 1 + 2 * s_ - 1, 2 * s_, cnt, r_)
            if bi + 1 < len(blocks):
                Hn = Hb2[(bi + 1) % 2]
                copy("dve", Hn.t[:, :, :, 0], Hb.t[:, :, :, nj], [(Hb, 'inc')], [(Hn, 'c0')])

        def back(bi):
            col0, nj, tok0 = blocks[bi]
            if tok0 is None:
                return
            Ug, Hb = Ug2[bi % 2], Hb2[bi % 2]
            ntok = 8 * nj
            gpb = 512 // nj
            copy("act", Hbf.t[:, :, :, 0:nj], Hb.t[:, :, :, 0:nj], [(Hb, 'c0'), (Hb, 'inc')], [Hbf])
            for gq in range(32 // gpb):
                p = newps()
                for gi in range(gpb):
                    g_ = gq * gpb + gi
                    o = p.t[:, gi * nj:(gi + 1) * nj]
                    mm(o, T['A'].t[:, g_, :], Ug.t[:, g_, 0:nj], True, False, [T['A'], Ug], [p])
                    mm(o, T['Q'].t[:, g_, 0, :], Hbf.t[:, 0, g_, 0:nj], False, False, [T['Q'], Hbf], [p])
                    mm(o, T['Q'].t[:, g_, 1, :], Hbf.t[:, 1, g_, 0:nj], False, True, [T['Q'], Hbf], [p])
                copy(ew(), Yg.t[:, gq * gpb:(gq + 1) * gpb, 0:nj], p.t[:, 0:gpb * nj].rearrange("p (g j) -> p g j", j=nj), [p], [Yg])
            for kt in range(4):
                p = newps()
                for tau in range(8):
                    for g8 in range(8):
                        mm(p.t[:, tau * nj:(tau + 1) * nj], T['Sel'].t[:, tau * 8 + g8, :], Yg.t[:, kt * 8 + g8, 0:nj], g8 == 0, g8 == 7, [T['Sel'], Yg], [p])
                copy(ew(), ysblk.t[:, kt % 2, 0:ntok].rearrange("p (j t) -> p t j", t=8), p.t[:, 0:ntok].rearrange("p (t j) -> p t j", t=8), [p], [ysblk])
                if kt % 2 == 1:
                    k0 = gh * 4 + kt - 1
                    k.dma("pool", ys_s.t.ap()[b, d, k0:k0 + 2, :, tok0:tok0 + ntok].rearrange("kt p t -> p kt t"), ysblk.t[:, :, 0:ntok], reads=[ysblk], writes=[ys_s])

        front(0)
        for bi in range(len(blocks)):
            scan(bi)
            if bi + 1 < len(blocks):
                front(bi + 1)
            back(bi)

    e2 = [0]

    def eng2():
        e2[0] += 1
        return "dve" if e2[0] % 2 else "pool"

    def rwkv_phase(d, b_list):
        esr = ExitStack()
        f = lambda n, sh, dt=F32: sbt(esr, n, sh, dt)
        tri2 = f("tri2", [64, 128])
        m_su = f("m_su", [128, 4, 64]); m_ue = f("m_ue", [128, 4, 64]); m_lt = f("m_lt", [128, 4, 64]); I4 = f("I4", [128, 4, 64])
        bones = f("bones", [128, 128])
        for t_ in (tri2, m_su, m_ue, m_lt, I4):
            k.op("pool", lambda e, t_=t_: e.memset(t_.t[:], 1.0), writes=[t_])
        asel_b(tri2, [[1, 64]], -1, 0, ALU.is_ge, view=tri2.t[:, 0:64])
        asel_b(tri2, [[1, 64]], -1, 0, ALU.is_gt, view=tri2.t[:, 64:128])
        for (mb, pat, cm_, op_) in ((m_su, [[0, 4], [1, 64]], -1, ALU.is_gt), (m_ue, [[0, 4], [1, 64]], -1, ALU.is_ge),
                                    (m_lt, [[0, 4], [-1, 64]], 1, ALU.is_gt), (I4, [[0, 4], [-1, 64]], 1, ALU.is_equal)):
            asel_b(mb, pat, cm_, 0, op_, view=mb.t[0:64])
            k.dma("sp", mb.t[64:128], mb.t[0:64], reads=[mb], writes=[mb])
        k.op("pool", lambda e: e.memset(bones.t[:], 0.0), writes=[bones])
        k.op("pool", lambda e: e.memset(bones.t[0:64, 0:64], 1.0), reads=[bones], writes=[bones])
        k.op("pool", lambda e: e.memset(bones.t[64:128, 64:128], 1.0), reads=[bones], writes=[bones])
        def pair_param(name, src_ap2d, rows):
            t_ = f(name, [128, rows])
            load_cols(src_ap2d, rows, 128, lambda r0, nr: t_.t[:, r0:r0 + nr])
            k.state[(t_.id, None)] = [("e", "dve", k.cnt["dve"]), []]
            return t_
        w0P = pair_param("w0P", inp['rwkv_w0'].t.ap()[d].rearrange("(h l) -> h l", l=128), 8)
        a0P = pair_param("a0P", inp['rwkv_a0'].t.ap()[d].rearrange("(h l) -> h l", l=128), 8)
        rkP = pair_param("rkP", inp['rwkv_r_k'].t.ap()[d].rearrange("(h l) -> h l", l=128), 8)
        kkP = pair_param("kkP", inp['rwkv_k_k'].t.ap().rearrange("(h l) -> h l", l=128), 8)
        kaP = pair_param("kaP", inp['rwkv_k_a'].t.ap().rearrange("(h l) -> h l", l=128), 8)
        omkaP = f("omkaP", [128, 8])
        k.op("dve", lambda e: e.tensor_scalar(out=omkaP.t[:], in0=kaP.t[:], scalar1=-1.0, scalar2=1.0, op0=ALU.mult, op1=ALU.add), reads=[kaP], writes=[omkaP])
        wupb = f("wupb", [64, D], BF16); aupb = f("aupb", [64, D], BF16)
        lst = f("lst", [64, D])
        k.dma("sp", lst.t[:], inp['rwkv_w_up'].t.ap()[d], writes=[lst])
        copy("dve", wupb.t[:], lst.t[:], [lst], [wupb])
        k.dma("sp", lst.t[:], inp['rwkv_a_up'].t.ap()[d], reads=[lst], writes=[lst])
        copy("dve", aupb.t[:], lst.t[:], [lst], [aupb])
        wb4 = [f("wb4_%d" % i, [128, 8, 128], BF16) for i in range(4)]
        AT = f("AT", [128, 8, TB], BF16); RT = f("RT", [128, 8, TB], BF16); BT = f("BT", [128, 8, TB], BF16); KT = f("KT", [128, 8, TB], BF16)
        Vtok = f("Vtok", [128, 8, 4, 64], BF16); BHtok = f("BHtok", [128, 8, 4, 64], BF16); KHtok = f("KHtok", [128, 8, 4, 64], BF16)
        gT = f("gT", [128, 8, 4])
        twd = f("twd", [64, TB], BF16); xad = f("xad", [64, TB], BF16)
        t64 = [f("t64_%d" % i, [64, TB]) for i in range(2)]
        tl = {n: f("t_" + n, [128, TB]) for n in ("r", "k", "v", "tmp", "tmp2", "kk", "asig", "kd", "bb", "sgw", "kkn")}
        tb16 = {n: f("tb_" + n, [128, TB], BF16) for n in ("bh", "kh", "vb")}
        E = {n: f("E_" + n, [128, 4, 64]) for n in ("in", "ex", "inv", "rat", "ds")}
        cpad = [f("cpad%d" % i, [128, 4, 96]) for i in range(2)]; cT = f("cT", [128, 4])
        for t_ in cpad:
            k.op("pool", lambda e, t_=t_: e.memset(t_.t[:], 0.0), writes=[t_])
        S0T = f("S0T", [128, 8, 64]); S0Tb = f("S0Tb", [128, 8, 64], BF16)
        Pb = [[f("Pb%d%d" % (g_, i), [128, 4, 64], BF16) for i in range(2)] for g_ in range(2)]
        PTb = [[f("PTb%d%d" % (g_, i), [128, 4, 64], BF16) for i in range(2)] for g_ in range(2)]
        Nb = [[f("Nb%d%d" % (g_, i), [128, 4, 64], BF16) for i in range(2)] for g_ in range(2)]
        AakT = [f("AakT%d" % i, [128, 4, 64], BF16) for i in range(2)]; ArbT = [f("ArbT%d" % i, [128, 4, 64], BF16) for i in range(2)]; ArkT = [f("ArkT%d" % i, [128, 4, 64], BF16) for i in range(2)]
        Xf = [f("Xf%d" % i, [128, 4, 64], BF16) for i in range(2)]; Ub = f("Ub", [128, 8, 64], BF16); Ych = [f("Ych%d" % i, [128, 8, 64]) for i in range(2)]
        stmp = [f("stmp%d" % i, [128, 4, 64]) for i in range(2)]
        winv = wbf['w_in'].t.ap().rearrange("(kc p) n -> p kc n", p=128)
        wi = [0]
        ych_i = [0]
        HV = (slice(0, 64), slice(64, 128))

        for b in b_list:
            build_hT(b, d == 1)
            k.op("dve", lambda e: e.memset(S0T.t[:], 0.0), writes=[(S0T, 0), (S0T, 1)])
            k.op("dve", lambda e: e.memset(S0Tb.t[:], 0.0), writes=[S0Tb])
            for kb in range(9):
                seg_ctx = kb == 0
                col0 = CTX0 if seg_ctx else LAT0 + (kb - 1) * TB
                tok0 = (kb - 1) * TB

                def offs(c64):
                    if seg_ctx:
                        o = -1 if c64 < 26 else 1
                        return (-o if d == 1 else o), mask1
                    o = [-1, 1, -64, 64][c64 // 13]
                    if d == 1:
                        o = -o
                    return o, (mask1 if abs(o) == 64 else (maskL if o == -1 else maskR))

                def lerp(P, rows, c64, mu_t, omu_t, mcol, dst_ap, tmp_ap):
                    o, msk = offs(c64)
                    k.op("dve", lambda e: e.scalar_tensor_tensor(out=tmp_ap, in0=P.t[rows, 64 + o:64 + o + TB], scalar=mu_t.t[rows, mcol:mcol + 1], in1=msk.t[rows, :], op0=ALU.mult, op1=ALU.mult),
                         reads=[P, mu_t, msk], writes=[tl["tmp"]])
                    k.op("dve", lambda e: e.scalar_tensor_tensor(out=dst_ap, in0=P.t[rows, 64:64 + TB], scalar=omu_t.t[rows, mcol:mcol + 1], in1=tmp_ap, op0=ALU.mult, op1=ALU.add),
                         reads=[P, omu_t, tl["tmp"]], writes=[])

                def proj64(c64, dst):
                    wt = wb4[wi[0] % 4]
                    wi[0] += 1
                    cc = 1024 + c64 * 64
                    k.dma("sp", wt.t[:, :, 0:64], winv[:, :, cc:cc + 64], reads=[wbf['w_in']], writes=[wt])
                    P = newps()
                    for kc in range(8):
                        mm(P.t[0:64, 0:TB + 128], wt.t[:, kc, 0:64], hT.t[:, kc, col0 - 64:col0 + TB + 64], kc == 0, kc == 7, [wt, (hT, kc)], [P])
                    lerp(P, slice(0, 64), c64, mu64, omu64, c64, dst.t[:], tl["tmp"].t[0:64, :])
                    k.state[(dst.id, None)] = [("e", "dve", k.cnt["dve"]), []]

                def proj128(j, dst):
                    wt = wb4[wi[0] % 4]
                    wi[0] += 1
                    cc = 1024 + j * 128
                    k.dma("sp", wt.t[:], winv[:, :, cc:cc + 128], reads=[wbf['w_in']], writes=[wt])
                    P = newps()
                    for kc in range(8):
                        mm(P.t[:, 0:TB + 128], wt.t[:, kc, :], hT.t[:, kc, col0 - 64:col0 + TB + 64], kc == 0, kc == 7, [wt, (hT, kc)], [P])
                    if offs(2 * j) == offs(2 * j + 1):
                        lerp(P, slice(0, 128), 2 * j, mu128, omu128, j, dst.t[:], tl["tmp"].t[:])
                    else:
                        for hp in range(2):
                            lerp(P, HV[hp], 2 * j + hp, mu128, omu128, j, dst.t[HV[hp], :], tl["tmp"].t[HV[hp], :])
                    k.state[(dst.id, None)] = [("e", "dve", k.cnt["dve"]), []]

                proj64(48, t64[0])
                k.op("act", lambda e: e.activation(out=twd.t[:], in_=t64[0].t[:], func=AF.Tanh), reads=[t64[0]], writes=[twd])
                proj64(49, t64[1])
                copy("act", xad.t[:], t64[1].t[:], [t64[1]], [xad])
                for HP in range(8):
                    r_h, k_h, v_h, tmp, tmp2, kk, asig, kd, bb, sgw, kkn = [tl[n] for n in ("r", "k", "v", "tmp", "tmp2", "kk", "asig", "kd", "bb", "sgw", "kkn")]
                    proj128(HP, r_h); proj128(8 + HP, k_h); proj128(16 + HP, v_h)
                    psl = slice(HP * 128, (HP + 1) * 128)
                    p = newps()
                    mm(p.t[:, 0:TB], wupb.t[:, psl], twd.t[:], True, True, [wupb, twd], [p])
                    k.op("dve", lambda e, p=p: e.tensor_scalar(out=tmp2.t[:], in0=p.t[:, 0:TB], scalar1=w0P.t[:, HP:HP + 1], scalar2=None, op0=ALU.add), reads=[p, w0P], writes=[tmp2])
                    k.op("act", lambda e: e.activation(out=cpad[0].t[:, :, 32:96], in_=tmp2.t[:].rearrange("p (c t) -> p c t", t=64), func=AF.Sigmoid), reads=[tmp2], writes=[cpad[0]])
                    src_, dst_ = cpad[0], cpad[1]
                    for s_ in (1, 2, 4, 8, 16, 32):
                        tt("dve", dst_.t[:, :, 32:96], src_.t[:, :, 32:96], src_.t[:, :, 32 - s_:96 - s_], ALU.add, [src_], [dst_])
                        src_, dst_ = dst_, src_
                    incl = src_.t[:, :, 32:96]
                    k.op("act", lambda e, incl=incl: e.activation(out=E["in"].t[:], in_=incl, func=AF.Exp, scale=-KAPPA), reads=[src_], writes=[E["in"]])
                    k.op("act", lambda e, incl=incl: e.activation(out=E["inv"].t[:], in_=incl, func=AF.Exp, scale=KAPPA), reads=[src_], writes=[E["inv"]])
                    excl = src_.t[:, :, 31:95]
                    k.op("act", lambda e, excl=excl: e.activation(out=E["ex"].t[:], in_=excl, func=AF.Exp, scale=-KAPPA), reads=[src_], writes=[E["ex"]])
                    copy("dve", cT.t[:], src_.t[:, :, 95], [src_], [cT])
                    tt("dve", E["ds"].t[:], incl, bc_last(cT.t[:], 64), ALU.subtract, [src_, cT], [E["ds"]])
                    k.op("act", lambda e: e.activation(out=E["rat"].t[:], in_=E["ds"].t[:], func=AF.Exp, scale=KAPPA), reads=[E["ds"]], writes=[E["rat"]])
                    copy("pool", gT.t[:, HP, :], E["in"].t[:, :, 63], [E["in"]], [gT])
                    p = newps()
                    mm(p.t[:, 0:TB], aupb.t[:, psl], xad.t[:], True, True, [aupb, xad], [p])
                    k.op("dve", lambda e, p=p: e.tensor_scalar(out=tmp2.t[:], in0=p.t[:, 0:TB], scalar1=a0P.t[:, HP:HP + 1], scalar2=None, op0=ALU.add), reads=[p, a0P], writes=[tmp2])
                    k.op("act", lambda e: e.activation(out=asig.t[:], in_=tmp2.t[:], func=AF.Sigmoid), reads=[tmp2], writes=[asig])
                    k.op("pool", lambda e: e.tensor_scalar(out=kk.t[:], in0=k_h.t[:], scalar1=kkP.t[:, HP:HP + 1], scalar2=None, op0=ALU.mult), reads=[k_h, kkP], writes=[kk])
                    tt("pool", tmp.t[:], kk.t[:], kk.t[:], ALU.mult, [kk], [tmp])
                    p = newps()
                    mm(p.t[:, 0:TB], bones.t[:, :], tmp.t[:], True, True, [bones, tmp], [p])
                    k.op("dve", lambda e, p=p: e.tensor_scalar(out=tmp2.t[:], in0=p.t[:, 0:TB], scalar1=1e-24, scalar2=None, op0=ALU.max), reads=[p], writes=[tmp2])
                    k.op("act", lambda e: e.activation(out=tmp2.t[:], in_=tmp2.t[:], func=AF.Sqrt), reads=[tmp2], writes=[tmp2])
                    k.op("dve", lambda e: e.reciprocal(out=tmp2.t[:], in_=tmp2.t[:]), reads=[tmp2], writes=[tmp2])
                    tt("dve", kkn.t[:], kk.t[:], tmp2.t[:], ALU.mult, [kk, tmp2], [kkn])
                    k.op("pool", lambda e: e.tensor_scalar(out=tmp.t[:], in0=asig.t[:], scalar1=kaP.t[:, HP:HP + 1], scalar2=omkaP.t[:, HP:HP + 1], op0=ALU.mult, op1=ALU.add), reads=[asig, kaP, omkaP], writes=[tmp])
                    tt("pool", kd.t[:], k_h.t[:], tmp.t[:], ALU.mult, [k_h, tmp], [kd])
                    tt("pool", bb.t[:], kkn.t[:], asig.t[:], ALU.mult, [kkn, asig], [bb])
                    k.op("dve", lambda e: e.scalar_tensor_tensor(out=tmp.t[:], in0=r_h.t[:], scalar=rkP.t[:, HP:HP + 1], in1=kd.t[:], op0=ALU.mult, op1=ALU.mult), reads=[r_h, rkP, kd], writes=[tmp])
                    if not seg_ctx:
                        p = newps()
                        mm(p.t[:, 0:TB], bones.t[:, :], tmp.t[:], True, True, [bones, tmp], [p])
                        tt("dve", tmp2.t[:], p.t[:, 0:TB], v_h.t[:], ALU.mult, [p, v_h], [tmp2])
                        k.dma("pool", bon_s.t.ap()[b, d, 2 * HP:2 * HP + 2, :, tok0:tok0 + TB].rearrange("h l t -> (h l) t"), tmp2.t[:], reads=[tmp2], writes=[bon_s])
                    v3 = lambda t_: t_.t[:].rearrange("p (c t) -> p c t", t=64)
                    tt("dve", RT.t[:, HP, :].rearrange("p (c t) -> p c t", t=64), v3(r_h), E["in"].t[:], ALU.mult, [r_h, E["in"]], [RT])
                    k.op("dve", lambda e: e.scalar_tensor_tensor(out=AT.t[:, HP, :].rearrange("p (c t) -> p c t", t=64), in0=v3(kkn), scalar=-1.0, in1=E["ex"].t[:], op0=ALU.mult, op1=ALU.mult),
                         reads=[kkn, E["ex"]], writes=[AT])
                    tt("dve", BT.t[:, HP, :].rearrange("p (c t) -> p c t", t=64), v3(bb), E["inv"].t[:], ALU.mult, [bb, E["inv"]], [BT])
                    tt("pool", KT.t[:, HP, :].rearrange("p (c t) -> p c t", t=64), v3(kd), E["inv"].t[:], ALU.mult, [kd, E["inv"]], [KT])
                    tt("dve", v3(tb16["bh"]), v3(bb), E["rat"].t[:], ALU.mult, [bb, E["rat"]], [tb16["bh"]])
                    tt("pool", v3(tb16["kh"]), v3(kd), E["rat"].t[:], ALU.mult, [kd, E["rat"]], [tb16["kh"]])
                    copy("act", tb16["vb"].t[:], v_h.t[:], [v_h], [tb16["vb"]])
                    for (srcb, dstb) in ((tb16["vb"], Vtok), (tb16["bh"], BHtok), (tb16["kh"], KHtok)):
                        p = newps()
                        for c in range(4):
                            for hp in range(2):
                                mm(p.t[HV[hp], c * 64:(c + 1) * 64], srcb.t[HV[hp], c * 64:(c + 1) * 64], identb.t[HV[hp], HV[hp]], True, True, [srcb, identb], [p])
                        copy(ew(), dstb.t[:, HP, :, :].rearrange("p c t -> p (c t)"), p.t[:, 0:256], [p], [dstb])
                for c in range(4):
                    sl = slice(c * 64, (c + 1) * 64)
                    Yc = Ych[ych_i[0] % 2]
                    ych_i[0] += 1
                    v4 = lambda pb: pb.t[:, 0:256].rearrange("p (h t) -> p h t", t=64)

                    def heads(hg):
                        for hi in range(4):
                            for hp in range(2):
                                yield hi, 4 * hg + hi, HV[hp], slice(hi * 64, (hi + 1) * 64)
                    stt_ = [None, None]
                    for hg in range(2):
                        pAB, pRB, pAK, pRK, pA = newps(), newps(), newps(), newps(), newps()
                        for hi, HP, hv, o_ in heads(hg):
                            mm(pAB.t[hv, o_], BT.t[hv, HP, sl], AT.t[hv, HP, sl], True, True, [BT, AT], [pAB])
                            mm(pRB.t[hv, o_], BT.t[hv, HP, sl], RT.t[hv, HP, sl], True, True, [BT, RT], [pRB])
                            mm(pAK.t[hv, o_], KT.t[hv, HP, sl], AT.t[hv, HP, sl], True, True, [KT, AT], [pAK])
                            mm(pRK.t[hv, o_], KT.t[hv, HP, sl], RT.t[hv, HP, sl], True, True, [KT, RT], [pRK])
                            mm(pA.t[hv, o_], AT.t[hv, HP, sl], BT.t[hv, HP, sl], True, True, [AT, BT], [pA])
                        Pc, PTc, Nc = Pb[hg][0], PTb[hg][0], Nb[hg][0]
                        tt("dve", Pc.t[:], v4(pAB), m_su.t[:], ALU.mult, [pAB, m_su], [Pc])
                        tt("dve", PTc.t[:], v4(pA), m_lt.t[:], ALU.mult, [pA, m_lt], [PTc])
                        tt("dve", AakT[hg].t[:], v4(pAK), m_su.t[:], ALU.mult, [pAK, m_su], [AakT[hg]])
                        tt("dve", ArbT[hg].t[:], v4(pRB), m_ue.t[:], ALU.mult, [pRB, m_ue], [ArbT[hg]])
                        tt("dve", ArkT[hg].t[:], v4(pRK), m_ue.t[:], ALU.mult, [pRK, m_ue], [ArkT[hg]])
                        tt("pool", Nc.t[:], Pc.t[:], I4.t[:], ALU.add, [Pc, I4], [Nc])
                        stt_[hg] = [Pc, PTc, Nc, 0]
                    for j in range(1, 6):
                        nxt = [None, None]
                        for hg in range(2):
                            Pc, PTc, Nc, cur = stt_[hg]
                            Pn, PTn, Nn = Pb[hg][1 - cur], PTb[hg][1 - cur], Nb[hg][1 - cur]
                            pPT = newps()
                            for hi, HP, hv, o_ in heads(hg):
                                mm(pPT.t[hv, o_], Pc.t[hv, hi, :], PTc.t[hv, hi, :], True, True, [Pc, PTc], [pPT])
                            copy("act" if hg == 0 else "dve", PTn.t[:], v4(pPT), [pPT], [PTn])
                            if j < 5:
                                pP = newps()
                                for hi, HP, hv, o_ in heads(hg):
                                    mm(pP.t[hv, o_], PTc.t[hv, hi, :], Pc.t[hv, hi, :], True, True, [Pc, PTc], [pP])
                                copy("dve" if hg == 0 else "act", Pn.t[:], v4(pP), [pP], [Pn])
                            nxt[hg] = (Pn, PTn, Nn)
                        for hg in range(2):
                            Pc, PTc, Nc, cur = stt_[hg]
                            Pn, PTn, Nn = nxt[hg]
                            pN = newps()
                            for hi, HP, hv, o_ in heads(hg):
                                mm(pN.t[hv, o_], PTn.t[hv, hi, :], Nc.t[hv, hi, :], True, True, [PTn, Nc], [pN])
                            tt("dve", Nn.t[:], v4(pN), Nc.t[:], ALU.add, [pN, Nc], [Nn])
                            stt_[hg] = [Pn, PTn, Nn, 1 - cur]
                    for hg in range(2):
                        pX = newps()
                        for hi, HP, hv, o_ in heads(hg):
                            mm(pX.t[hv, o_], AT.t[hv, HP, sl], S0Tb.t[hv, HP, :], True, False, [AT, S0Tb], [pX])
                            mm(pX.t[hv, o_], AakT[hg].t[hv, hi, :], Vtok.t[hv, HP, c, :], False, True, [AakT[hg], Vtok], [pX])
                        copy("act", Xf[hg].t[:], v4(pX), [pX], [Xf[hg]])
                    for hg in range(2):
                        Nc = stt_[hg][2]
                        pU = newps()
                        for hi, HP, hv, o_ in heads(hg):
                            mm(pU.t[hv, o_], Nc.t[hv, hi, :], Xf[hg].t[hv, hi, :], True, True, [Nc, Xf[hg]], [pU])
                        copy("dve", Ub.t[:, 4 * hg:4 * hg + 4, :], v4(pU), [pU], [(Ub, hg)])
                    for hg in range(2):
                        pY = newps()
                        for hi, HP, hv, o_ in heads(hg):
                            mm(pY.t[hv, o_], RT.t[hv, HP, sl], S0Tb.t[hv, HP, :], True, False, [RT, S0Tb], [pY])
                            mm(pY.t[hv, o_], ArbT[hg].t[hv, hi, :], Ub.t[hv, HP, :], False, False, [ArbT[hg], (Ub, hg)], [pY])
                            mm(pY.t[hv, o_], ArkT[hg].t[hv, hi, :], Vtok.t[hv, HP, c, :], False, True, [ArkT[hg], Vtok], [pY])
                        copy("act", Yc.t[:, 4 * hg:4 * hg + 4, :], v4(pY), [pY], [Yc])
                    for hg in range(2):
                        gsl = slice(4 * hg, 4 * hg + 4)
                        pS = newps()
                        for hi, HP, hv, o_ in heads(hg):
                            mm(pS.t[hv, o_], BHtok.t[hv, HP, c, :], Ub.t[hv, HP, :], True, False, [BHtok, (Ub, hg)], [pS])
                            mm(pS.t[hv, o_], KHtok.t[hv, HP, c, :], Vtok.t[hv, HP, c, :], False, True, [KHtok, Vtok], [pS])
                        tt("pool", stmp[hg].t[:], S0T.t[:, gsl, :], bc_last(gT.t[:, gsl, c], 64), ALU.mult, [(S0T, hg), gT], [stmp[hg]])
                        tt("dve", S0T.t[:, gsl, :], v4(pS), stmp[hg].t[:], ALU.add, [pS, stmp[hg]], [(S0T, hg)])
                        copy("act", S0Tb.t[:, gsl, :], S0T.t[:, gsl, :], [(S0T, hg)], [S0Tb])
                    if not seg_ctx:
                        wv_ = wkv_s.t.ap()[b, d, tok0 + c * 64:tok0 + (c + 1) * 64, :].rearrange("t (g hp v) -> t g hp v", hp=2, v=64)
                        for hp in range(2):
                            k.dma("act", wv_[:, :, hp, :], Yc.t[HV[hp], :, :], reads=[Yc], writes=[wkv_s])
        barrier()
        esr.close()

    def rev_last(ap, n):
        dims = [list(x) for x in ap.ap]
        dims[-1] = [-1, n]
        return AP(ap.tensor, ap.offset + n - 1, dims)

    def final_phase(b):
        esf = ExitStack()
        f = lambda n, sh, dt=F32: sbt(esf, n, sh, dt)
        h2T = f("h2T", [128, 8, SEQ + 2], BF16)
        gtBb = f("gtBb", [128, 2, D])
        sgt = f("sgt", [128, TB]); sgt2 = f("sgt2", [128, TB])
        wq = [f("wq%d" % i, [128, 8, 128], BF16) for i in range(4)]
        esA = ExitStack()
        fa = lambda n, sh, dt=F32: sbt(esA, n, sh, dt)
        G = [fa("G%d" % i, [128, 8, TB]) for i in range(5)]
        gl = fa("gl", [128, 8, TB], BF16); rw = fa("rw", [128, 8, TB], BF16); merged = fa("merged", [128, 8, TB], BF16)
        sgb = fa("sgb", [128, TB], BF16)
        wo = fa("wo", [128, 8, 512], BF16)
        condBb = fa("condBb", [128, 8, 128])
        wmf = [fa("wmf%d" % i, [128, 8, 128]) for i in range(2)]
        bmr = fa("bmr", [1, 128])
        st8 = fa("st8", [128, 32])
        gupb = fa("gupb", [128, D], BF16)
        wqi = [0]

        def loadw(name, c0, ncols=128):
            wt = wq[wqi[0] % 4]
            wqi[0] += 1
            k.dma("sp", wt.t[:, :, 0:ncols], wbf[name].t.ap().rearrange("(kc p) n -> p kc n", p=128)[:, :, c0:c0 + ncols], reads=[wbf[name]], writes=[wt])
            return wt

        for kc in range(8):
            k.op("dve", lambda e, kc=kc: e.tensor_copy(out=condBb.t[:, kc, :], in_=condT.t[:, kc, b:b + 1].to_broadcast([128, 128])), reads=[condT], writes=[condBb])
        wmod_v = inp['w_mod'].t.ap().rearrange("(kc p) n -> p kc n", p=128)
        for which, c00 in ((0, 2048), (1, 5120)):
            for q in range(8):
                cc = c00 + q * 128
                w_ = wmf[q % 2]
                k.dma("sp", w_.t[:], wmod_v[:, :, cc:cc + 128], reads=[w_], writes=[w_])
                k.dma("sp", bmr.t[:], inp['b_mod'].t.ap().rearrange("(o n) -> o n", o=1)[:, cc:cc + 128], reads=[bmr], writes=[bmr])
                p = newps()
                for kc in range(8):
                    mm(p.t[:, 0:128], condBb.t[:, kc, :], w_.t[:, kc, :], kc == 0, False, [w_, condBb], [p])
                mm(p.t[:, 0:128], ones.t[0:1, 0:128], bmr.t[0:1, :], False, True, [ones, bmr], [p])
                copy("act", gtBb.t[:, which, q * 128:(q + 1) * 128], p.t[:, 0:128], [p], [gtBb])
        k.dma("sp", xin[0].t[:], inp['rwkv_g_up'].t.ap(), reads=[xin[0]], writes=[xin[0]])
        copy("dve", gupb.t[:], xin[0].t[:], [xin[0]], [gupb])
        for kc_ in range(8):
            k.op("dve", lambda e, kc_=kc_: e.memset(h2T.t[:, kc_, :], 0.0), writes=[h2T])
        build_hT(b, False)

        def mirror(t0, n):
            return SEQ - t0 - n

        for i in range(8):
            t0 = i * TB
            col0 = LAT0 + t0
            m0 = mirror(t0, TB)
            k.dma("sp", G[0].t[:], ys_s.t.ap()[b, 0, :, :, t0:t0 + TB].rearrange("kt p t -> p kt t"), reads=[ys_s, G[0]], writes=[G[0]])
            k.dma("sp", G[1].t[:], ys_s.t.ap()[b, 1, :, :, m0:m0 + TB].rearrange("kt p t -> p kt t"), reads=[ys_s, G[1]], writes=[G[1]])
            for oc in range(8):
                wt = loadw('w_in', oc * 128)
                p = newps()
                for kc in range(8):
                    mm(p.t[:, 0:TB], wt.t[:, kc, :], hT.t[:, kc, col0:col0 + TB], kc == 0, kc == 7, [wt, (hT, kc)], [p])
                k.op("dve", lambda e, p=p, oc=oc: e.scalar_tensor_tensor(out=G[2].t[:, oc, :], in0=p.t[:, 0:TB], scalar=s5d.t[:, oc:oc + 1], in1=G[0].t[:, oc, :], op0=ALU.mult, op1=ALU.add),
                     reads=[p, s5d, G[0]], writes=[G[2]])
                tt("dve", G[2].t[:, oc, :], G[2].t[:, oc, :], rev_last(G[1].t[:, oc, :], TB), ALU.add, [G[2], G[1]], [G[2]])
            for oc in range(8):
                x_ = G[2].t[:, oc, :]
                tt("pool", sgt.t[:], x_, x_, ALU.mult, [G[2]], [sgt])
                k.op("pool", lambda e: e.tensor_scalar(out=sgt.t[:], in0=sgt.t[:], scalar1=0.044715, scalar2=1.0, op0=ALU.mult, op1=ALU.add), reads=[sgt], writes=[sgt])
                tt("pool", sgt.t[:], sgt.t[:], x_, ALU.mult, [sgt, G[2]], [sgt])
                k.op("act", lambda e: e.activation(out=sgt2.t[:], in_=sgt.t[:], func=AF.Sigmoid, scale=1.5957691216), reads=[sgt], writes=[sgt2])
                tt("dve", gl.t[:, oc, :], x_, sgt2.t[:], ALU.mult, [G[2], sgt2], [gl])
            for oc in range(8):
                wa = loadw('s5_glu_w', oc * 128)
                wb_ = loadw('s5_glu_w', D + oc * 128)
                pa = newps(); pb = newps()
                for kc in range(8):
                    mm(pa.t[:, 0:TB], wa.t[:, kc, :], gl.t[:, kc, :], kc == 0, kc == 7, [wa, gl], [pa])
                for kc in range(8):
                    mm(pb.t[:, 0:TB], wb_.t[:, kc, :], gl.t[:, kc, :], kc == 0, kc == 7, [wb_, gl], [pb])
                k.op("act", lambda e, pb=pb: e.activation(out=sgt.t[:], in_=pb.t[:, 0:TB], func=AF.Sigmoid), reads=[pb], writes=[sgt])
                tt("dve", G[3].t[:, oc, :], pa.t[:, 0:TB], sgt.t[:], ALU.mult, [pa, sgt], [G[3]])
            for j in range(2):
                ts = t0 + j * 128
                ms = mirror(ts, 128)
                k.dma("sp", xin[0].t[:], wkv_s.t.ap()[b, 0, ts:ts + 128, :], reads=[wkv_s, xin[0]], writes=[xin[0]])
                k.dma("sp", xin[1].t[:], wkv_s.t.ap()[b, 1, ms:ms + 128, :], reads=[wkv_s, xin[1]], writes=[xin[1]])
                for hf in range(2):
                    p = newps()
                    mm(p.t[:, :], ident.t[:, :], xin[0].t[:, hf * 512:(hf + 1) * 512], True, False, [ident, xin[0]], [p])
                    mm(p.t[:, :], J128.t[:, :], xin[1].t[:, hf * 512:(hf + 1) * 512], False, True, [J128, xin[1]], [p])
                    pv = p.t[:, :].rearrange("p (h v) -> p h v", v=64)
                    xc = xn[0].t[:, hf * 512:(hf + 1) * 512].rearrange("p (h v) -> p h v", v=64)
                    sq = junk.t[:, hf * 512:(hf + 1) * 512].rearrange("p (h v) -> p h v", v=64)
                    yn = xn[1].t[:, hf * 512:(hf + 1) * 512].rearrange("p (h v) -> p h v", v=64)
                    k.op("dve", lambda e, pv=pv: e.tensor_reduce(out=st8.t[:, 0:8], in_=pv, axis=AX.X, op=ALU.add), reads=[p], writes=[st8])
                    k.op("dve", lambda e: e.tensor_scalar(out=st8.t[:, 8:16], in0=st8.t[:, 0:8], scalar1=1.0 / 64, scalar2=None, op0=ALU.mult), reads=[st8], writes=[st8])
                    tt("dve", xc, pv, bc_last(st8.t[:, 8:16], 64), ALU.subtract, [p, st8], [xn[0]])
                    tt("pool", sq, xc, xc, ALU.mult, [xn[0]], [junk])
                    k.op("dve", lambda e, sq=sq: e.tensor_reduce(out=st8.t[:, 16:24], in_=sq, axis=AX.X, op=ALU.add), reads=[junk], writes=[st8])
                    k.op("dve", lambda e: e.tensor_scalar(out=st8.t[:, 16:24], in0=st8.t[:, 16:24], scalar1=1.0 / 64, scalar2=64e-5, op0=ALU.mult, op1=ALU.add), reads=[st8], writes=[st8])
                    k.op("act", lambda e: e.activation(out=st8.t[:, 16:24], in_=st8.t[:, 16:24], func=AF.Sqrt), reads=[st8], writes=[st8])
                    k.op("dve", lambda e: e.reciprocal(out=st8.t[:, 24:32], in_=st8.t[:, 16:24]), reads=[st8], writes=[st8])
                    tt("dve", yn, xc, bc_last(st8.t[:, 24:32], 64), ALU.mult, [xn[0], st8], [xn[1]])
                for half in range(2):
                    p = newps()
                    for q in range(4):
                        kc = half * 4 + q
                        mm(p.t[:, q * 128:(q + 1) * 128], xn[1].t[:, kc * 128:(kc + 1) * 128], ident.t[:, :], True, True, [xn[1], ident], [p])
                    for q in range(4):
                        kc = half * 4 + q
                        k.op("dve", lambda e, p=p, q=q, kc=kc: e.tensor_scalar(out=G[2].t[:, kc, j * 128:(j + 1) * 128], in0=p.t[:, q * 128:(q + 1) * 128],
                                                                              scalar1=lnxg.t[:, kc:kc + 1], scalar2=lnxb.t[:, kc:kc + 1], op0=ALU.mult, op1=ALU.add),
                             reads=[p, lnxg, lnxb], writes=[G[2]])
            bon_v = bon_s.t.ap().rearrange("b d (kt hp) l t -> b d kt (hp l) t", hp=2)
            k.dma("sp", G[0].t[:], bon_v[b, 0, :, :, t0:t0 + TB].rearrange("kt p t -> p kt t"), reads=[bon_s, G[0]], writes=[G[0]])
            k.dma("sp", G[1].t[:], bon_v[b, 1, :, :, m0:m0 + TB].rearrange("kt p t -> p kt t"), reads=[bon_s, G[1]], writes=[G[1]])
            for oc in range(8):
                tt("pool", G[2].t[:, oc, :], G[2].t[:, oc, :], G[0].t[:, oc, :], ALU.add, [G[2], G[0]], [G[2]])
                tt("dve", G[2].t[:, oc, :], G[2].t[:, oc, :], rev_last(G[1].t[:, oc, :], TB), ALU.add, [G[2], G[1]], [G[2]])
            wt = loadw('w_in', 4224)
            P = newps(); Ps = newps()
            for kc in range(8):
                mm(P.t[:, 0:TB], wt.t[:, kc, :], hT.t[:, kc, col0:col0 + TB], kc == 0, kc == 7, [wt, (hT, kc)], [P])
            for kc in range(8):
                mm(Ps.t[:, 0:TB], wt.t[:, kc, :], hT.t[:, kc, col0 + 64:col0 + 64 + TB], kc == 0, kc == 7, [wt, (hT, kc)], [Ps])
            k.op("dve", lambda e: e.tensor_scalar(out=sgt.t[:], in0=Ps.t[:, 0:TB], scalar1=mu128.t[:, 25:26], scalar2=None, op0=ALU.mult), reads=[Ps, mu128], writes=[sgt])
            k.op("dve", lambda e: e.scalar_tensor_tensor(out=sgt2.t[:], in0=P.t[:, 0:TB], scalar=omu128.t[:, 25:26], in1=sgt.t[:], op0=ALU.mult, op1=ALU.add), reads=[P, omu128, sgt], writes=[sgt2])
            k.op("act", lambda e: e.activation(out=sgb.t[:], in_=sgt2.t[:], func=AF.Sigmoid), reads=[sgt2], writes=[sgb])
            for oc in range(8):
                p = newps()
                mm(p.t[:, 0:TB], gupb.t[:, oc * 128:(oc + 1) * 128], sgb.t[:], True, True, [gupb, sgb], [p])
                tt("dve", rw.t[:, oc, :], p.t[:, 0:TB], G[2].t[:, oc, :], ALU.mult, [p, G[2]], [rw])
            for oc in range(8):
                wt = loadw('rwkv_w_out', oc * 128)
                p = newps()
                for kc in range(8):
                    mm(p.t[:, 0:TB], wt.t[:, kc, :], rw.t[:, kc, :], kc == 0, kc == 7, [wt, rw], [p])
                copy("act", G[4].t[:, oc, :], p.t[:, 0:TB], [p], [G[4]])
            for oc in range(8):
                wa = loadw('w_in', 4352 + oc * 128)
                wb_ = loadw('w_in', 5376 + oc * 128)
                pa = newps(); pb = newps()
                for kc in range(8):
                    mm(pa.t[:, 0:TB], wa.t[:, kc, :], hT.t[:, kc, col0:col0 + TB], kc == 0, kc == 7, [wa, (hT, kc)], [pa])
                for kc in range(8):
                    mm(pb.t[:, 0:TB], wb_.t[:, kc, :], hT.t[:, kc, col0:col0 + TB], kc == 0, kc == 7, [wb_, (hT, kc)], [pb])
                k.op("act", lambda e, pa=pa: e.activation(out=sgt.t[:], in_=pa.t[:, 0:TB], func=AF.Sigmoid), reads=[pa], writes=[sgt])
                k.op("act", lambda e, pb=pb: e.activation(out=sgt2.t[:], in_=pb.t[:, 0:TB], func=AF.Sigmoid), reads=[pb], writes=[sgt2])
                tt("dve", sgt.t[:], sgt.t[:], G[3].t[:, oc, :], ALU.mult, [sgt, G[3]], [sgt])
                tt("pool", sgt2.t[:], sgt2.t[:], G[4].t[:, oc, :], ALU.mult, [sgt2, G[4]], [sgt2])
                tt("dve", merged.t[:, oc, :], sgt.t[:], sgt2.t[:], ALU.add, [sgt, sgt2], [merged])
            for j in range(2):
                ts = t0 + j * 128
                k.dma("sp", xin[0].t[:], inp['x'].t.ap()[b, ts:ts + 128, :], reads=[xin[0]], writes=[xin[0]])
                for hf in range(2):
                    k.dma("sp", wo.t[:], wbf['w_out'].t.ap().rearrange("(kc p) n -> p kc n", p=128)[:, :, hf * 512:(hf + 1) * 512], reads=[wbf['w_out'], wo], writes=[wo])
                    p = newps()
                    for kc in range(8):
                        mm(p.t[:, :], merged.t[:, kc, j * 128:(j + 1) * 128], wo.t[:, kc, :], kc == 0, kc == 7, [merged, wo], [p])
                    tt("dve", xn[0].t[:, hf * 512:(hf + 1) * 512], p.t[:, :], gtBb.t[:, 0, hf * 512:(hf + 1) * 512], ALU.mult, [p, gtBb], [xn[0]])
                tt("pool", xn[0].t[:], xn[0].t[:], xin[0].t[:], ALU.add, [xn[0], xin[0]], [xn[0]])
                k.dma("pool", hx1_s.t.ap()[b, ts:ts + 128, :], xn[0].t[:], reads=[xn[0]], writes=[hx1_s])
                rms_rstd(xn[0], xn[0].t[:], ssq.t[:, 2:3])
                k.op("dve", lambda e: e.tensor_scalar(out=xn[1].t[:], in0=xn[0].t[:], scalar1=ssq.t[:, 2:3], scalar2=None, op0=ALU.mult), reads=[xn[0], ssq], writes=[xn[1]])
                for half in range(2):
                    p = newps()
                    for q in range(4):
                        kc = half * 4 + q
                        mm(p.t[:, q * 128:(q + 1) * 128], xn[1].t[:, kc * 128:(kc + 1) * 128], ident.t[:, :], True, True, [xn[1], ident], [p])
                    for q in range(4):
                        kc = half * 4 + q
                        k.op("dve", lambda e, p=p, q=q, kc=kc: e.tensor_scalar(out=h2T.t[:, kc, 1 + ts:1 + ts + 128], in0=p.t[:, q * 128:(q + 1) * 128],
                                                                              scalar1=s2T.t[:, kc, b:b + 1], scalar2=modT.t[:, 24 + kc, b:b + 1], op0=ALU.mult, op1=ALU.add),
                             reads=[p, s2T, modT], writes=[h2T])
        barrier()
        esA.close()
        esB = ExitStack()
        ff = sbt(esB, "ff", [128, 22, TB], BF16)
        wd_ = [sbt(esB, "wdn%d" % i, [128, D], BF16) for i in range(4)]
        wdv = wbf['ffn_w_down'].t.ap()
        wdi = [0]
        for i in range(8):
            t0 = i * TB
            for jc in range(22):
                wg = loadw('ffn_w_up', jc * 128)
                wv = loadw('ffn_w_up', D_FF + jc * 128)
                pg = newps(); pv_ = newps()
                for kc in range(8):
                    mm(pg.t[:, 0:TB + 2], wg.t[:, kc, :], h2T.t[:, kc, t0:t0 + TB + 2], kc == 0, kc == 7, [wg, h2T], [pg])
                for kc in range(8):
                    mm(pv_.t[:, 0:TB + 2], wv.t[:, kc, :], h2T.t[:, kc, t0:t0 + TB + 2], kc == 0, kc == 7, [wv, h2T], [pv_])
                for (pp, ch, dstt) in ((pg, jc, sgt), (pv_, 22 + jc, sgt2)):
                    k.op("dve", lambda e, pp=pp, ch=ch, dstt=dstt: e.tensor_scalar(out=dstt.t[:], in0=pp.t[:, 0:TB], scalar1=cvw.t[:, ch:ch + 1], scalar2=cvb.t[:, ch:ch + 1], op0=ALU.mult, op1=ALU.add),
                         reads=[pp, cvw, cvb], writes=[dstt])
                    k.op("dve", lambda e, pp=pp, ch=ch, dstt=dstt: e.scalar_tensor_tensor(out=dstt.t[:], in0=pp.t[:, 1:TB + 1], scalar=cvw.t[:, 44 + ch:45 + ch], in1=dstt.t[:], op0=ALU.mult, op1=ALU.add),
                         reads=[pp, cvw, dstt], writes=[dstt])
                    k.op("dve", lambda e, pp=pp, ch=ch, dstt=dstt: e.scalar_tensor_tensor(out=dstt.t[:], in0=pp.t[:, 2:TB + 2], scalar=cvw.t[:, 88 + ch:89 + ch], in1=dstt.t[:], op0=ALU.mult, op1=ALU.add),
                         reads=[pp, cvw, dstt], writes=[dstt])
                k.op("act", lambda e: e.activation(out=sgt.t[:], in_=sgt.t[:], func=AF.Silu), reads=[sgt], writes=[sgt])
                tt("pool", ff.t[:, jc, :], sgt.t[:], sgt2.t[:], ALU.mult, [sgt, sgt2], [ff])
            for j in range(2):
                ts = t0 + j * 128
                pa = newps(); pb = newps()
                for kc in range(22):
                    wt = wd_[wdi[0] % 4]
                    wdi[0] += 1
                    k.dma("sp", wt.t[:], wdv[kc * 128:(kc + 1) * 128, :], reads=[wbf['ffn_w_down']], writes=[wt])
                    mm(pa.t[:, :], ff.t[:, kc, j * 128:(j + 1) * 128], wt.t[:, 0:512], kc == 0, kc == 21, [ff, wt], [pa])
                    mm(pb.t[:, :], ff.t[:, kc, j * 128:(j + 1) * 128], wt.t[:, 512:1024], kc == 0, kc == 21, [ff, wt], [pb])
                k.dma("sp", xin[0].t[:], hx1_s.t.ap()[b, ts:ts + 128, :], reads=[hx1_s, xin[0]], writes=[xin[0]])
                tt("dve", xn[0].t[:, 0:512], pa.t[:, :], gtBb.t[:, 1, 0:512], ALU.mult, [pa, gtBb], [xn[0]])
                tt("dve", xn[0].t[:, 512:1024], pb.t[:, :], gtBb.t[:, 1, 512:1024], ALU.mult, [pb, gtBb], [xn[0]])
                tt("pool", xn[0].t[:], xn[0].t[:], xin[0].t[:], ALU.add, [xn[0], xin[0]], [xn[0]])
                rms_rstd(xn[0], xn[0].t[:], ssq.t[:, 2:3])
                k.op("dve", lambda e: e.scalar_tensor_tensor(out=xn[1].t[:], in0=xn[0].t[:], scalar=ssq.t[:, 2:3], in1=fgB.t[:], op0=ALU.mult, op1=ALU.mult), reads=[xn[0], ssq, fgB], writes=[xn[1]])
                k.dma("pool", outb.t.ap()[b, ts:ts + 128, :], xn[1].t[:], reads=[xn[1]], writes=[outb])
        barrier()
        esB.close()
        esf.close()

    def finish():
        k.finish("sp")
        for e in ("act", "dve", "pool", "pe"):
            k.finish(e)
        k.close()
        es.close()

    if stage in ("hT0", "hT1"):
        rev = stage == "hT1"
        build_hT(0, rev)
        hf = sb("hf", [128, NCOL])
        for kc in range(8 if os.environ.get('HT_SKIP_DUMP') is None else 0):
            copy(os.environ.get("DUMPENG", "dve"), hf.t[:], hT.t[:, kc, :], hT_all, [hf])
            k.dma("sp", dbg["dbg_hT"].t.ap()[kc], hf.t[:], reads=[hf], writes=[dbg["dbg_hT"]])
        finish()
        return nc
    def s5_phase(d, b_list):
        es5 = ExitStack()
        Sel_ = sbt(es5, "Sel", [128, 64, 128], BF16)
        k.op("pool", lambda e: e.memset(Sel_.t[:], 1.0), writes=[Sel_])
        sv = Sel_.t[:].rearrange("p (a b) m -> p a b m", a=8)
        for (pat, cm_, base_, op_) in (([[-16, 8], [16, 8], [-1, 128]], 1, 0, ALU.is_equal),
                                       ([[-16, 8], [0, 8], [0, 128]], 1, 0, ALU.is_ge),
                                       ([[16, 8], [0, 8], [0, 128]], -1, 15, ALU.is_ge)):
            asel_b(Sel_, pat, cm_, base_, op_, view=sv)
        T = {'A': sbt(es5, "tA", [128, 32, 128], BF16), 'Wr': sbt(es5, "tWr", [128, 32, 64], BF16), 'Wi': sbt(es5, "tWi", [128, 32, 64], BF16),
             'Q': sbt(es5, "tQ", [64, 32, 2, 128], BF16), 'LPr': sbt(es5, "tLPr", [64, 7, 2, 32]), 'LPis': sbt(es5, "tLPi", [64, 7, 2, 32]), 'Sel': Sel_}
        for gh in range(2):
            s5_build_tables(d, gh, T)
            esw = ExitStack()
            W_ = ([sbt(esw, "up%d" % i, [128, 4, 8, 64], BF16) for i in range(2)], [sbt(esw, "Ug%d" % i, [128, 32, 64], BF16) for i in range(2)],
                  [sbt(esw, "Hb%d" % i, [64, 2, 32, 65]) for i in range(2)],
                  sbt(esw, "Hbf", [64, 2, 32, 64], BF16), sbt(esw, "Yg", [128, 32, 64], BF16), sbt(esw, "ysblk", [128, 2, 512]),
                  [sbt(esw, "wu%d" % i, [128, 8, 256], BF16) for i in range(1)], sbt(esw, "T13", [64, 2, 32, 32]), sbt(esw, "T24", [64, 2, 32, 32]))
            for b in b_list:
                build_hT(b, d == 1)
                s5_run(b, d, gh, T, W_)
            barrier()
            esw.close()
        es5.close()
        return None

    if stage in ("s5_0", "s5_1"):
        d = int(stage[-1])
        globals_sel = {}
        s5_phase(d, [0])
        esd = ExitStack()
        yb = sbt(esd, "ybd", [128, 8, TB])
        for blk in range(8):
            k.dma("sp", yb.t[:], ys_s.t.ap()[0, d, :, :, blk * TB:(blk + 1) * TB].rearrange("kt p t -> p kt t"), reads=[ys_s, yb], writes=[yb])
            k.dma("sp", dbg["dbg_ys"].t.ap()[:, :, blk * TB:(blk + 1) * TB].rearrange("kt p t -> p kt t"), yb.t[:], reads=[yb], writes=[dbg["dbg_ys"]])
        barrier()
        esd.close()
        finish()
        return nc
    if stage in ("rw_0", "rw_1"):
        d = int(stage[-1])
        rwkv_phase(d, [0])
        esd = ExitStack()
        yb = sbt(esd, "ybd", [128, D])
        for blk in range(16):
            k.dma("sp", yb.t[:], wkv_s.t.ap()[0, d, blk * 128:(blk + 1) * 128, :], reads=[wkv_s, yb], writes=[yb])
            k.dma("sp", dbg["dbg_wkv"].t.ap()[blk * 128:(blk + 1) * 128, :], yb.t[:], reads=[yb], writes=[dbg["dbg_wkv"]])
        bb_ = sbt(esd, "bbd", [64, 16, 256])
        for blk in range(8):
            k.dma("sp", bb_.t[:], bon_s.t.ap()[0, d, :, :, blk * 256:(blk + 1) * 256].rearrange("h p t -> p h t"), reads=[bon_s, bb_], writes=[bb_])
            k.dma("sp", dbg["dbg_bon"].t.ap()[:, :, blk * 256:(blk + 1) * 256].rearrange("h p t -> p h t"), bb_.t[:], reads=[bb_], writes=[dbg["dbg_bon"]])
        barrier()
        esd.close()
        finish()
        return nc
    if stage == "all":
        for d in range(2):
            s5_phase(d, list(range(NB)))
            rwkv_phase(d, list(range(NB)))
        for b in range(NB):
            final_phase(b)
        finish()
        return nc
    raise NotImplementedError(stage)


_CACHE = {}


def kernel(**inputs):
    n_cores = 8
    if "nc" not in _CACHE:
        _CACHE["nc"] = build_program("all")
    nc = _CACHE["nc"]
    in_maps = []
    for ci in range(n_cores):
        m = {}
        for n, shp in INPUT_SHAPES.items():
            a = np.asarray(inputs[n])
            if n in ('x', 'c', 'ctx'):
                a = a[NB * ci:NB * (ci + 1)]
            m[n] = np.ascontiguousarray(a, dtype=np.float32).reshape(shp)
        in_maps.append(m)
    res = run_bass_kernel_spmd(nc, in_maps, core_ids=list(range(n_cores)))
    out = np.concatenate([r["y"] for r in res.results], axis=0)
    return out.astype(np.float32)
```

```python
import os
import numpy as np
from contextlib import ExitStack
import concourse.bass as bass
import concourse.mybir as mybir
from concourse.ap import AP
from concourse.bass_utils import run_bass_kernel_spmd

F32 = mybir.dt.float32
BF16 = mybir.dt.bfloat16
AF = mybir.ActivationFunctionType
ALU = mybir.AluOpType
AX = mybir.AxisListType

D = 1024
SEQ = 2048
CTX = 256
NB = 2
N_IN = 6400
D_FF = 2816
TB = 256
PADC = 64
CTX0 = PADC
LAT0 = PADC + CTX + PADC
NCOL = LAT0 + SEQ + PADC
KAPPA = 0.6065306597126334


class Buf:
    _n = 0

    def __init__(self, t, name):
        self.t = t
        self.name = name
        Buf._n += 1
        self.id = Buf._n

    def __getitem__(self, k):
        return self.t[k]


class K:
    def __init__(self, nc, n_dma_sems=32):
        self.nc = nc
        self.eng = {"pe": nc.tensor, "act": nc.scalar, "dve": nc.vector, "pool": nc.gpsimd, "sp": nc.sync}
        self.sem = {}
        self.cnt = {}
        self._ctx = []
        for e in self.eng:
            cm = nc.semaphore("sem_" + e)
            self.sem[e] = cm.__enter__()
            self._ctx.append(cm)
            self.cnt[e] = 0
        self.dma_sems = []
        for i in range(n_dma_sems):
            cm = nc.semaphore("dsem%d" % i)
            self.dma_sems.append([cm.__enter__(), 0])
            self._ctx.append(cm)
        self.dma_rr = 0
        self.seen = {e: {} for e in self.eng}
        self.state = {}
        self.ninst = 0

    def _wait(self, e, tok):
        kind, a, v = tok
        key = (kind, a)
        if self.seen[e].get(key, 0) >= v:
            return
        self.seen[e][key] = v
        if kind == "e":
            if a == e and e == "pe":
                return
            self.eng[e].wait_ge(self.sem[a], v)
        else:
            self.eng[e].wait_ge(self.dma_sems[a][0], v)

    def _deps(self, e, reads, writes):
        toks = []
        for (b, k) in reads:
            st = self.state.get((b.id, k))
            if st and st[0] is not None:
                toks.append(st[0])
        for (b, k) in writes:
            st = self.state.get((b.id, k))
            if st:
                if st[0] is not None:
                    toks.append(st[0])
                toks.extend(st[1])
        for t in toks:
            self._wait(e, t)

    def _record(self, tok, reads, writes):
        for (b, k) in reads:
            st = self.state.setdefault((b.id, k), [None, []])
            if tok[0] == "e":
                st[1] = [t for t in st[1] if not (t[0] == "e" and t[1] == tok[1])]
            st[1].append(tok)
        for (b, k) in writes:
            self.state[(b.id, k)] = [tok, []]

    @staticmethod
    def _norm(lst):
        out = []
        for x in lst:
            if isinstance(x, Buf):
                out.append((x, None))
            else:
                out.append(x)
        return out

    def op(self, e, fn, reads=(), writes=()):
        reads = self._norm(reads)
        writes = self._norm(writes)
        self._deps(e, reads, writes)
        ins = fn(self.eng[e])
        self.cnt[e] += 1
        ins.then_inc(self.sem[e], 1)
        tok = ("e", e, self.cnt[e])
        self._record(tok, reads, writes)
        self.ninst += 1
        return ins

    def dma(self, e, out, in_, reads=(), writes=(), **kw):
        reads = self._norm(reads)
        writes = self._norm(writes)
        self._deps(e, reads, writes)
        i = self.dma_rr
        self.dma_rr = (self.dma_rr + 1) % len(self.dma_sems)
        s = self.dma_sems[i]
        if s[1] > 0:
            self._wait(e, ("d", i, s[1]))
        s[1] += 16
        ins = self.eng[e].dma_start(out=out, in_=in_, **kw)
        ins.then_inc(s[0], 16)
        tok = ("d", i, s[1])
        self._record(tok, reads, writes)
        self.ninst += 1
        return tok

    def finish(self, e="sp"):
        for en in self.eng:
            if self.cnt[en] > 0:
                self._wait(e, ("e", en, self.cnt[en]))
        for i, s in enumerate(self.dma_sems):
            if s[1] > 0:
                self._wait(e, ("d", i, s[1]))

    def close(self):
        for cm in reversed(self._ctx):
            cm.__exit__(None, None, None)


INPUT_SHAPES = {
    'x': [NB, SEQ, D], 'c': [NB, D], 'ctx': [NB, CTX, D], 'c_ctx': [D],
    'w_mod': [D, 6 * D], 'b_mod': [6 * D], 'norm1_g': [D], 'norm2_g': [D], 'w_in': [D, N_IN],
    's5_a_re': [2, 64, 64], 's5_a_im': [2, 64, 64], 's5_log_dt': [2, 64],
    's5_b_re': [2, 64, 64, 16], 's5_b_im': [2, 64, 64, 16], 's5_c_re': [2, 64, 16, 64], 's5_c_im': [2, 64, 16, 64],
    's5_d': [D], 's5_glu_w': [D, 2 * D], 'rwkv_mu': [3328], 'rwkv_w0': [2, D], 'rwkv_w_up': [2, 64, D],
    'rwkv_a0': [2, D], 'rwkv_a_up': [2, 64, D], 'rwkv_g_up': [128, D], 'rwkv_k_k': [D], 'rwkv_k_a': [D],
    'rwkv_r_k': [2, D], 'rwkv_lnx_g': [D], 'rwkv_lnx_b': [D], 'rwkv_w_out': [D, D], 'w_out': [D, D],
    'ffn_w_up': [D, 2 * D_FF], 'ffn_conv_w': [3, 2 * D_FF], 'ffn_conv_b': [2 * D_FF], 'ffn_w_down': [D_FF, D],
    'final_norm_g': [D],
}


def _cb(C):
    n = -(-C // 2048)
    while C % n:
        n += 1
    return C // n


def build_program(stage="all", dbg_specs=None):
    nc = bass.Bass("TRN2", target_bir_lowering=False)
    inp = {n: Buf(nc.dram_tensor(n, s, F32, kind="ExternalInput"), n) for n, s in INPUT_SHAPES.items()}
    outb = Buf(nc.dram_tensor("y", [NB, SEQ, D], F32, kind="ExternalOutput"), "y")
    dbg = {}
    for n, s in (dbg_specs or {}).items():
        dbg[n] = Buf(nc.dram_tensor(n, list(s), F32, kind="ExternalOutput"), n)

    def dram(name, shape, dt=F32):
        return Buf(nc.dram_tensor(name, list(shape), dt), name)

    wbf = {n: dram(n + "_bf", INPUT_SHAPES[n], BF16) for n in
           ['w_in', 's5_glu_w', 'rwkv_w_out', 'w_out', 'ffn_w_up', 'ffn_w_down']}
    ys_s = dram("ys_s", [NB, 2, 8, 128, SEQ])
    wkv_s = dram("wkv_s", [NB, 2, SEQ, D])
    bon_s = dram("bon_s", [NB, 2, 16, 64, SEQ])
    hx1_s = dram("hx1_s", [NB, SEQ, D])
    hT_s = dram("hT_s", [NB, 2, 128, 8 * NCOL], BF16)
    hT_done = set()

    es = ExitStack()
    k = K(nc)

    uniq = [0]

    def sbt(stack, name, shape, dt=F32):
        uniq[0] += 1
        nm = "%s_%d" % (name, uniq[0])
        return Buf(stack.enter_context(nc.sbuf_tensor(nm, list(shape), dt)), nm)

    def sb(name, shape, dt=F32):
        return sbt(es, name, shape, dt)

    psb = [Buf(es.enter_context(nc.psum_tensor("psb%d" % i, [128, 512], F32)), "psb%d" % i) for i in range(8)]
    pstate = [0]

    def newps():
        p = psb[pstate[0] % 8]
        pstate[0] += 1
        return p

    def barrier():
        for e in k.eng:
            k.finish(e)

    rr = [0]

    def ew():
        rr[0] += 1
        return "dve" if rr[0] % 2 else "act"

    def mm(out, lhsT, rhs, start, stop, R, W):
        k.op("pe", lambda e: e.matmul(out, lhsT=lhsT, rhs=rhs, start=start, stop=stop), reads=R, writes=W)

    def copy(eng, out, in_, R, W):
        if eng == "act":
            k.op("act", lambda e: e.activation(out=out, in_=in_, func=AF.Copy), reads=R, writes=W)
        else:
            k.op(eng, lambda e: e.tensor_copy(out=out, in_=in_), reads=R, writes=W)

    def dout(name, src_ap, R):
        if name in dbg:
            k.dma("sp", dbg[name].t.ap(), src_ap, reads=R, writes=[dbg[name]])

    ident = sb("ident", [128, 128])
    J128 = sb("J128", [128, 128])
    ones = sb("ones", [128, 128])
    identb = sb("identb", [128, 128], BF16)
    maskL = sb("maskL", [128, TB])
    maskR = sb("maskR", [128, TB])
    mask1 = sb("mask1", [128, TB])
    epsT = sb("epsT", [128, 1])
    eps2T = sb("eps2T", [128, 1])
    hpiT = sb("hpiT", [128, 1])
    for t_ in (ident, J128, ones, maskL, maskR, mask1):
        k.op("pool", lambda e, t_=t_: e.memset(t_.t[:], 1.0), writes=[t_])
    k.op("pool", lambda e: e.memset(epsT.t[:], 1e-6), writes=[epsT])
    k.op("pool", lambda e: e.memset(eps2T.t[:], 64e-5), writes=[eps2T])
    k.op("pool", lambda e: e.memset(hpiT.t[:], float(np.pi / 2)), writes=[hpiT])

    def asel_b(buf, pattern, cm, base, op, view=None):
        v = buf.t[:] if view is None else view
        k.op("pool", lambda e: e.affine_select(out=v, in_=v, pattern=pattern, compare_op=op, fill=0.0,
                                               base=base, channel_multiplier=cm), reads=[buf], writes=[buf])

    asel_b(ident, [[-1, 128]], 1, 0, ALU.is_equal)
    asel_b(J128, [[1, 128]], 1, -127, ALU.is_equal)
    k.op("pool", lambda e: e.memset(maskL.t[:].rearrange("p (a b) -> p a b", b=64)[:, :, 0:1], 0.0), reads=[maskL], writes=[maskL])
    k.op("pool", lambda e: e.memset(maskR.t[:].rearrange("p (a b) -> p a b", b=64)[:, :, 63:64], 0.0), reads=[maskR], writes=[maskR])
    copy("dve", identb.t[:], ident.t[:], [ident], [identb])

    stage_t = sb("stage_t", [128, 128])

    def load_cols(src2d, rows, lanes, dst_fn):
        r0 = 0
        while r0 < rows:
            nr = min(128, rows - r0)
            k.dma("sp", stage_t.t[0:nr, 0:lanes], src2d[r0:r0 + nr, :], reads=[], writes=[stage_t])
            p = newps()
            mm(p.t[0:lanes, 0:nr], stage_t.t[0:nr, 0:lanes], ident.t[0:nr, 0:nr], True, True, [stage_t, ident], [p])
            copy("dve", dst_fn(r0, nr), p.t[0:lanes, 0:nr], [p], [])
            r0 += nr

    def colparam(name, src_buf, pat, lanes, ncols, **kw):
        t = sb(name, [lanes, ncols])
        src = src_buf.t.ap().rearrange(pat, **kw)
        load_cols(src, ncols, lanes, lambda r0, nr: t.t[:, r0:r0 + nr])
        k.state[(t.id, None)] = [("e", "dve", k.cnt["dve"]), []]
        return t

    mu64 = colparam("mu64", inp['rwkv_mu'], "(r l) -> r l", 64, 52, l=64)
    mu128 = colparam("mu128", inp['rwkv_mu'], "(r l) -> r l", 128, 26, l=128)
    w0T = colparam("w0T", inp['rwkv_w0'], "d (h l) -> (d h) l", 64, 32, l=64)
    a0T = colparam("a0T", inp['rwkv_a0'], "d (h l) -> (d h) l", 64, 32, l=64)
    rkT = colparam("rkT", inp['rwkv_r_k'], "d (h l) -> (d h) l", 64, 32, l=64)
    kkT = colparam("kkT", inp['rwkv_k_k'], "(h l) -> h l", 64, 16, l=64)
    kaT = colparam("kaT", inp['rwkv_k_a'], "(h l) -> h l", 64, 16, l=64)
    lnxg = colparam("lnxg", inp['rwkv_lnx_g'], "(h l) -> h l", 128, 8, l=128)
    lnxb = colparam("lnxb", inp['rwkv_lnx_b'], "(h l) -> h l", 128, 8, l=128)
    s5d = colparam("s5d", inp['s5_d'], "(h l) -> h l", 128, 8, l=128)
    g1c = colparam("g1c", inp['norm1_g'], "(h l) -> h l", 128, 8, l=128)
    g2c = colparam("g2c", inp['norm2_g'], "(h l) -> h l", 128, 8, l=128)
    cvw = colparam("cvw", inp['ffn_conv_w'], "j (h l) -> (j h) l", 128, 132, l=128)
    cvb = colparam("cvb", inp['ffn_conv_b'], "(h l) -> h l", 128, 44, l=128)
    bmc = colparam("bmc", inp['b_mod'], "(h l) -> h l", 128, 48, l=128)
    omu64 = sb("omu64", [64, 52])
    omu128 = sb("omu128", [128, 26])
    omka = sb("omka", [64, 16])
    k.op("dve", lambda e: e.tensor_scalar(out=omu64.t[:], in0=mu64.t[:], scalar1=-1.0, scalar2=1.0, op0=ALU.mult, op1=ALU.add), reads=[mu64], writes=[omu64])
    k.op("dve", lambda e: e.tensor_scalar(out=omu128.t[:], in0=mu128.t[:], scalar1=-1.0, scalar2=1.0, op0=ALU.mult, op1=ALU.add), reads=[mu128], writes=[omu128])
    k.op("dve", lambda e: e.tensor_scalar(out=omka.t[:], in0=kaT.t[:], scalar1=-1.0, scalar2=1.0, op0=ALU.mult, op1=ALU.add), reads=[kaT], writes=[omka])

    if stage == "p0a":
        k.finish("sp")
        for e_ in ("act", "dve", "pool", "pe"):
            k.finish(e_)
        return nc
    condT = sb("condT", [128, 8, 3])
    modT = sb("modT", [128, 48, 3])
    s1T = sb("s1T", [128, 8, 3])
    s2T = sb("s2T", [128, 8, 3])
    fgB = sb("fgB", [128, D])
    hT = sb("hT", [128, 8, NCOL], BF16)
    xin = [sb("xin%d" % i, [128, D]) for i in range(2)]
    xn = [sb("xn%d" % i, [128, D]) for i in range(2)]
    junk = sb("junk", [128, D])
    ssq = sb("ssq", [128, 4])

    es0 = ExitStack()
    fgrow = sbt(es0, "fgrow", [1, D])
    k.dma("sp", fgrow.t[:], inp['final_norm_g'].t.ap().rearrange("(o n) -> o n", o=1), writes=[fgrow])
    for hf in range(2):
        p = newps()
        mm(p.t[:, :], ones.t[0:1, 0:128], fgrow.t[0:1, hf * 512:(hf + 1) * 512], True, True, [ones, fgrow], [p])
        copy("dve", fgB.t[:, hf * 512:(hf + 1) * 512], p.t[:, :], [p], [fgB])

    cst_f = [sbt(es0, "cst_f%d" % i, [128, 2048]) for i in range(4)]
    cst_b = [sbt(es0, "cst_b%d" % i, [128, 2048], BF16) for i in range(4)]
    ci = 0
    for n, wb in wbf.items():
        R_, C_ = INPUT_SHAPES[n]
        cb = _cb(C_)
        src = inp[n].t.ap()
        dst = wb.t.ap()
        for rc in range(R_ // 128):
            for c0 in range(0, C_, cb):
                f = cst_f[ci % 4]
                bb = cst_b[ci % 4]
                ce = ["act", "pool"][ci % 2]
                k.dma("sp", f.t[:, 0:cb], src[rc * 128:(rc + 1) * 128, c0:c0 + cb], reads=[], writes=[f])
                copy(ce, bb.t[:, 0:cb], f.t[:, 0:cb], [f], [bb])
                k.dma(ce, dst[rc * 128:(rc + 1) * 128, c0:c0 + cb], bb.t[:, 0:cb], reads=[bb], writes=[wb])
                ci += 1

    if stage == "p0b":
        k.finish("sp")
        for e_ in ("act", "dve", "pool", "pe"):
            k.finish(e_)
        return nc
    c3 = sbt(es0, "c3", [3, D])
    k.dma("sp", c3.t[0:2, :], inp['c'].t.ap(), writes=[c3])
    k.dma("sp", c3.t[2:3, :], inp['c_ctx'].t.ap().rearrange("(o n) -> o n", o=1), reads=[c3], writes=[c3])
    cond3 = sbt(es0, "cond3", [3, D])
    k.op("act", lambda e: e.activation(out=cond3.t[:], in_=c3.t[:], func=AF.Silu), reads=[c3], writes=[cond3])
    for kc in range(8):
        p = newps()
        mm(p.t[:, 0:3], cond3.t[0:3, kc * 128:(kc + 1) * 128], ident.t[0:3, 0:3], True, True, [cond3, ident], [p])
        copy("dve", condT.t[:, kc, :], p.t[:, 0:3], [p], [condT])
    wm = [sbt(es0, "wm%d" % i, [128, 8, 256]) for i in range(2)]
    wmod_v = inp['w_mod'].t.ap().rearrange("(kc p) n -> p kc n", p=128)
    for jb in range(24):
        w_ = wm[jb % 2]
        k.dma("sp", w_.t[:], wmod_v[:, :, jb * 256:(jb + 1) * 256], reads=[], writes=[w_])
        for oc in range(2):
            p = newps()
            for kc in range(8):
                mm(p.t[:, 0:3], w_.t[:, kc, oc * 128:(oc + 1) * 128], condT.t[:, kc, :], kc == 0, kc == 7, [w_, condT], [p])
            col = jb * 2 + oc
            k.op("dve", lambda e, p=p, col=col: e.tensor_scalar(out=modT.t[:, col, :], in0=p.t[:, 0:3], scalar1=bmc.t[:, col:col + 1], scalar2=None, op0=ALU.add),
                 reads=[p, bmc], writes=[modT])
    for i in range(3):
        k.op("dve", lambda e, i=i: e.scalar_tensor_tensor(out=s1T.t[:, :, i], in0=modT.t[:, 8:16, i], scalar=1.0, in1=g1c.t[:, 0:8], op0=ALU.add, op1=ALU.mult),
             reads=[modT, g1c], writes=[s1T])
        k.op("dve", lambda e, i=i: e.scalar_tensor_tensor(out=s2T.t[:, :, i], in0=modT.t[:, 32:40, i], scalar=1.0, in1=g2c.t[:, 0:8], op0=ALU.add, op1=ALU.mult),
             reads=[modT, g2c], writes=[s2T])
    dout("dbg_mod", modT.t[:].rearrange("p a b -> p (a b)"), [modT])
    barrier()
    es0.close()

    if stage == "p0c":
        k.finish("sp")
        for e_ in ("act", "dve", "pool", "pe"):
            k.finish(e_)
        return nc
    for kc_ in range(8):
        k.op("dve", lambda e, kc_=kc_: e.memset(hT.t[:, kc_, :], 0.0), writes=[(hT, kc_)])

    def rms_rstd(srcb, src, dst_col, eps=1e-6, n=D):
        k.op("dve", lambda e: e.tensor_tensor(out=junk.t[:, 0:n], in0=src, in1=src, op=ALU.mult), reads=[srcb], writes=[junk])
        k.op("dve", lambda e: e.tensor_reduce(out=ssq.t[:, 0:1], in_=junk.t[:, 0:n], axis=AX.X, op=ALU.add), reads=[junk], writes=[ssq])
        k.op("dve", lambda e: e.tensor_scalar(out=ssq.t[:, 1:2], in0=ssq.t[:, 0:1], scalar1=1.0 / n, scalar2=eps, op0=ALU.mult, op1=ALU.add), reads=[ssq], writes=[ssq])
        k.op("act", lambda e: e.activation(out=ssq.t[:, 1:2], in_=ssq.t[:, 1:2], func=AF.Sqrt), reads=[ssq], writes=[ssq])
        k.op("dve", lambda e: e.reciprocal(out=dst_col, in_=ssq.t[:, 1:2]), reads=[ssq], writes=[ssq])

    if stage in ("hTa", "hTb", "hTc"):
        xt = xin[0]; xn_ = xn[0]
        k.dma("sp", xt.t[:], inp['x'].t.ap()[0, 0:128, :], reads=[], writes=[xt])
        if stage == "hTa":
            k.op("dve", lambda e: e.tensor_tensor(out=junk.t[:], in0=xt.t[:], in1=xt.t[:], op=ALU.mult), reads=[xt], writes=[junk])
            k.op("dve", lambda e: e.tensor_reduce(out=ssq.t[:, 0:1], in_=junk.t[:], axis=AX.X, op=ALU.add), reads=[junk], writes=[ssq])
            k.op("dve", lambda e: e.tensor_scalar(out=xn_.t[:], in0=xt.t[:], scalar1=ssq.t[:, 0:1], scalar2=None, op0=ALU.mult), reads=[xt, ssq], writes=[xn_])
        elif stage == "hTb":
            rms_rstd(xt, xt.t[:], ssq.t[:, 2:3])
            k.op("dve", lambda e: e.tensor_scalar(out=xn_.t[:], in0=xt.t[:], scalar1=ssq.t[:, 2:3], scalar2=None, op0=ALU.mult), reads=[xt, ssq], writes=[xn_])
        else:
            p = newps()
            mm(p.t[:, 0:128], xt.t[:, 0:128], ident.t[:, :], True, True, [xt, ident], [p])
            k.op("dve", lambda e: e.tensor_scalar(out=xn_.t[:, 0:128], in0=p.t[:, 0:128], scalar1=s1T.t[:, 0, 0:1], scalar2=modT.t[:, 0, 0:1], op0=ALU.mult, op1=ALU.add), reads=[p, s1T, modT], writes=[xn_])
            k.op("dve", lambda e: e.tensor_copy(out=xn_.t[:, 128:1024], in_=xt.t[:, 128:1024]), reads=[xt], writes=[xn_])
        k.dma("sp", dbg["dbg_x"].t.ap(), xn_.t[:], reads=[xn_], writes=[dbg["dbg_x"]])
        k.finish("sp")
        for e_ in ("act", "dve", "pool", "pe"):
            k.finish(e_)
        return nc
    def build_hT(b, rev):
        key_ = (b, 1 if rev else 0)
        hflat = hT.t[:].rearrange("p a n -> p (a n)")
        if key_ in hT_done:
            k.dma("sp", hflat, hT_s.t.ap()[b, key_[1]], reads=[hT_s] + hT_all, writes=hT_all)
            return
        build_hT_raw(b, rev)
        k.dma("sp", hT_s.t.ap()[b, key_[1]], hflat, reads=hT_all, writes=[hT_s])
        hT_done.add(key_)

    def build_hT_raw(b, rev):
        ti = 0
        for (src3, ntile, item, base, seglen) in ((inp['ctx'], 2, 2, CTX0, CTX), (inp['x'], 16, b, LAT0, SEQ)):
            for i in range(min(ntile, int(os.environ.get('HT_NT', '99')))):
                xt = xin[ti % 2]
                xn_ = xn[ti % 2]
                ti += 1
                k.dma("sp", xt.t[:], src3.t.ap()[b, i * 128:(i + 1) * 128, :], reads=[], writes=[xt])
                rms_rstd(xt, xt.t[:], ssq.t[:, 2:3])
                k.op("dve", lambda e, xt=xt, xn_=xn_: e.tensor_scalar(out=xn_.t[:], in0=xt.t[:], scalar1=ssq.t[:, 2:3], scalar2=None, op0=ALU.mult),
                     reads=[xt, ssq], writes=[xn_])
                dest = base + (i * 128 if not rev else seglen - 128 - i * 128)
                T_ = J128 if rev else ident
                for half in range(2):
                    p = newps()
                    for q in range(4):
                        kc = half * 4 + q
                        mm(p.t[:, q * 128:(q + 1) * 128], xn_.t[:, kc * 128:(kc + 1) * 128], T_.t[:, :], True, True, [xn_, T_], [p])
                    for q in range(0 if os.environ.get('HT_SKIP_EVAC') is None else 9, 4):
                        kc = half * 4 + q
                        if False:
                            k.op("act", lambda e, p=p, q=q, kc=kc: e.activation(out=hT.t[:, kc, dest:dest + 128], in_=p.t[:, q * 128:(q + 1) * 128], func=AF.Identity,
                                                                               scale=s1T.t[:, kc, item:item + 1], bias=modT.t[:, kc, item:item + 1]),
                                 reads=[p, s1T, modT], writes=[(hT, kc)])
                        else:
                            k.op("dve", lambda e, p=p, q=q, kc=kc: e.tensor_scalar(out=hT.t[:, kc, dest:dest + 128], in0=p.t[:, q * 128:(q + 1) * 128],
                                                                                  scalar1=s1T.t[:, kc, item:item + 1], scalar2=modT.t[:, kc, item:item + 1],
                                                                                  op0=ALU.mult, op1=ALU.add),
                                 reads=[p, s1T, modT], writes=[(hT, kc)])

    hT_all = [(hT, kc) for kc in range(8)]

    def bc_last(ap, n):
        return AP(ap.tensor, ap.offset, [list(x) for x in ap.ap] + [[0, n]])

    def tt(eng, out, in0, in1, op, R, W):
        k.op(eng, lambda e: e.tensor_tensor(out=out, in0=in0, in1=in1, op=op), reads=R, writes=W)

    def s5_build_tables(d, gh, T):
        goff = gh * 32
        est = ExitStack()
        f = lambda n, sh: sbt(est, n, sh)
        aTr, aTi, dtT, mag, ang, cs, sn, t1, t2, t3, lamr, lami, rden, lm1, cfr, cfi = [f("s5t%d" % i, [64, 32]) for i in range(16)]
        dtrow = f("dtrow", [1, 32])
        Lr = f("Lr", [64, 9, 32])
        Li = f("Li", [64, 9, 32])
        Br = f("Br", [64, 32, 16]); Bi = f("Bi", [64, 32, 16])
        Bbr = f("Bbr", [64, 32, 16]); Bbi = f("Bbi", [64, 32, 16])
        CTr = f("CTr", [64, 32, 16]); CTi = f("CTi", [64, 32, 16])
        tmpa = f("tmpa", [64, 32, 16]); tmpb = f("tmpb", [64, 32, 16])
        Dr = f("Dr", [64, 8, 9, 16]); Dni = f("Dni", [64, 8, 9, 16])
        T1r = f("T1r", [64, 8, 8, 16]); T1i = f("T1i", [64, 8, 8, 16])
        tq1 = f("tq1", [64, 8, 9, 16]); tq2 = f("tq2", [64, 8, 9, 16])
        ZB = [[sbt(est, "ZB%d%d" % (i, j), [64, 240], BF16) for j in range(2)] for i in range(2)]
        ZD = [[sbt(est, "ZD%d%d" % (i, j), [64, 256], BF16) for j in range(2)] for i in range(2)]
        for i in range(2):
            for j in range(2):
                for z in (ZB[i][j], ZD[i][j]):
                    k.op("pool", lambda e, z=z: e.memset(z.t[:], 0.0), writes=[z])

        def lc(src, dstb):
            load_cols(src, 32, 64, lambda r0, nr: dstb.t[:, r0:r0 + nr])
            k.state[(dstb.id, None)] = [("e", "dve", k.cnt["dve"]), []]
        lc(inp['s5_a_re'].t.ap()[d, goff:goff + 32, :], aTr)
        lc(inp['s5_a_im'].t.ap()[d, goff:goff + 32, :], aTi)
        k.dma("sp", dtrow.t[:], inp['s5_log_dt'].t.ap()[d:d + 1, goff:goff + 32], writes=[dtrow])
        p = newps()
        mm(p.t[0:64, 0:32], ones.t[0:1, 0:64], dtrow.t[0:1, :], True, True, [ones, dtrow], [p])
        k.op("act", lambda e: e.activation(out=dtT.t[:], in_=p.t[0:64, 0:32], func=AF.Exp), reads=[p], writes=[dtT])
        tt("dve", t1.t[:], aTr.t[:], dtT.t[:], ALU.mult, [aTr, dtT], [t1])
        k.op("act", lambda e: e.activation(out=mag.t[:], in_=t1.t[:], func=AF.Exp), reads=[t1], writes=[mag])
        tt("dve", ang.t[:], aTi.t[:], dtT.t[:], ALU.mult, [aTi, dtT], [ang])
        k.op("act", lambda e: e.activation(out=sn.t[:], in_=ang.t[:], func=AF.Sin, scale=0.125), reads=[ang], writes=[sn])
        k.op("act", lambda e: e.activation(out=cs.t[:], in_=ang.t[:], func=AF.Sin, scale=-0.125, bias=hpiT.t[0:64, :]), reads=[ang, hpiT], writes=[cs])
        for _ in range(3):
            tt("dve", t1.t[:], cs.t[:], cs.t[:], ALU.mult, [cs], [t1])
            tt("dve", t2.t[:], sn.t[:], sn.t[:], ALU.mult, [sn], [t2])
            tt("dve", t3.t[:], cs.t[:], sn.t[:], ALU.mult, [cs, sn], [t3])
            tt("dve", cs.t[:], t1.t[:], t2.t[:], ALU.subtract, [t1, t2], [cs])
            tt("dve", sn.t[:], t3.t[:], t3.t[:], ALU.add, [t3], [sn])
        tt("dve", lamr.t[:], mag.t[:], cs.t[:], ALU.mult, [mag, cs], [lamr])
        tt("dve", lami.t[:], mag.t[:], sn.t[:], ALU.mult, [mag, sn], [lami])
        tt("dve", t1.t[:], aTr.t[:], aTr.t[:], ALU.mult, [aTr], [t1])
        tt("dve", t2.t[:], aTi.t[:], aTi.t[:], ALU.mult, [aTi], [t2])
        tt("dve", t1.t[:], t1.t[:], t2.t[:], ALU.add, [t1, t2], [t1])
        k.op("dve", lambda e: e.reciprocal(out=rden.t[:], in_=t1.t[:]), reads=[t1], writes=[rden])
        k.op("dve", lambda e: e.tensor_scalar(out=lm1.t[:], in0=lamr.t[:], scalar1=-1.0, scalar2=None, op0=ALU.add), reads=[lamr], writes=[lm1])
        tt("dve", t1.t[:], lm1.t[:], aTr.t[:], ALU.mult, [lm1, aTr], [t1])
        tt("dve", t2.t[:], lami.t[:], aTi.t[:], ALU.mult, [lami, aTi], [t2])
        tt("dve", t1.t[:], t1.t[:], t2.t[:], ALU.add, [t1, t2], [t1])
        tt("dve", cfr.t[:], t1.t[:], rden.t[:], ALU.mult, [t1, rden], [cfr])
        tt("dve", t1.t[:], lami.t[:], aTr.t[:], ALU.mult, [lami, aTr], [t1])
        tt("dve", t2.t[:], lm1.t[:], aTi.t[:], ALU.mult, [lm1, aTi], [t2])
        tt("dve", t1.t[:], t1.t[:], t2.t[:], ALU.subtract, [t1, t2], [t1])
        tt("dve", cfi.t[:], t1.t[:], rden.t[:], ALU.mult, [t1, rden], [cfi])
        k.op("dve", lambda e: e.memset(Lr.t[:, 0, :], 1.0), writes=[Lr])
        k.op("dve", lambda e: e.memset(Li.t[:, 0, :], 0.0), writes=[Li])
        for q in range(8):
            tt("dve", t1.t[:], Lr.t[:, q, :], lamr.t[:], ALU.mult, [Lr, lamr], [t1])
            tt("dve", t2.t[:], Li.t[:, q, :], lami.t[:], ALU.mult, [Li, lami], [t2])
            tt("dve", Lr.t[:, q + 1, :], t1.t[:], t2.t[:], ALU.subtract, [t1, t2], [Lr])
            tt("dve", t1.t[:], Lr.t[:, q, :], lami.t[:], ALU.mult, [Lr, lami], [t1])
            tt("dve", t2.t[:], Li.t[:, q, :], lamr.t[:], ALU.mult, [Li, lamr], [t2])
            tt("dve", Li.t[:, q + 1, :], t1.t[:], t2.t[:], ALU.add, [t1, t2], [Li])
        copy("dve", cs.t[:], Lr.t[:, 8, :], [Lr], [cs])
        copy("dve", sn.t[:], Li.t[:, 8, :], [Li], [sn])
        for r_ in range(7):
            copy("dve", T['LPr'].t[:, r_, 0, :], cs.t[:], [cs], [T['LPr']])
            copy("dve", T['LPr'].t[:, r_, 1, :], cs.t[:], [cs], [T['LPr']])
            k.op("dve", lambda e, r_=r_: e.tensor_scalar(out=T['LPis'].t[:, r_, 0, :], in0=sn.t[:], scalar1=-1.0, scalar2=None, op0=ALU.mult), reads=[sn], writes=[T['LPis']])
            copy("dve", T['LPis'].t[:, r_, 1, :], sn.t[:], [sn], [T['LPis']])
            if r_ < 6:
                tt("dve", t1.t[:], cs.t[:], cs.t[:], ALU.mult, [cs], [t1])
                tt("dve", t2.t[:], sn.t[:], sn.t[:], ALU.mult, [sn], [t2])
                tt("dve", t3.t[:], cs.t[:], sn.t[:], ALU.mult, [cs, sn], [t3])
                tt("dve", cs.t[:], t1.t[:], t2.t[:], ALU.subtract, [t1, t2], [cs])
                tt("dve", sn.t[:], t3.t[:], t3.t[:], ALU.add, [t3], [sn])
        for (srcn, dstb) in (('s5_b_re', Br), ('s5_b_im', Bi)):
            for q in range(2):
                k.dma("sp", dstb.t[:, q * 16:(q + 1) * 16, :], inp[srcn].t.ap()[d, goff + q * 16:goff + (q + 1) * 16].rearrange("g p c -> p g c"), reads=[dstb], writes=[dstb])
        tt("dve", tmpa.t[:], Br.t[:], bc_last(cfr.t[:], 16), ALU.mult, [Br, cfr], [tmpa])
        tt("dve", tmpb.t[:], Bi.t[:], bc_last(cfi.t[:], 16), ALU.mult, [Bi, cfi], [tmpb])
        tt("dve", Bbr.t[:], tmpa.t[:], tmpb.t[:], ALU.subtract, [tmpa, tmpb], [Bbr])
        tt("dve", tmpa.t[:], Bi.t[:], bc_last(cfr.t[:], 16), ALU.mult, [Bi, cfr], [tmpa])
        tt("dve", tmpb.t[:], Br.t[:], bc_last(cfi.t[:], 16), ALU.mult, [Br, cfi], [tmpb])
        tt("dve", Bbi.t[:], tmpa.t[:], tmpb.t[:], ALU.add, [tmpa, tmpb], [Bbi])
        for (srcn, dstb) in (('s5_c_re', CTr), ('s5_c_im', CTi)):
            cv = inp[srcn].t.ap()[d, goff:goff + 32].rearrange("g c p -> (g c) p")
            for rt in range(4):
                k.dma("sp", stage_t.t[:, 0:64], cv[rt * 128:(rt + 1) * 128, :], reads=[], writes=[stage_t])
                p = newps()
                mm(p.t[0:64, 0:128], stage_t.t[:, 0:64], ident.t[:, :], True, True, [stage_t, ident], [p])
                copy("dve", dstb.t[:, rt * 8:(rt + 1) * 8, :].rearrange("p g c -> p (g c)"), p.t[0:64, 0:128], [p], [dstb])

        def ap4(base_ap, dims):
            return AP(base_ap.tensor, base_ap.offset, [list(base_ap.ap[0])] + dims)
        for gb in range(4):
            g0 = gb * 8
            ctr_v = ap4(CTr.t[:, g0:g0 + 8, :], [[16, 8], [0, 9], [1, 16]])
            cti_v = ap4(CTi.t[:, g0:g0 + 8, :], [[16, 8], [0, 9], [1, 16]])
            lr_v = ap4(Lr.t[:, 0:9, g0:g0 + 8], [[1, 8], [32, 9], [0, 16]])
            li_v = ap4(Li.t[:, 0:9, g0:g0 + 8], [[1, 8], [32, 9], [0, 16]])
            tt("dve", tq1.t[:], ctr_v, lr_v, ALU.mult, [CTr, Lr], [tq1])
            tt("pool", tq2.t[:], cti_v, li_v, ALU.mult, [CTi, Li], [tq2])
            tt("dve", Dr.t[:], tq1.t[:], tq2.t[:], ALU.subtract, [tq1, tq2], [Dr])
            tt("dve", tq1.t[:], ctr_v, li_v, ALU.mult, [CTr, Li], [tq1])
            tt("pool", tq2.t[:], cti_v, lr_v, ALU.mult, [CTi, Lr], [tq2])
            k.op("dve", lambda e: e.scalar_tensor_tensor(out=Dni.t[:], in0=tq1.t[:], scalar=-1.0, in1=tq2.t[:], op0=ALU.mult, op1=ALU.subtract),
                 reads=[tq1, tq2], writes=[Dni])
            bbr_v = ap4(Bbr.t[:, g0:g0 + 8, :], [[16, 8], [0, 8], [1, 16]])
            bbi_v = ap4(Bbi.t[:, g0:g0 + 8, :], [[16, 8], [0, 8], [1, 16]])
            l7r = ap4(Lr.t[:, 7:8, g0:g0 + 8], [[1, 8], [-32, 8], [0, 16]])
            l7i = ap4(Li.t[:, 7:8, g0:g0 + 8], [[1, 8], [-32, 8], [0, 16]])
            q1 = tq1.t[:, :, 0:8, :]
            q2 = tq2.t[:, :, 0:8, :]
            tt("dve", q1, bbr_v, l7r, ALU.mult, [Bbr, Lr], [tq1])
            tt("pool", q2, bbi_v, l7i, ALU.mult, [Bbi, Li], [tq2])
            tt("dve", T1r.t[:], q1, q2, ALU.subtract, [tq1, tq2], [T1r])
            tt("dve", q1, bbi_v, l7r, ALU.mult, [Bbi, Lr], [tq1])
            tt("pool", q2, bbr_v, l7i, ALU.mult, [Bbr, Li], [tq2])
            tt("dve", T1i.t[:], q1, q2, ALU.add, [tq1, tq2], [T1i])
            for (Tsrc, Wt) in ((T1r, T['Wr']), (T1i, T['Wi'])):
                p = newps()
                for gi in range(8):
                    mm(p.t[:, gi * 64:(gi + 1) * 64], Tsrc.t[:, gi, :, :].rearrange("p t c -> p (t c)"), ident.t[0:64, 0:64], True, True, [Tsrc, ident], [p])
                copy(ew(), Wt.t[:, g0:g0 + 8, :].rearrange("p g m -> p (g m)"), p.t[:, :], [p], [Wt])
            copy("act", T['Q'].t[:, g0:g0 + 8, 0, :].rearrange("p g (t c) -> p g t c", c=16), Dr.t[:, :, 1:9, :], [Dr], [T['Q']])
            copy("act", T['Q'].t[:, g0:g0 + 8, 1, :].rearrange("p g (t c) -> p g t c", c=16), Dni.t[:, :, 1:9, :], [Dni], [T['Q']])
            for gq in range(2):
                p = newps()
                for gi4 in range(4):
                    gi = gq * 4 + gi4
                    g = g0 + gi
                    zb = ZB[gi % 2]
                    zd = ZD[gi % 2]
                    copy("pool", zb[0].t[:, 112:128], Bbr.t[:, g, :], [Bbr], [zb[0]])
                    copy("pool", zb[1].t[:, 112:128], Bbi.t[:, g, :], [Bbi], [zb[1]])
                    copy("act", zd[0].t[:, 128:256], Dr.t[:, gi, 0:8, :].rearrange("p t c -> p (t c)"), [Dr], [zd[0]])
                    copy("act", zd[1].t[:, 128:256], Dni.t[:, gi, 0:8, :].rearrange("p t c -> p (t c)"), [Dni], [zd[1]])
                    o = p.t[:, gi4 * 128:(gi4 + 1) * 128]
                    for s_ in range(8):
                        mm(o, zb[0].t[:, 112 - 16 * s_:240 - 16 * s_], zd[0].t[:, 128 - 16 * s_:256 - 16 * s_], s_ == 0, False, [zb[0], zd[0]], [p])
                    for s_ in range(8):
                        mm(o, zb[1].t[:, 112 - 16 * s_:240 - 16 * s_], zd[1].t[:, 128 - 16 * s_:256 - 16 * s_], False, s_ == 7, [zb[1], zd[1]], [p])
                copy(ew(), T['A'].t[:, g0 + gq * 4:g0 + gq * 4 + 4, :].rearrange("p g m -> p (g m)"), p.t[:, :], [p], [T['A']])
        barrier()
        est.close()

    def s5_run(b, d, gh, T, W_):
        up2, Ug2, Hb2, Hbf, Yg, ysblk, wu, T13, T24 = W_
        winv = wbf['w_in'].t.ap().rearrange("(kc p) n -> p kc n", p=128)
        blocks = [(CTX0, 32, None)] + [(LAT0 + 512 * i, 64, 512 * i) for i in range(4)]
        for Hb in Hb2:
            k.op("dve", lambda e, Hb=Hb: e.memset(Hb.t[:], 0.0), writes=[(Hb, 'c0'), (Hb, 'inc')])
        def front(bi):
            col0, nj, tok0 = blocks[bi]
            up, Ug, Hb = up2[bi % 2], Ug2[bi % 2], Hb2[bi % 2]
            ntok = 8 * nj
            gpb = 512 // nj
            for half in range(2):
                w_ = wu[0]
                c00 = gh * 512 + half * 256
                k.dma("sp", w_.t[:], winv[:, :, c00:c00 + 256], reads=[wbf['w_in'], w_], writes=[w_])
                for oc2 in range(2):
                    oc = half * 2 + oc2
                    p = newps()
                    for kc in range(8):
                        mm(p.t[:, 0:ntok], w_.t[:, kc, oc2 * 128:(oc2 + 1) * 128], hT.t[:, kc, col0:col0 + ntok], kc == 0, kc == 7, [w_, (hT, kc)], [p])
                    copy("act", up.t[:, oc, :, 0:nj], p.t[:, 0:ntok].rearrange("p (j t) -> p t j", t=8), [p], [up])
            for gq in range(32 // gpb):
                p = newps()
                for gi in range(gpb):
                    g_ = gq * gpb + gi
                    kt, g8 = g_ // 8, g_ % 8
                    for tau in range(8):
                        mm(p.t[:, gi * nj:(gi + 1) * nj], T['Sel'].t[:, g8 * 8 + tau, :], up.t[:, kt, tau, 0:nj], tau == 0, tau == 7, [T['Sel'], up], [p])
                copy("act", Ug.t[:, gq * gpb:(gq + 1) * gpb, 0:nj], p.t[:, 0:gpb * nj].rearrange("p (g j) -> p g j", j=nj), [p], [Ug])
            for part, Wt in ((0, T['Wr']), (1, T['Wi'])):
                for gq in range(32 // gpb):
                    p = newps()
                    for gi in range(gpb):
                        g_ = gq * gpb + gi
                        mm(p.t[0:64, gi * nj:(gi + 1) * nj], Wt.t[:, g_, :], Ug.t[:, g_, 0:nj], True, True, [Wt, Ug], [p])
                    copy("act", Hb.t[:, part, gq * gpb:(gq + 1) * gpb, 1:nj + 1], p.t[0:64, 0:gpb * nj].rearrange("p (g j) -> p g j", j=nj), [p], [(Hb, 'inc')])

        def scan(bi):
            col0, nj, tok0 = blocks[bi]
            Hb = Hb2[bi % 2]
            HK = [(Hb, 'c0'), (Hb, 'inc')]

            def cstep(dst_c0, src_c0, stride, cnt, r_):
                def colv(c0, swap):
                    base = Hb.t[:, :, :, c0:c0 + 1]
                    if swap:
                        return AP(base.tensor, base.offset + 32 * 65, [list(base.ap[0]), [-32 * 65, 2], [65, 32], [stride, cnt]])
                    return AP(base.tensor, base.offset, [list(base.ap[0]), [32 * 65, 2], [65, 32], [stride, cnt]])
                tt("dve", T13.t[:, :, :, 0:cnt], colv(src_c0, False), bc_last(T['LPr'].t[:, r_, :, :], cnt), ALU.mult, HK + [T['LPr']], [T13])
                tt("pool", T24.t[:, :, :, 0:cnt], colv(src_c0, True), bc_last(T['LPis'].t[:, r_, :, :], cnt), ALU.mult, HK + [T['LPis']], [T24])
                tt("dve", colv(dst_c0, False), colv(dst_c0, False), T13.t[:, :, :, 0:cnt], ALU.add, HK + [T13], [(Hb, 'inc')])
                tt("dve", colv(dst_c0, False), colv(dst_c0, False), T24.t[:, :, :, 0:cnt], ALU.add, HK + [T24], [(Hb, 'inc')])
            cstep(1, 0, 1, 1, 0)
            nr = nj.bit_length() - 1
            for r_ in range(nr):
                s_ = 1 << r_
                cstep(1 + 2 * s_ - 1, 1 + s_ - 1, 2 * s_, nj // (2 * s_), r_)
            for r_ in range(nr - 2, -1, -1):
                s_ = 1 << r_
                cnt = (nj - 3 * s_) // (2 * s_) + 1
                cstep(1 + 3 * s_ - 1, 1 + 2 * s_ - 1, 2 * s_, cnt, r_)
            if bi + 1 < len(blocks):
                Hn = Hb2[(bi + 1) % 2]
                copy("dve", Hn.t[:, :, :, 0], Hb.t[:, :, :, nj], [(Hb, 'inc')], [(Hn, 'c0')])

        def back(bi):
            col0, nj, tok0 = blocks[bi]
            if tok0 is None:
                return
            Ug, Hb = Ug2[bi % 2], Hb2[bi % 2]
            ntok = 8 * nj
            gpb = 512 // nj
            copy("act", Hbf.t[:, :, :, 0:nj], Hb.t[:, :, :, 0:nj], [(Hb, 'c0'), (Hb, 'inc')], [Hbf])
            for gq in range(32 // gpb):
                p = newps()
                for gi in range(gpb):
                    g_ = gq * gpb + gi
                    o = p.t[:, gi * nj:(gi + 1) * nj]
                    mm(o, T['A'].t[:, g_, :], Ug.t[:, g_, 0:nj], True, False, [T['A'], Ug], [p])
                    mm(o, T['Q'].t[:, g_, 0, :], Hbf.t[:, 0, g_, 0:nj], False, False, [T['Q'], Hbf], [p])
                    mm(o, T['Q'].t[:, g_, 1, :], Hbf.t[:, 1, g_, 0:nj], False, True, [T['Q'], Hbf], [p])
                copy(ew(), Yg.t[:, gq * gpb:(gq + 1) * gpb, 0:nj], p.t[:, 0:gpb * nj].rearrange("p (g j) -> p g j", j=nj), [p], [Yg])
            for kt in range(4):
                p = newps()
                for tau in range(8):
                    for g8 in range(8):
                        mm(p.t[:, tau * nj:(tau + 1) * nj], T['Sel'].t[:, tau * 8 + g8, :], Yg.t[:, kt * 8 + g8, 0:nj], g8 == 0, g8 == 7, [T['Sel'], Yg], [p])
                copy(ew(), ysblk.t[:, kt % 2, 0:ntok].rearrange("p (j t) -> p t j", t=8), p.t[:, 0:ntok].rearrange("p (t j) -> p t j", t=8), [p], [ysblk])
                if kt % 2 == 1:
                    k0 = gh * 4 + kt - 1
                    k.dma("pool", ys_s.t.ap()[b, d, k0:k0 + 2, :, tok0:tok0 + ntok].rearrange("kt p t -> p kt t"), ysblk.t[:, :, 0:ntok], reads=[ysblk], writes=[ys_s])

        front(0)
        for bi in range(len(blocks)):
            scan(bi)
            if bi + 1 < len(blocks):
                front(bi + 1)
            back(bi)

    e2 = [0]

    def eng2():
        e2[0] += 1
        return "dve" if e2[0] % 2 else "pool"

    def rwkv_phase(d, b_list):
        esr = ExitStack()
        f = lambda n, sh, dt=F32: sbt(esr, n, sh, dt)
        tri2 = f("tri2", [64, 128])
        m_su = f("m_su", [128, 4, 64]); m_ue = f("m_ue", [128, 4, 64]); m_lt = f("m_lt", [128, 4, 64]); I4 = f("I4", [128, 4, 64])
        bones = f("bones", [128, 128])
        for t_ in (tri2, m_su, m_ue, m_lt, I4):
            k.op("pool", lambda e, t_=t_: e.memset(t_.t[:], 1.0), writes=[t_])
        asel_b(tri2, [[1, 64]], -1, 0, ALU.is_ge, view=tri2.t[:, 0:64])
        asel_b(tri2, [[1, 64]], -1, 0, ALU.is_gt, view=tri2.t[:, 64:128])
        for (mb, pat, cm_, op_) in ((m_su, [[0, 4], [1, 64]], -1, ALU.is_gt), (m_ue, [[0, 4], [1, 64]], -1, ALU.is_ge),
                                    (m_lt, [[0, 4], [-1, 64]], 1, ALU.is_gt), (I4, [[0, 4], [-1, 64]], 1, ALU.is_equal)):
            asel_b(mb, pat, cm_, 0, op_, view=mb.t[0:64])
            k.dma("sp", mb.t[64:128], mb.t[0:64], reads=[mb], writes=[mb])
        k.op("pool", lambda e: e.memset(bones.t[:], 0.0), writes=[bones])
        k.op("pool", lambda e: e.memset(bones.t[0:64, 0:64], 1.0), reads=[bones], writes=[bones])
        k.op("pool", lambda e: e.memset(bones.t[64:128, 64:128], 1.0), reads=[bones], writes=[bones])
        def pair_param(name, src_ap2d, rows):
            t_ = f(name, [128, rows])
            load_cols(src_ap2d, rows, 128, lambda r0, nr: t_.t[:, r0:r0 + nr])
            k.state[(t_.id, None)] = [("e", "dve", k.cnt["dve"]), []]
            return t_
        w0P = pair_param("w0P", inp['rwkv_w0'].t.ap()[d].rearrange("(h l) -> h l", l=128), 8)
        a0P = pair_param("a0P", inp['rwkv_a0'].t.ap()[d].rearrange("(h l) -> h l", l=128), 8)
        rkP = pair_param("rkP", inp['rwkv_r_k'].t.ap()[d].rearrange("(h l) -> h l", l=128), 8)
        kkP = pair_param("kkP", inp['rwkv_k_k'].t.ap().rearrange("(h l) -> h l", l=128), 8)
        kaP = pair_param("kaP", inp['rwkv_k_a'].t.ap().rearrange("(h l) -> h l", l=128), 8)
        omkaP = f("omkaP", [128, 8])
        k.op("dve", lambda e: e.tensor_scalar(out=omkaP.t[:], in0=kaP.t[:], scalar1=-1.0, scalar2=1.0, op0=ALU.mult, op1=ALU.add), reads=[kaP], writes=[omkaP])
        wupb = f("wupb", [64, D], BF16); aupb = f("aupb", [64, D], BF16)
        lst = f("lst", [64, D])
        k.dma("sp", lst.t[:], inp['rwkv_w_up'].t.ap()[d], writes=[lst])
        copy("dve", wupb.t[:], lst.t[:], [lst], [wupb])
        k.dma("sp", lst.t[:], inp['rwkv_a_up'].t.ap()[d], reads=[lst], writes=[lst])
        copy("dve", aupb.t[:], lst.t[:], [lst], [aupb])
        wb4 = [f("wb4_%d" % i, [128, 8, 128], BF16) for i in range(4)]
        AT = f("AT", [128, 8, TB], BF16); RT = f("RT", [128, 8, TB], BF16); BT = f("BT", [128, 8, TB], BF16); KT = f("KT", [128, 8, TB], BF16)
        Vtok = f("Vtok", [128, 8, 4, 64], BF16); BHtok = f("BHtok", [128, 8, 4, 64], BF16); KHtok = f("KHtok", [128, 8, 4, 64], BF16)
        gT = f("gT", [128, 8, 4])
        twd = f("twd", [64, TB], BF16); xad = f("xad", [64, TB], BF16)
        t64 = [f("t64_%d" % i, [64, TB]) for i in range(2)]
        tl = {n: f("t_" + n, [128, TB]) for n in ("r", "k", "v", "tmp", "tmp2", "kk", "asig", "kd", "bb", "sgw", "kkn")}
        tb16 = {n: f("tb_" + n, [128, TB], BF16) for n in ("bh", "kh", "vb")}
        E = {n: f("E_" + n, [128, 4, 64]) for n in ("in", "ex", "inv", "rat", "ds")}
        cpad = [f("cpad%d" % i, [128, 4, 96]) for i in range(2)]; cT = f("cT", [128, 4])
        for t_ in cpad:
            k.op("pool", lambda e, t_=t_: e.memset(t_.t[:], 0.0), writes=[t_])
        S0T = f("S0T", [128, 8, 64]); S0Tb = f("S0Tb", [128, 8, 64], BF16)
        Pb = [[f("Pb%d%d" % (g_, i), [128, 4, 64], BF16) for i in range(2)] for g_ in range(2)]
        PTb = [[f("PTb%d%d" % (g_, i), [128, 4, 64], BF16) for i in range(2)] for g_ in range(2)]
        Nb = [[f("Nb%d%d" % (g_, i), [128, 4, 64], BF16) for i in range(2)] for g_ in range(2)]
        AakT = [f("AakT%d" % i, [128, 4, 64], BF16) for i in range(2)]; ArbT = [f("ArbT%d" % i, [128, 4, 64], BF16) for i in range(2)]; ArkT = [f("ArkT%d" % i, [128, 4, 64], BF16) for i in range(2)]
        Xf = [f("Xf%d" % i, [128, 4, 64], BF16) for i in range(2)]; Ub = f("Ub", [128, 8, 64], BF16); Ych = [f("Ych%d" % i, [128, 8, 64]) for i in range(2)]
        stmp = [f("stmp%d" % i, [128, 4, 64]) for i in range(2)]
        winv = wbf['w_in'].t.ap().rearrange("(kc p) n -> p kc n", p=128)
        wi = [0]
        ych_i = [0]
        HV = (slice(0, 64), slice(64, 128))

        for b in b_list:
            build_hT(b, d == 1)
            k.op("dve", lambda e: e.memset(S0T.t[:], 0.0), writes=[(S0T, 0), (S0T, 1)])
            k.op("dve", lambda e: e.memset(S0Tb.t[:], 0.0), writes=[S0Tb])
            for kb in range(9):
                seg_ctx = kb == 0
                col0 = CTX0 if seg_ctx else LAT0 + (kb - 1) * TB
                tok0 = (kb - 1) * TB

                def offs(c64):
                    if seg_ctx:
                        o = -1 if c64 < 26 else 1
                        return (-o if d == 1 else o), mask1
                    o = [-1, 1, -64, 64][c64 // 13]
                    if d == 1:
                        o = -o
                    return o, (mask1 if abs(o) == 64 else (maskL if o == -1 else maskR))

                def lerp(P, rows, c64, mu_t, omu_t, mcol, dst_ap, tmp_ap):
                    o, msk = offs(c64)
                    k.op("dve", lambda e: e.scalar_tensor_tensor(out=tmp_ap, in0=P.t[rows, 64 + o:64 + o + TB], scalar=mu_t.t[rows, mcol:mcol + 1], in1=msk.t[rows, :], op0=ALU.mult, op1=ALU.mult),
                         reads=[P, mu_t, msk], writes=[tl["tmp"]])
                    k.op("dve", lambda e: e.scalar_tensor_tensor(out=dst_ap, in0=P.t[rows, 64:64 + TB], scalar=omu_t.t[rows, mcol:mcol + 1], in1=tmp_ap, op0=ALU.mult, op1=ALU.add),
                         reads=[P, omu_t, tl["tmp"]], writes=[])

                def proj64(c64, dst):
                    wt = wb4[wi[0] % 4]
                    wi[0] += 1
                    cc = 1024 + c64 * 64
                    k.dma("sp", wt.t[:, :, 0:64], winv[:, :, cc:cc + 64], reads=[wbf['w_in']], writes=[wt])
                    P = newps()
                    for kc in range(8):
                        mm(P.t[0:64, 0:TB + 128], wt.t[:, kc, 0:64], hT.t[:, kc, col0 - 64:col0 + TB + 64], kc == 0, kc == 7, [wt, (hT, kc)], [P])
                    lerp(P, slice(0, 64), c64, mu64, omu64, c64, dst.t[:], tl["tmp"].t[0:64, :])
                    k.state[(dst.id, None)] = [("e", "dve", k.cnt["dve"]), []]

                def proj128(j, dst):
                    wt = wb4[wi[0] % 4]
                    wi[0] += 1
                    cc = 1024 + j * 128
                    k.dma("sp", wt.t[:], winv[:, :, cc:cc + 128], reads=[wbf['w_in']], writes=[wt])
                    P = newps()
                    for kc in range(8):
                        mm(P.t[:, 0:TB + 128], wt.t[:, kc, :], hT.t[:, kc, col0 - 64:col0 + TB + 64], kc == 0, kc == 7, [wt, (hT, kc)], [P])
                    if offs(2 * j) == offs(2 * j + 1):
                        lerp(P, slice(0, 128), 2 * j, mu128, omu128, j, dst.t[:], tl["tmp"].t[:])
                    else:
                        for hp in range(2):
                            lerp(P, HV[hp], 2 * j + hp, mu128, omu128, j, dst.t[HV[hp], :], tl["tmp"].t[HV[hp], :])
                    k.state[(dst.id, None)] = [("e", "dve", k.cnt["dve"]), []]

                proj64(48, t64[0])
                k.op("act", lambda e: e.activation(out=twd.t[:], in_=t64[0].t[:], func=AF.Tanh), reads=[t64[0]], writes=[twd])
                proj64(49, t64[1])
                copy("act", xad.t[:], t64[1].t[:], [t64[1]], [xad])
                for HP in range(8):
                    r_h, k_h, v_h, tmp, tmp2, kk, asig, kd, bb, sgw, kkn = [tl[n] for n in ("r", "k", "v", "tmp", "tmp2", "kk", "asig", "kd", "bb", "sgw", "kkn")]
                    proj128(HP, r_h); proj128(8 + HP, k_h); proj128(16 + HP, v_h)
                    psl = slice(HP * 128, (HP + 1) * 128)
                    p = newps()
                    mm(p.t[:, 0:TB], wupb.t[:, psl], twd.t[:], True, True, [wupb, twd], [p])
                    k.op("dve", lambda e, p=p: e.tensor_scalar(out=tmp2.t[:], in0=p.t[:, 0:TB], scalar1=w0P.t[:, HP:HP + 1], scalar2=None, op0=ALU.add), reads=[p, w0P], writes=[tmp2])
                    k.op("act", lambda e: e.activation(out=cpad[0].t[:, :, 32:96], in_=tmp2.t[:].rearrange("p (c t) -> p c t", t=64), func=AF.Sigmoid), reads=[tmp2], writes=[cpad[0]])
                    src_, dst_ = cpad[0], cpad[1]
                    for s_ in (1, 2, 4, 8, 16, 32):
                        tt("dve", dst_.t[:, :, 32:96], src_.t[:, :, 32:96], src_.t[:, :, 32 - s_:96 - s_], ALU.add, [src_], [dst_])
                        src_, dst_ = dst_, src_
                    incl = src_.t[:, :, 32:96]
                    k.op("act", lambda e, incl=incl: e.activation(out=E["in"].t[:], in_=incl, func=AF.Exp, scale=-KAPPA), reads=[src_], writes=[E["in"]])
                    k.op("act", lambda e, incl=incl: e.activation(out=E["inv"].t[:], in_=incl, func=AF.Exp, scale=KAPPA), reads=[src_], writes=[E["inv"]])
                    excl = src_.t[:, :, 31:95]
                    k.op("act", lambda e, excl=excl: e.activation(out=E["ex"].t[:], in_=excl, func=AF.Exp, scale=-KAPPA), reads=[src_], writes=[E["ex"]])
                    copy("dve", cT.t[:], src_.t[:, :, 95], [src_], [cT])
                    tt("dve", E["ds"].t[:], incl, bc_last(cT.t[:], 64), ALU.subtract, [src_, cT], [E["ds"]])
                    k.op("act", lambda e: e.activation(out=E["rat"].t[:], in_=E["ds"].t[:], func=AF.Exp, scale=KAPPA), reads=[E["ds"]], writes=[E["rat"]])
                    copy("pool", gT.t[:, HP, :], E["in"].t[:, :, 63], [E["in"]], [gT])
                    p = newps()
                    mm(p.t[:, 0:TB], aupb.t[:, psl], xad.t[:], True, True, [aupb, xad], [p])
                    k.op("dve", lambda e, p=p: e.tensor_scalar(out=tmp2.t[:], in0=p.t[:, 0:TB], scalar1=a0P.t[:, HP:HP + 1], scalar2=None, op0=ALU.add), reads=[p, a0P], writes=[tmp2])
                    k.op("act", lambda e: e.activation(out=asig.t[:], in_=tmp2.t[:], func=AF.Sigmoid), reads=[tmp2], writes=[asig])
                    k.op("pool", lambda e: e.tensor_scalar(out=kk.t[:], in0=k_h.t[:], scalar1=kkP.t[:, HP:HP + 1], scalar2=None, op0=ALU.mult), reads=[k_h, kkP], writes=[kk])
                    tt("pool", tmp.t[:], kk.t[:], kk.t[:], ALU.mult, [kk], [tmp])
                    p = newps()
                    mm(p.t[:, 0:TB], bones.t[:, :], tmp.t[:], True, True, [bones, tmp], [p])
                    k.op("act", lambda e, p=p: e.activation(out=tmp2.t[:], in_=p.t[:, 0:TB], func=AF.Sqrt), reads=[p], writes=[tmp2])
                    k.op("dve", lambda e: e.tensor_scalar(out=tmp2.t[:], in0=tmp2.t[:], scalar1=1e-12, scalar2=None, op0=ALU.max), reads=[tmp2], writes=[tmp2])
                    k.op("dve", lambda e: e.reciprocal(out=tmp2.t[:], in_=tmp2.t[:]), reads=[tmp2], writes=[tmp2])
                    tt("dve", kkn.t[:], kk.t[:], tmp2.t[:], ALU.mult, [kk, tmp2], [kkn])
                    k.op("pool", lambda e: e.tensor_scalar(out=tmp.t[:], in0=asig.t[:], scalar1=kaP.t[:, HP:HP + 1], scalar2=omkaP.t[:, HP:HP + 1], op0=ALU.mult, op1=ALU.add), reads=[asig, kaP, omkaP], writes=[tmp])
                    tt("pool", kd.t[:], k_h.t[:], tmp.t[:], ALU.mult, [k_h, tmp], [kd])
                    tt("pool", bb.t[:], kkn.t[:], asig.t[:], ALU.mult, [kkn, asig], [bb])
                    k.op("dve", lambda e: e.scalar_tensor_tensor(out=tmp.t[:], in0=r_h.t[:], scalar=rkP.t[:, HP:HP + 1], in1=kd.t[:], op0=ALU.mult, op1=ALU.mult), reads=[r_h, rkP, kd], writes=[tmp])
                    if not seg_ctx:
                        p = newps()
                        mm(p.t[:, 0:TB], bones.t[:, :], tmp.t[:], True, True, [bones, tmp], [p])
                        tt("dve", tmp2.t[:], p.t[:, 0:TB], v_h.t[:], ALU.mult, [p, v_h], [tmp2])
                        k.dma("pool", bon_s.t.ap()[b, d, 2 * HP:2 * HP + 2, :, tok0:tok0 + TB].rearrange("h l t -> (h l) t"), tmp2.t[:], reads=[tmp2], writes=[bon_s])
                    v3 = lambda t_: t_.t[:].rearrange("p (c t) -> p c t", t=64)
                    tt("dve", RT.t[:, HP, :].rearrange("p (c t) -> p c t", t=64), v3(r_h), E["in"].t[:], ALU.mult, [r_h, E["in"]], [RT])
                    k.op("dve", lambda e: e.scalar_tensor_tensor(out=AT.t[:, HP, :].rearrange("p (c t) -> p c t", t=64), in0=v3(kkn), scalar=-1.0, in1=E["ex"].t[:], op0=ALU.mult, op1=ALU.mult),
                         reads=[kkn, E["ex"]], writes=[AT])
                    tt("dve", BT.t[:, HP, :].rearrange("p (c t) -> p c t", t=64), v3(bb), E["inv"].t[:], ALU.mult, [bb, E["inv"]], [BT])
                    tt("pool", KT.t[:, HP, :].rearrange("p (c t) -> p c t", t=64), v3(kd), E["inv"].t[:], ALU.mult, [kd, E["inv"]], [KT])
                    tt("dve", v3(tb16["bh"]), v3(bb), E["rat"].t[:], ALU.mult, [bb, E["rat"]], [tb16["bh"]])
                    tt("pool", v3(tb16["kh"]), v3(kd), E["rat"].t[:], ALU.mult, [kd, E["rat"]], [tb16["kh"]])
                    copy("act", tb16["vb"].t[:], v_h.t[:], [v_h], [tb16["vb"]])
                    for (srcb, dstb) in ((tb16["vb"], Vtok), (tb16["bh"], BHtok), (tb16["kh"], KHtok)):
                        p = newps()
                        for c in range(4):
                            for hp in range(2):
                                mm(p.t[HV[hp], c * 64:(c + 1) * 64], srcb.t[HV[hp], c * 64:(c + 1) * 64], identb.t[HV[hp], HV[hp]], True, True, [srcb, identb], [p])
                        copy(ew(), dstb.t[:, HP, :, :].rearrange("p c t -> p (c t)"), p.t[:, 0:256], [p], [dstb])
                for c in range(4):
                    sl = slice(c * 64, (c + 1) * 64)
                    Yc = Ych[ych_i[0] % 2]
                    ych_i[0] += 1
                    v4 = lambda pb: pb.t[:, 0:256].rearrange("p (h t) -> p h t", t=64)

                    def heads(hg):
                        for hi in range(4):
                            for hp in range(2):
                                yield hi, 4 * hg + hi, HV[hp], slice(hi * 64, (hi + 1) * 64)
                    stt_ = [None, None]
                    for hg in range(2):
                        pAB, pRB, pAK, pRK, pA = newps(), newps(), newps(), newps(), newps()
                        for hi, HP, hv, o_ in heads(hg):
                            mm(pAB.t[hv, o_], BT.t[hv, HP, sl], AT.t[hv, HP, sl], True, True, [BT, AT], [pAB])
                            mm(pRB.t[hv, o_], BT.t[hv, HP, sl], RT.t[hv, HP, sl], True, True, [BT, RT], [pRB])
                            mm(pAK.t[hv, o_], KT.t[hv, HP, sl], AT.t[hv, HP, sl], True, True, [KT, AT], [pAK])
                            mm(pRK.t[hv, o_], KT.t[hv, HP, sl], RT.t[hv, HP, sl], True, True, [KT, RT], [pRK])
                            mm(pA.t[hv, o_], AT.t[hv, HP, sl], BT.t[hv, HP, sl], True, True, [AT, BT], [pA])
                        Pc, PTc, Nc = Pb[hg][0], PTb[hg][0], Nb[hg][0]
                        tt("dve", Pc.t[:], v4(pAB), m_su.t[:], ALU.mult, [pAB, m_su], [Pc])
                        tt("dve", PTc.t[:], v4(pA), m_lt.t[:], ALU.mult, [pA, m_lt], [PTc])
                        tt("dve", AakT[hg].t[:], v4(pAK), m_su.t[:], ALU.mult, [pAK, m_su], [AakT[hg]])
                        tt("dve", ArbT[hg].t[:], v4(pRB), m_ue.t[:], ALU.mult, [pRB, m_ue], [ArbT[hg]])
                        tt("dve", ArkT[hg].t[:], v4(pRK), m_ue.t[:], ALU.mult, [pRK, m_ue], [ArkT[hg]])
                        tt("pool", Nc.t[:], Pc.t[:], I4.t[:], ALU.add, [Pc, I4], [Nc])
                        stt_[hg] = [Pc, PTc, Nc, 0]
                    for j in range(1, 6):
                        nxt = [None, None]
                        for hg in range(2):
                            Pc, PTc, Nc, cur = stt_[hg]
                            Pn, PTn, Nn = Pb[hg][1 - cur], PTb[hg][1 - cur], Nb[hg][1 - cur]
                            pPT = newps()
                            for hi, HP, hv, o_ in heads(hg):
                                mm(pPT.t[hv, o_], Pc.t[hv, hi, :], PTc.t[hv, hi, :], True, True, [Pc, PTc], [pPT])
                            copy("act" if hg == 0 else "dve", PTn.t[:], v4(pPT), [pPT], [PTn])
                            if j < 5:
                                pP = newps()
                                for hi, HP, hv, o_ in heads(hg):
                                    mm(pP.t[hv, o_], PTc.t[hv, hi, :], Pc.t[hv, hi, :], True, True, [Pc, PTc], [pP])
                                copy("dve" if hg == 0 else "act", Pn.t[:], v4(pP), [pP], [Pn])
                            nxt[hg] = (Pn, PTn, Nn)
                        for hg in range(2):
                            Pc, PTc, Nc, cur = stt_[hg]
                            Pn, PTn, Nn = nxt[hg]
                            pN = newps()
                            for hi, HP, hv, o_ in heads(hg):
                                mm(pN.t[hv, o_], PTn.t[hv, hi, :], Nc.t[hv, hi, :], True, True, [PTn, Nc], [pN])
                            tt("dve", Nn.t[:], v4(pN), Nc.t[:], ALU.add, [pN, Nc], [Nn])
                            stt_[hg] = [Pn, PTn, Nn, 1 - cur]
                    for hg in range(2):
                        pX = newps()
                        for hi, HP, hv, o_ in heads(hg):
                            mm(pX.t[hv, o_], AT.t[hv, HP, sl], S0Tb.t[hv, HP, :], True, False, [AT, S0Tb], [pX])
                            mm(pX.t[hv, o_], AakT[hg].t[hv, hi, :], Vtok.t[hv, HP, c, :], False, True, [AakT[hg], Vtok], [pX])
                        copy("act", Xf[hg].t[:], v4(pX), [pX], [Xf[hg]])
                    for hg in range(2):
                        Nc = stt_[hg][2]
                        pU = newps()
                        for hi, HP, hv, o_ in heads(hg):
                            mm(pU.t[hv, o_], Nc.t[hv, hi, :], Xf[hg].t[hv, hi, :], True, True, [Nc, Xf[hg]], [pU])
                        copy("dve", Ub.t[:, 4 * hg:4 * hg + 4, :], v4(pU), [pU], [(Ub, hg)])
                    for hg in range(2):
                        pY = newps()
                        for hi, HP, hv, o_ in heads(hg):
                            mm(pY.t[hv, o_], RT.t[hv, HP, sl], S0Tb.t[hv, HP, :], True, False, [RT, S0Tb], [pY])
                            mm(pY.t[hv, o_], ArbT[hg].t[hv, hi, :], Ub.t[hv, HP, :], False, False, [ArbT[hg], (Ub, hg)], [pY])
                            mm(pY.t[hv, o_], ArkT[hg].t[hv, hi, :], Vtok.t[hv, HP, c, :], False, True, [ArkT[hg], Vtok], [pY])
                        copy("act", Yc.t[:, 4 * hg:4 * hg + 4, :], v4(pY), [pY], [Yc])
                    for hg in range(2):
                        gsl = slice(4 * hg, 4 * hg + 4)
                        pS = newps()
                        for hi, HP, hv, o_ in heads(hg):
                            mm(pS.t[hv, o_], BHtok.t[hv, HP, c, :], Ub.t[hv, HP, :], True, False, [BHtok, (Ub, hg)], [pS])
                            mm(pS.t[hv, o_], KHtok.t[hv, HP, c, :], Vtok.t[hv, HP, c, :], False, True, [KHtok, Vtok], [pS])
                        tt("pool", stmp[hg].t[:], S0T.t[:, gsl, :], bc_last(gT.t[:, gsl, c], 64), ALU.mult, [(S0T, hg), gT], [stmp[hg]])
                        tt("dve", S0T.t[:, gsl, :], v4(pS), stmp[hg].t[:], ALU.add, [pS, stmp[hg]], [(S0T, hg)])
                        copy("act", S0Tb.t[:, gsl, :], S0T.t[:, gsl, :], [(S0T, hg)], [S0Tb])
                    if not seg_ctx:
                        wv_ = wkv_s.t.ap()[b, d, tok0 + c * 64:tok0 + (c + 1) * 64, :].rearrange("t (g hp v) -> t g hp v", hp=2, v=64)
                        for hp in range(2):
                            k.dma("act", wv_[:, :, hp, :], Yc.t[HV[hp], :, :], reads=[Yc], writes=[wkv_s])
        barrier()
        esr.close()

    def rev_last(ap, n):
        dims = [list(x) for x in ap.ap]
        dims[-1] = [-1, n]
        return AP(ap.tensor, ap.offset + n - 1, dims)

    def final_phase(b):
        esf = ExitStack()
        f = lambda n, sh, dt=F32: sbt(esf, n, sh, dt)
        h2T = f("h2T", [128, 8, SEQ + 2], BF16)
        gtBb = f("gtBb", [128, 2, D])
        sgt = f("sgt", [128, TB]); sgt2 = f("sgt2", [128, TB])
        wq = [f("wq%d" % i, [128, 8, 128], BF16) for i in range(4)]
        esA = ExitStack()
        fa = lambda n, sh, dt=F32: sbt(esA, n, sh, dt)
        G = [fa("G%d" % i, [128, 8, TB]) for i in range(5)]
        gl = fa("gl", [128, 8, TB], BF16); rw = fa("rw", [128, 8, TB], BF16); merged = fa("merged", [128, 8, TB], BF16)
        sgb = fa("sgb", [128, TB], BF16)
        wo = fa("wo", [128, 8, 512], BF16)
        condBb = fa("condBb", [128, 8, 128])
        wmf = [fa("wmf%d" % i, [128, 8, 128]) for i in range(2)]
        bmr = fa("bmr", [1, 128])
        st8 = fa("st8", [128, 32])
        gupb = fa("gupb", [128, D], BF16)
        wqi = [0]

        def loadw(name, c0, ncols=128):
            wt = wq[wqi[0] % 4]
            wqi[0] += 1
            k.dma("sp", wt.t[:, :, 0:ncols], wbf[name].t.ap().rearrange("(kc p) n -> p kc n", p=128)[:, :, c0:c0 + ncols], reads=[wbf[name]], writes=[wt])
            return wt

        for kc in range(8):
            k.op("dve", lambda e, kc=kc: e.tensor_copy(out=condBb.t[:, kc, :], in_=condT.t[:, kc, b:b + 1].to_broadcast([128, 128])), reads=[condT], writes=[condBb])
        wmod_v = inp['w_mod'].t.ap().rearrange("(kc p) n -> p kc n", p=128)
        for which, c00 in ((0, 2048), (1, 5120)):
            for q in range(8):
                cc = c00 + q * 128
                w_ = wmf[q % 2]
                k.dma("sp", w_.t[:], wmod_v[:, :, cc:cc + 128], reads=[w_], writes=[w_])
                k.dma("sp", bmr.t[:], inp['b_mod'].t.ap().rearrange("(o n) -> o n", o=1)[:, cc:cc + 128], reads=[bmr], writes=[bmr])
                p = newps()
                for kc in range(8):
                    mm(p.t[:, 0:128], condBb.t[:, kc, :], w_.t[:, kc, :], kc == 0, False, [w_, condBb], [p])
                mm(p.t[:, 0:128], ones.t[0:1, 0:128], bmr.t[0:1, :], False, True, [ones, bmr], [p])
                copy("act", gtBb.t[:, which, q * 128:(q + 1) * 128], p.t[:, 0:128], [p], [gtBb])
        k.dma("sp", xin[0].t[:], inp['rwkv_g_up'].t.ap(), reads=[xin[0]], writes=[xin[0]])
        copy("dve", gupb.t[:], xin[0].t[:], [xin[0]], [gupb])
        for kc_ in range(8):
            k.op("dve", lambda e, kc_=kc_: e.memset(h2T.t[:, kc_, :], 0.0), writes=[h2T])
        build_hT(b, False)

        def mirror(t0, n):
            return SEQ - t0 - n

        for i in range(8):
            t0 = i * TB
            col0 = LAT0 + t0
            m0 = mirror(t0, TB)
            k.dma("sp", G[0].t[:], ys_s.t.ap()[b, 0, :, :, t0:t0 + TB].rearrange("kt p t -> p kt t"), reads=[ys_s, G[0]], writes=[G[0]])
            k.dma("sp", G[1].t[:], ys_s.t.ap()[b, 1, :, :, m0:m0 + TB].rearrange("kt p t -> p kt t"), reads=[ys_s, G[1]], writes=[G[1]])
            for oc in range(8):
                wt = loadw('w_in', oc * 128)
                p = newps()
                for kc in range(8):
                    mm(p.t[:, 0:TB], wt.t[:, kc, :], hT.t[:, kc, col0:col0 + TB], kc == 0, kc == 7, [wt, (hT, kc)], [p])
                k.op("dve", lambda e, p=p, oc=oc: e.scalar_tensor_tensor(out=G[2].t[:, oc, :], in0=p.t[:, 0:TB], scalar=s5d.t[:, oc:oc + 1], in1=G[0].t[:, oc, :], op0=ALU.mult, op1=ALU.add),
                     reads=[p, s5d, G[0]], writes=[G[2]])
                tt("dve", G[2].t[:, oc, :], G[2].t[:, oc, :], rev_last(G[1].t[:, oc, :], TB), ALU.add, [G[2], G[1]], [G[2]])
            for oc in range(8):
                x_ = G[2].t[:, oc, :]
                tt("pool", sgt.t[:], x_, x_, ALU.mult, [G[2]], [sgt])
                k.op("pool", lambda e: e.tensor_scalar(out=sgt.t[:], in0=sgt.t[:], scalar1=0.044715, scalar2=1.0, op0=ALU.mult, op1=ALU.add), reads=[sgt], writes=[sgt])
                tt("pool", sgt.t[:], sgt.t[:], x_, ALU.mult, [sgt, G[2]], [sgt])
                k.op("act", lambda e: e.activation(out=sgt2.t[:], in_=sgt.t[:], func=AF.Sigmoid, scale=1.5957691216), reads=[sgt], writes=[sgt2])
                tt("dve", gl.t[:, oc, :], x_, sgt2.t[:], ALU.mult, [G[2], sgt2], [gl])
            for oc in range(8):
                wa = loadw('s5_glu_w', oc * 128)
                wb_ = loadw('s5_glu_w', D + oc * 128)
                pa = newps(); pb = newps()
                for kc in range(8):
                    mm(pa.t[:, 0:TB], wa.t[:, kc, :], gl.t[:, kc, :], kc == 0, kc == 7, [wa, gl], [pa])
                for kc in range(8):
                    mm(pb.t[:, 0:TB], wb_.t[:, kc, :], gl.t[:, kc, :], kc == 0, kc == 7, [wb_, gl], [pb])
                k.op("act", lambda e, pb=pb: e.activation(out=sgt.t[:], in_=pb.t[:, 0:TB], func=AF.Sigmoid), reads=[pb], writes=[sgt])
                tt("dve", G[3].t[:, oc, :], pa.t[:, 0:TB], sgt.t[:], ALU.mult, [pa, sgt], [G[3]])
            for j in range(2):
                ts = t0 + j * 128
                ms = mirror(ts, 128)
                k.dma("sp", xin[0].t[:], wkv_s.t.ap()[b, 0, ts:ts + 128, :], reads=[wkv_s, xin[0]], writes=[xin[0]])
                k.dma("sp", xin[1].t[:], wkv_s.t.ap()[b, 1, ms:ms + 128, :], reads=[wkv_s, xin[1]], writes=[xin[1]])
                for hf in range(2):
                    p = newps()
                    mm(p.t[:, :], ident.t[:, :], xin[0].t[:, hf * 512:(hf + 1) * 512], True, False, [ident, xin[0]], [p])
                    mm(p.t[:, :], J128.t[:, :], xin[1].t[:, hf * 512:(hf + 1) * 512], False, True, [J128, xin[1]], [p])
                    pv = p.t[:, :].rearrange("p (h v) -> p h v", v=64)
                    xc = xn[0].t[:, hf * 512:(hf + 1) * 512].rearrange("p (h v) -> p h v", v=64)
                    sq = junk.t[:, hf * 512:(hf + 1) * 512].rearrange("p (h v) -> p h v", v=64)
                    yn = xn[1].t[:, hf * 512:(hf + 1) * 512].rearrange("p (h v) -> p h v", v=64)
                    k.op("dve", lambda e, pv=pv: e.tensor_reduce(out=st8.t[:, 0:8], in_=pv, axis=AX.X, op=ALU.add), reads=[p], writes=[st8])
                    k.op("dve", lambda e: e.tensor_scalar(out=st8.t[:, 8:16], in0=st8.t[:, 0:8], scalar1=1.0 / 64, scalar2=None, op0=ALU.mult), reads=[st8], writes=[st8])
                    tt("dve", xc, pv, bc_last(st8.t[:, 8:16], 64), ALU.subtract, [p, st8], [xn[0]])
                    tt("pool", sq, xc, xc, ALU.mult, [xn[0]], [junk])
                    k.op("dve", lambda e, sq=sq: e.tensor_reduce(out=st8.t[:, 16:24], in_=sq, axis=AX.X, op=ALU.add), reads=[junk], writes=[st8])
                    k.op("dve", lambda e: e.tensor_scalar(out=st8.t[:, 16:24], in0=st8.t[:, 16:24], scalar1=1.0 / 64, scalar2=64e-5, op0=ALU.mult, op1=ALU.add), reads=[st8], writes=[st8])
                    k.op("act", lambda e: e.activation(out=st8.t[:, 16:24], in_=st8.t[:, 16:24], func=AF.Sqrt), reads=[st8], writes=[st8])
                    k.op("dve", lambda e: e.reciprocal(out=st8.t[:, 24:32], in_=st8.t[:, 16:24]), reads=[st8], writes=[st8])
                    tt("dve", yn, xc, bc_last(st8.t[:, 24:32], 64), ALU.mult, [xn[0], st8], [xn[1]])
                for half in range(2):
                    p = newps()
                    for q in range(4):
                        kc = half * 4 + q
                        mm(p.t[:, q * 128:(q + 1) * 128], xn[1].t[:, kc * 128:(kc + 1) * 128], ident.t[:, :], True, True, [xn[1], ident], [p])
                    for q in range(4):
                        kc = half * 4 + q
                        k.op("dve", lambda e, p=p, q=q, kc=kc: e.tensor_scalar(out=G[2].t[:, kc, j * 128:(j + 1) * 128], in0=p.t[:, q * 128:(q + 1) * 128],
                                                                              scalar1=lnxg.t[:, kc:kc + 1], scalar2=lnxb.t[:, kc:kc + 1], op0=ALU.mult, op1=ALU.add),
                             reads=[p, lnxg, lnxb], writes=[G[2]])
            bon_v = bon_s.t.ap().rearrange("b d (kt hp) l t -> b d kt (hp l) t", hp=2)
            k.dma("sp", G[0].t[:], bon_v[b, 0, :, :, t0:t0 + TB].rearrange("kt p t -> p kt t"), reads=[bon_s, G[0]], writes=[G[0]])
            k.dma("sp", G[1].t[:], bon_v[b, 1, :, :, m0:m0 + TB].rearrange("kt p t -> p kt t"), reads=[bon_s, G[1]], writes=[G[1]])
            for oc in range(8):
                tt("pool", G[2].t[:, oc, :], G[2].t[:, oc, :], G[0].t[:, oc, :], ALU.add, [G[2], G[0]], [G[2]])
                tt("dve", G[2].t[:, oc, :], G[2].t[:, oc, :], rev_last(G[1].t[:, oc, :], TB), ALU.add, [G[2], G[1]], [G[2]])
            wt = loadw('w_in', 4224)
            P = newps(); Ps = newps()
            for kc in range(8):
                mm(P.t[:, 0:TB], wt.t[:, kc, :], hT.t[:, kc, col0:col0 + TB], kc == 0, kc == 7, [wt, (hT, kc)], [P])
            for kc in range(8):
                mm(Ps.t[:, 0:TB], wt.t[:, kc, :], hT.t[:, kc, col0 + 64:col0 + 64 + TB], kc == 0, kc == 7, [wt, (hT, kc)], [Ps])
            k.op("dve", lambda e: e.tensor_scalar(out=sgt.t[:], in0=Ps.t[:, 0:TB], scalar1=mu128.t[:, 25:26], scalar2=None, op0=ALU.mult), reads=[Ps, mu128], writes=[sgt])
            k.op("dve", lambda e: e.scalar_tensor_tensor(out=sgt2.t[:], in0=P.t[:, 0:TB], scalar=omu128.t[:, 25:26], in1=sgt.t[:], op0=ALU.mult, op1=ALU.add), reads=[P, omu128, sgt], writes=[sgt2])
            k.op("act", lambda e: e.activation(out=sgb.t[:], in_=sgt2.t[:], func=AF.Sigmoid), reads=[sgt2], writes=[sgb])
            for oc in range(8):
                p = newps()
                mm(p.t[:, 0:TB], gupb.t[:, oc * 128:(oc + 1) * 128], sgb.t[:], True, True, [gupb, sgb], [p])
                tt("dve", rw.t[:, oc, :], p.t[:, 0:TB], G[2].t[:, oc, :], ALU.mult, [p, G[2]], [rw])
            for oc in range(8):
                wt = loadw('rwkv_w_out', oc * 128)
                p = newps()
                for kc in range(8):
                    mm(p.t[:, 0:TB], wt.t[:, kc, :], rw.t[:, kc, :], kc == 0, kc == 7, [wt, rw], [p])
                copy("act", G[4].t[:, oc, :], p.t[:, 0:TB], [p], [G[4]])
            for oc in range(8):
                wa = loadw('w_in', 4352 + oc * 128)
                wb_ = loadw('w_in', 5376 + oc * 128)
                pa = newps(); pb = newps()
                for kc in range(8):
                    mm(pa.t[:, 0:TB], wa.t[:, kc, :], hT.t[:, kc, col0:col0 + TB], kc == 0, kc == 7, [wa, (hT, kc)], [pa])
                for kc in range(8):
                    mm(pb.t[:, 0:TB], wb_.t[:, kc, :], hT.t[:, kc, col0:col0 + TB], kc == 0, kc == 7, [wb_, (hT, kc)], [pb])
                k.op("act", lambda e, pa=pa: e.activation(out=sgt.t[:], in_=pa.t[:, 0:TB], func=AF.Sigmoid), reads=[pa], writes=[sgt])
                k.op("act", lambda e, pb=pb: e.activation(out=sgt2.t[:], in_=pb.t[:, 0:TB], func=AF.Sigmoid), reads=[pb], writes=[sgt2])
                tt("dve", sgt.t[:], sgt.t[:], G[3].t[:, oc, :], ALU.mult, [sgt, G[3]], [sgt])
                tt("pool", sgt2.t[:], sgt2.t[:], G[4].t[:, oc, :], ALU.mult, [sgt2, G[4]], [sgt2])
                tt("dve", merged.t[:, oc, :], sgt.t[:], sgt2.t[:], ALU.add, [sgt, sgt2], [merged])
            for j in range(2):
                ts = t0 + j * 128
                k.dma("sp", xin[0].t[:], inp['x'].t.ap()[b, ts:ts + 128, :], reads=[xin[0]], writes=[xin[0]])
                for hf in range(2):
                    k.dma("sp", wo.t[:], wbf['w_out'].t.ap().rearrange("(kc p) n -> p kc n", p=128)[:, :, hf * 512:(hf + 1) * 512], reads=[wbf['w_out'], wo], writes=[wo])
                    p = newps()
                    for kc in range(8):
                        mm(p.t[:, :], merged.t[:, kc, j * 128:(j + 1) * 128], wo.t[:, kc, :], kc == 0, kc == 7, [merged, wo], [p])
                    tt("dve", xn[0].t[:, hf * 512:(hf + 1) * 512], p.t[:, :], gtBb.t[:, 0, hf * 512:(hf + 1) * 512], ALU.mult, [p, gtBb], [xn[0]])
                tt("pool", xn[0].t[:], xn[0].t[:], xin[0].t[:], ALU.add, [xn[0], xin[0]], [xn[0]])
                k.dma("pool", hx1_s.t.ap()[b, ts:ts + 128, :], xn[0].t[:], reads=[xn[0]], writes=[hx1_s])
                rms_rstd(xn[0], xn[0].t[:], ssq.t[:, 2:3])
                k.op("dve", lambda e: e.tensor_scalar(out=xn[1].t[:], in0=xn[0].t[:], scalar1=ssq.t[:, 2:3], scalar2=None, op0=ALU.mult), reads=[xn[0], ssq], writes=[xn[1]])
                for half in range(2):
                    p = newps()
                    for q in range(4):
                        kc = half * 4 + q
                        mm(p.t[:, q * 128:(q + 1) * 128], xn[1].t[:, kc * 128:(kc + 1) * 128], ident.t[:, :], True, True, [xn[1], ident], [p])
                    for q in range(4):
                        kc = half * 4 + q
                        k.op("dve", lambda e, p=p, q=q, kc=kc: e.tensor_scalar(out=h2T.t[:, kc, 1 + ts:1 + ts + 128], in0=p.t[:, q * 128:(q + 1) * 128],
                                                                              scalar1=s2T.t[:, kc, b:b + 1], scalar2=modT.t[:, 24 + kc, b:b + 1], op0=ALU.mult, op1=ALU.add),
                             reads=[p, s2T, modT], writes=[h2T])
        barrier()
        esA.close()
        esB = ExitStack()
        ff = sbt(esB, "ff", [128, 22, TB], BF16)
        wd_ = [sbt(esB, "wdn%d" % i, [128, D], BF16) for i in range(4)]
        wdv = wbf['ffn_w_down'].t.ap()
        wdi = [0]
        for i in range(8):
            t0 = i * TB
            for jc in range(22):
                wg = loadw('ffn_w_up', jc * 128)
                wv = loadw('ffn_w_up', D_FF + jc * 128)
                pg = newps(); pv_ = newps()
                for kc in range(8):
                    mm(pg.t[:, 0:TB + 2], wg.t[:, kc, :], h2T.t[:, kc, t0:t0 + TB + 2], kc == 0, kc == 7, [wg, h2T], [pg])
                for kc in range(8):
                    mm(pv_.t[:, 0:TB + 2], wv.t[:, kc, :], h2T.t[:, kc, t0:t0 + TB + 2], kc == 0, kc == 7, [wv, h2T], [pv_])
                for (pp, ch, dstt) in ((pg, jc, sgt), (pv_, 22 + jc, sgt2)):
                    k.op("dve", lambda e, pp=pp, ch=ch, dstt=dstt: e.tensor_scalar(out=dstt.t[:], in0=pp.t[:, 0:TB], scalar1=cvw.t[:, ch:ch + 1], scalar2=cvb.t[:, ch:ch + 1], op0=ALU.mult, op1=ALU.add),
                         reads=[pp, cvw, cvb], writes=[dstt])
                    k.op("dve", lambda e, pp=pp, ch=ch, dstt=dstt: e.scalar_tensor_tensor(out=dstt.t[:], in0=pp.t[:, 1:TB + 1], scalar=cvw.t[:, 44 + ch:45 + ch], in1=dstt.t[:], op0=ALU.mult, op1=ALU.add),
                         reads=[pp, cvw, dstt], writes=[dstt])
                    k.op("dve", lambda e, pp=pp, ch=ch, dstt=dstt: e.scalar_tensor_tensor(out=dstt.t[:], in0=pp.t[:, 2:TB + 2], scalar=cvw.t[:, 88 + ch:89 + ch], in1=dstt.t[:], op0=ALU.mult, op1=ALU.add),
                         reads=[pp, cvw, dstt], writes=[dstt])
                k.op("act", lambda e: e.activation(out=sgt.t[:], in_=sgt.t[:], func=AF.Silu), reads=[sgt], writes=[sgt])
                tt("pool", ff.t[:, jc, :], sgt.t[:], sgt2.t[:], ALU.mult, [sgt, sgt2], [ff])
            for j in range(2):
                ts = t0 + j * 128
                pa = newps(); pb = newps()
                for kc in range(22):
                    wt = wd_[wdi[0] % 4]
                    wdi[0] += 1
                    k.dma("sp", wt.t[:], wdv[kc * 128:(kc + 1) * 128, :], reads=[wbf['ffn_w_down']], writes=[wt])
                    mm(pa.t[:, :], ff.t[:, kc, j * 128:(j + 1) * 128], wt.t[:, 0:512], kc == 0, kc == 21, [ff, wt], [pa])
                    mm(pb.t[:, :], ff.t[:, kc, j * 128:(j + 1) * 128], wt.t[:, 512:1024], kc == 0, kc == 21, [ff, wt], [pb])
                k.dma("sp", xin[0].t[:], hx1_s.t.ap()[b, ts:ts + 128, :], reads=[hx1_s, xin[0]], writes=[xin[0]])
                tt("dve", xn[0].t[:, 0:512], pa.t[:, :], gtBb.t[:, 1, 0:512], ALU.mult, [pa, gtBb], [xn[0]])
                tt("dve", xn[0].t[:, 512:1024], pb.t[:, :], gtBb.t[:, 1, 512:1024], ALU.mult, [pb, gtBb], [xn[0]])
                tt("pool", xn[0].t[:], xn[0].t[:], xin[0].t[:], ALU.add, [xn[0], xin[0]], [xn[0]])
                rms_rstd(xn[0], xn[0].t[:], ssq.t[:, 2:3])
                k.op("dve", lambda e: e.scalar_tensor_tensor(out=xn[1].t[:], in0=xn[0].t[:], scalar=ssq.t[:, 2:3], in1=fgB.t[:], op0=ALU.mult, op1=ALU.mult), reads=[xn[0], ssq, fgB], writes=[xn[1]])
                k.dma("pool", outb.t.ap()[b, ts:ts + 128, :], xn[1].t[:], reads=[xn[1]], writes=[outb])
        barrier()
        esB.close()
        esf.close()

    def finish():
        k.finish("sp")
        for e in ("act", "dve", "pool", "pe"):
            k.finish(e)
        k.close()
        es.close()

    if stage in ("hT0", "hT1"):
        rev = stage == "hT1"
        build_hT(0, rev)
        hf = sb("hf", [128, NCOL])
        for kc in range(8 if os.environ.get('HT_SKIP_DUMP') is None else 0):
            copy(os.environ.get("DUMPENG", "dve"), hf.t[:], hT.t[:, kc, :], hT_all, [hf])
            k.dma("sp", dbg["dbg_hT"].t.ap()[kc], hf.t[:], reads=[hf], writes=[dbg["dbg_hT"]])
        finish()
        return nc
    def s5_phase(d, b_list):
        es5 = ExitStack()
        Sel_ = sbt(es5, "Sel", [128, 64, 128], BF16)
        k.op("pool", lambda e: e.memset(Sel_.t[:], 1.0), writes=[Sel_])
        sv = Sel_.t[:].rearrange("p (a b) m -> p a b m", a=8)
        for (pat, cm_, base_, op_) in (([[-16, 8], [16, 8], [-1, 128]], 1, 0, ALU.is_equal),
                                       ([[-16, 8], [0, 8], [0, 128]], 1, 0, ALU.is_ge),
                                       ([[16, 8], [0, 8], [0, 128]], -1, 15, ALU.is_ge)):
            asel_b(Sel_, pat, cm_, base_, op_, view=sv)
        T = {'A': sbt(es5, "tA", [128, 32, 128], BF16), 'Wr': sbt(es5, "tWr", [128, 32, 64], BF16), 'Wi': sbt(es5, "tWi", [128, 32, 64], BF16),
             'Q': sbt(es5, "tQ", [64, 32, 2, 128], BF16), 'LPr': sbt(es5, "tLPr", [64, 7, 2, 32]), 'LPis': sbt(es5, "tLPi", [64, 7, 2, 32]), 'Sel': Sel_}
        for gh in range(2):
            s5_build_tables(d, gh, T)
            esw = ExitStack()
            W_ = ([sbt(esw, "up%d" % i, [128, 4, 8, 64], BF16) for i in range(2)], [sbt(esw, "Ug%d" % i, [128, 32, 64], BF16) for i in range(2)],
                  [sbt(esw, "Hb%d" % i, [64, 2, 32, 65]) for i in range(2)],
                  sbt(esw, "Hbf", [64, 2, 32, 64], BF16), sbt(esw, "Yg", [128, 32, 64], BF16), sbt(esw, "ysblk", [128, 2, 512]),
                  [sbt(esw, "wu%d" % i, [128, 8, 256], BF16) for i in range(1)], sbt(esw, "T13", [64, 2, 32, 32]), sbt(esw, "T24", [64, 2, 32, 32]))
            for b in b_list:
                build_hT(b, d == 1)
                s5_run(b, d, gh, T, W_)
            barrier()
            esw.close()
        es5.close()
        return None

    if stage in ("s5_0", "s5_1"):
        d = int(stage[-1])
        globals_sel = {}
        s5_phase(d, [0])
        esd = ExitStack()
        yb = sbt(esd, "ybd", [128, 8, TB])
        for blk in range(8):
            k.dma("sp", yb.t[:], ys_s.t.ap()[0, d, :, :, blk * TB:(blk + 1) * TB].rearrange("kt p t -> p kt t"), reads=[ys_s, yb], writes=[yb])
            k.dma("sp", dbg["dbg_ys"].t.ap()[:, :, blk * TB:(blk + 1) * TB].rearrange("kt p t -> p kt t"), yb.t[:], reads=[yb], writes=[dbg["dbg_ys"]])
        barrier()
        esd.close()
        finish()
        return nc
    if stage in ("rw_0", "rw_1"):
        d = int(stage[-1])
        rwkv_phase(d, [0])
        esd = ExitStack()
        yb = sbt(esd, "ybd", [128, D])
        for blk in range(16):
            k.dma("sp", yb.t[:], wkv_s.t.ap()[0, d, blk * 128:(blk + 1) * 128, :], reads=[wkv_s, yb], writes=[yb])
            k.dma("sp", dbg["dbg_wkv"].t.ap()[blk * 128:(blk + 1) * 128, :], yb.t[:], reads=[yb], writes=[dbg["dbg_wkv"]])
        bb_ = sbt(esd, "bbd", [64, 16, 256])
        for blk in range(8):
            k.dma("sp", bb_.t[:], bon_s.t.ap()[0, d, :, :, blk * 256:(blk + 1) * 256].rearrange("h p t -> p h t"), reads=[bon_s, bb_], writes=[bb_])
            k.dma("sp", dbg["dbg_bon"].t.ap()[:, :, blk * 256:(blk + 1) * 256].rearrange("h p t -> p h t"), bb_.t[:], reads=[bb_], writes=[dbg["dbg_bon"]])
        barrier()
        esd.close()
        finish()
        return nc
    if stage == "all":
        for d in range(2):
            s5_phase(d, list(range(NB)))
            rwkv_phase(d, list(range(NB)))
        for b in range(NB):
            final_phase(b)
        finish()
        return nc
    raise NotImplementedError(stage)


_CACHE = {}


def kernel(**inputs):
    n_cores = 8
    if "nc" not in _CACHE:
        _CACHE["nc"] = build_program("all")
    nc = _CACHE["nc"]
    in_maps = []
    for ci in range(n_cores):
        m = {}
        for n, shp in INPUT_SHAPES.items():
            a = np.asarray(inputs[n])
            if n in ('x', 'c', 'ctx'):
                a = a[NB * ci:NB * (ci + 1)]
            m[n] = np.ascontiguousarray(a, dtype=np.float32).reshape(shp)
        in_maps.append(m)
    res = run_bass_kernel_spmd(nc, in_maps, core_ids=list(range(n_cores)))
    out = np.concatenate([r["y"] for r in res.results], axis=0)
    return out.astype(np.float32)
```
